# Optimizing a Trainium2 kernel written in Bass

```python
import math
import jax, jax.numpy as jnp
from jax import lax
import numpy as np

D_MODEL = 1024
BATCH = 8
SEQ = 4096
DEPTH = 4

CHUNK = 64
QBLK = 128
PLE_DIM = 256
N_BUCKETS = 32
MAX_DISTANCE = 256
ROPE_THETA = 10000.0
LN_EPS = 1e-5
RMS_EPS = 1e-6

H_A = 8
A_NOPE = 64
A_ROPE = 32
A_V = 64
A_Q_RANK = 512
A_KV_RANK = 256
A_SCALE = (A_NOPE + A_ROPE) ** -0.5
H_B = 8
B_DH = 32
B_SCALE = B_DH ** -0.5
H_C = 16
KV_C = 4
C_GROUP = H_C // KV_C
C_DH = 64
C_SCALE = C_DH ** -0.5
H_IDX = 16
IDX_DH = 64
IDX_ROPE = 32
IDX_SCALE = IDX_DH ** -0.5
TOPK_MAX = 256
P_HEADS = 8
N_KEYS = 128
N_EXPERTS = N_KEYS * N_KEYS
P_DQ = 256
P_DHALF = P_DQ // 2
P_TOPK = 16
P_BLK = 128

EVEN_MIX = H_A * A_V + H_B * 2 * B_DH
ODD_MIX = H_C * C_DH
EVEN_SPLITS = (A_Q_RANK, A_KV_RANK, A_ROPE, H_B * 2 * B_DH, H_B * 2 * B_DH, H_B * 2 * B_DH)
ODD_SPLITS = (H_C * C_DH, KV_C * C_DH, KV_C * C_DH, H_IDX * IDX_DH, IDX_DH, H_IDX)
N_EVEN = (DEPTH + 1) // 2
N_ODD = DEPTH // 2
DN_ALPHA = (2 * DEPTH) ** 0.25
DN_BETA = (8 * DEPTH) ** -0.25

kernel_name = 'hybrid_mla_diff_dsa_peer_encoder'


def split_cols(a, sizes):
    out, start = [], 0
    for s in sizes:
        out.append(a[..., start:start + s])
        start += s
    return out


def layer_norm(x, g, b):
    xf = x.astype(jnp.float32)
    mu = jnp.mean(xf, axis=-1, keepdims=True)
    xc = xf - mu
    var = jnp.mean(xc * xc, axis=-1, keepdims=True)
    return (xc * lax.rsqrt(var + LN_EPS) * g.astype(jnp.float32) + b.astype(jnp.float32)).astype(x.dtype)


def rms_norm(x, g):
    xf = x.astype(jnp.float32)
    y = xf * lax.rsqrt(jnp.mean(xf * xf, axis=-1, keepdims=True) + RMS_EPS)
    return (y * g.astype(jnp.float32)).astype(x.dtype)


def rope(x, pos):
    half = x.shape[-1] // 2
    freqs = ROPE_THETA ** (-jnp.arange(half, dtype=jnp.float32) / half)
    ang = pos.astype(jnp.float32)[..., None] * freqs
    ang = ang.reshape(ang.shape[:2] + (1,) * (x.ndim - 3) + (half,))
    cos, sin = jnp.cos(ang), jnp.sin(ang)
    x1 = x[..., :half].astype(jnp.float32)
    x2 = x[..., half:].astype(jnp.float32)
    return jnp.concatenate([x1 * cos - x2 * sin, x2 * cos + x1 * sin], axis=-1).astype(x.dtype)


def rope_partial(x, pos):
    return jnp.concatenate([rope(x[..., :IDX_ROPE], pos), x[..., IDX_ROPE:]], axis=-1)


def t5_bucket(rel):
    nb = N_BUCKETS // 2
    max_exact = nb // 2
    n = jnp.abs(rel)
    large = max_exact + (jnp.log(jnp.maximum(n, 1).astype(jnp.float32) / max_exact)
                         / math.log(MAX_DISTANCE / max_exact) * (nb - max_exact)).astype(jnp.int32)
    large = jnp.minimum(large, nb - 1)
    return jnp.where(rel > 0, nb, 0) + jnp.where(n < max_exact, n, large)


def to_blocks(a):
    b, s = a.shape[:2]
    return jnp.moveaxis(a.reshape((b, s // QBLK, QBLK) + a.shape[2:]), 1, 0)


def from_blocks(a):
    a = jnp.moveaxis(a, 0, 1)
    return a.reshape((a.shape[0], a.shape[1] * a.shape[2]) + a.shape[3:])


def even_mixer(h, positions, w_in, w_uq, w_ukv, q_norm_g, kv_norm_g,
               lam_q1, lam_k1, lam_q2, lam_k2, subln_g, w_o, bias_tab, lam_init):
    b, s, _ = h.shape
    c_q, c_kv, k_r, q_d, k_d, v_d = split_cols(h @ w_in, EVEN_SPLITS)
    q_a = (rms_norm(c_q, q_norm_g) @ w_uq).reshape(b, s, H_A, A_NOPE + A_ROPE)
    q_nope, q_rope = q_a[..., :A_NOPE], rope(q_a[..., A_NOPE:], positions)
    kv_a = (rms_norm(c_kv, kv_norm_g) @ w_ukv).reshape(b, s, H_A, A_NOPE + A_V)
    k_nope, v_a = kv_a[..., :A_NOPE], kv_a[..., A_NOPE:]
    k_rope = rope(k_r, positions)
    q_d = q_d.reshape(b, s, H_B, 2, B_DH)
    k_d = k_d.reshape(b, s, H_B, 2, B_DH)
    v_d = v_d.reshape(b, s, H_B, 2 * B_DH)
    lam = (jnp.exp(jnp.sum(lam_q1 * lam_k1, dtype=jnp.float32))
           - jnp.exp(jnp.sum(lam_q2 * lam_k2, dtype=jnp.float32)) + lam_init)
    key_chunk = jnp.arange(s) // CHUNK

    def block(args):
        qn, qr, qd, qpos, bi = args
        q_chunk = (bi * QBLK + jnp.arange(QBLK)) // CHUNK
        allowed = key_chunk[None, :] <= q_chunk[:, None]
        s_a = (jnp.einsum('bqhd,bkhd->bhqk', qn, k_nope)
               + jnp.einsum('bqhr,bkr->bhqk', qr, k_rope)) * A_SCALE
        p_a = jax.nn.softmax(jnp.where(allowed, s_a.astype(jnp.float32), -jnp.inf), axis=-1)
        o_a = jnp.einsum('bhqk,bkhd->bqhd', p_a.astype(v_a.dtype), v_a)
        bucket = t5_bucket(positions[:, None, :] - qpos[:, :, None])
        bias = jnp.moveaxis(bias_tab[bucket], -1, 1).astype(jnp.float32)
        s_d = jnp.einsum('bqhmd,bkhmd->bhmqk', qd, k_d).astype(jnp.float32) * B_SCALE + bias[:, :, None]
        p_d = jax.nn.softmax(jnp.where(allowed, s_d, -jnp.inf), axis=-1)
        w_d = p_d[:, :, 0] - lam * p_d[:, :, 1]
        o_d = jnp.einsum('bhqk,bkhd->bqhd', w_d.astype(v_d.dtype), v_d)
        o_d = rms_norm(o_d, subln_g) * (1.0 - lam_init)
        return jnp.concatenate([o_a.reshape(b, QBLK, H_A * A_V),
                                o_d.reshape(b, QBLK, H_B * 2 * B_DH)], axis=-1)

    nb = s // QBLK
    o = lax.map(block, (to_blocks(q_nope), to_blocks(q_rope), to_blocks(q_d),
                        to_blocks(positions), jnp.arange(nb)))
    return from_blocks(o) @ w_o


def odd_mixer(h, positions, w_in, w_o, bias_tab):
    b, s, _ = h.shape
    q, k, v, qi, ki, wi = split_cols(h @ w_in, ODD_SPLITS)
    q = q.reshape(b, s, KV_C, C_GROUP, C_DH)
    k = k.reshape(b, s, KV_C, C_DH)
    v = v.reshape(b, s, KV_C, C_DH)
    qi = rope_partial(qi.reshape(b, s, H_IDX, IDX_DH), positions)
    ki = rope_partial(ki, positions)
    wi = wi * (H_IDX ** -0.5)
    n_sel = min(TOPK_MAX, s // 4)
    key_chunk = jnp.arange(s) // CHUNK
    gather = jax.vmap(lambda a, idx: a[idx])

    def block(args):
        qb, qib, wib, qpos, bi = args
        q_chunk = (bi * QBLK + jnp.arange(QBLK)) // CHUNK
        allowed = key_chunk[None, :] <= q_chunk[:, None]
        idx_logits = jnp.einsum('bqhd,bkd->bqhk', qib, ki) * IDX_SCALE
        score = jnp.einsum('bqh,bqhk->bqk', wib, jax.nn.relu(idx_logits)).astype(jnp.float32)
        top_s, sel = lax.top_k(jnp.where(allowed, score, -jnp.inf), n_sel)
        valid = jnp.isfinite(top_s)
        kg = gather(k, sel)
        vg = gather(v, sel)
        rel = gather(positions, sel) - qpos[:, :, None]
        bias = bias_tab[t5_bucket(rel)].reshape(b, QBLK, n_sel, KV_C, C_GROUP)
        bias = jnp.transpose(bias, (0, 1, 3, 4, 2)).astype(jnp.float32)
        sc = jnp.einsum('bqkgd,bqnkd->bqkgn', qb, kg).astype(jnp.float32) * C_SCALE + bias
        pr = jax.nn.softmax(jnp.where(valid[:, :, None, None, :], sc, -jnp.inf), axis=-1)
        o = jnp.einsum('bqkgn,bqnkd->bqkgd', pr.astype(vg.dtype), vg)
        return o.reshape(b, QBLK, H_C * C_DH)

    nb = s // QBLK
    o = lax.map(block, (to_blocks(q), to_blocks(qi), to_blocks(wi),
                        to_blocks(positions), jnp.arange(nb)))
    return from_blocks(o) @ w_o


def peer(h, w_q, sub_k1, sub_k2, u_tab, v_tab):
    b, s, d = h.shape
    t = b * s
    xt = h.reshape(t, d)
    q = (xt @ w_q).reshape(t, P_HEADS, 2, P_DHALF)
    s1 = jnp.einsum('thd,nd->thn', q[:, :, 0], sub_k1).astype(jnp.float32)
    s2 = jnp.einsum('thd,nd->thn', q[:, :, 1], sub_k2).astype(jnp.float32)
    t1, i1 = lax.top_k(s1, P_TOPK)
    t2, i2 = lax.top_k(s2, P_TOPK)
    cand = (t1[..., :, None] + t2[..., None, :]).reshape(t, P_HEADS, P_TOPK * P_TOPK)
    cidx = (i1[..., :, None] * N_KEYS + i2[..., None, :]).reshape(t, P_HEADS, P_TOPK * P_TOPK)
    top_s, pick = lax.top_k(cand, P_TOPK)
    eidx = jnp.take_along_axis(cidx, pick, axis=-1).reshape(t, P_HEADS * P_TOPK)
    g = jax.nn.softmax(top_s, axis=-1).reshape(t, P_HEADS * P_TOPK).astype(h.dtype)
    nb = t // P_BLK

    def expert_block(args):
        xb, eb, gb = args
        hb = jnp.einsum('tkd,td->tk', u_tab[eb], xb)
        return jnp.einsum('tk,tkd->td', gb * jax.nn.gelu(hb), v_tab[eb])

    out = lax.map(expert_block, (xt.reshape(nb, P_BLK, d),
                                 eidx.reshape(nb, P_BLK, P_HEADS * P_TOPK),
                                 g.reshape(nb, P_BLK, P_HEADS * P_TOPK)))
    return out.reshape(b, s, d)


def setup_inputs(seed: int = 0) -> dict:
    key = jax.random.key(seed)
    ks = iter(jax.random.split(key, 32))

    def nrm(shape, scale):
        return jax.random.normal(next(ks), shape, jnp.float32) * scale

    offs = jax.random.randint(next(ks), (BATCH,), 0, 64, dtype=jnp.int32) * CHUNK
    positions = offs[:, None] + jnp.arange(SEQ, dtype=jnp.int32)[None, :]
    ev_in = sum(EVEN_SPLITS)
    od_in = sum(ODD_SPLITS)
    return {
        'x': nrm((BATCH, SEQ, D_MODEL), 1.0),
        'p': nrm((DEPTH, BATCH, SEQ, PLE_DIM), 1.0),
        'positions': positions,
        'rel_bias': nrm((N_BUCKETS, H_B + H_C), 0.2),
        'ev_w_in': nrm((N_EVEN, D_MODEL, ev_in), D_MODEL ** -0.5),
        'ev_w_uq': nrm((N_EVEN, A_Q_RANK, H_A * (A_NOPE + A_ROPE)), A_Q_RANK ** -0.5),
        'ev_w_ukv': nrm((N_EVEN, A_KV_RANK, H_A * (A_NOPE + A_V)), A_KV_RANK ** -0.5),
        'ev_q_norm': 1.0 + nrm((N_EVEN, A_Q_RANK), 0.02),
        'ev_kv_norm': 1.0 + nrm((N_EVEN, A_KV_RANK), 0.02),
        'ev_lam_q1': nrm((N_EVEN, B_DH), 0.1),
        'ev_lam_k1': nrm((N_EVEN, B_DH), 0.1),
        'ev_lam_q2': nrm((N_EVEN, B_DH), 0.1),
        'ev_lam_k2': nrm((N_EVEN, B_DH), 0.1),
        'ev_subln': 1.0 + nrm((N_EVEN, 2 * B_DH), 0.02),
        'ev_w_o': nrm((N_EVEN, EVEN_MIX, D_MODEL), DN_BETA * EVEN_MIX ** -0.5),
        'od_w_in': nrm((N_ODD, D_MODEL, od_in), D_MODEL ** -0.5),
        'od_w_o': nrm((N_ODD, ODD_MIX, D_MODEL), DN_BETA * ODD_MIX ** -0.5),
        'ln1_g': 1.0 + nrm((DEPTH, D_MODEL), 0.02),
        'ln1_b': nrm((DEPTH, D_MODEL), 0.02),
        'ln2_g': 1.0 + nrm((DEPTH, D_MODEL), 0.02),
        'ln2_b': nrm((DEPTH, D_MODEL), 0.02),
        'peer_w_q': nrm((DEPTH, D_MODEL, P_HEADS * P_DQ), D_MODEL ** -0.5),
        'peer_k1': nrm((DEPTH, N_KEYS, P_DHALF), P_DHALF ** -0.5),
        'peer_k2': nrm((DEPTH, N_KEYS, P_DHALF), P_DHALF ** -0.5),
        'peer_u': nrm((DEPTH, N_EXPERTS, D_MODEL), D_MODEL ** -0.5),
        'peer_v': nrm((DEPTH, N_EXPERTS, D_MODEL), DN_BETA * P_HEADS ** -0.5),
        'ple_w': nrm((DEPTH, PLE_DIM, D_MODEL), DN_BETA * PLE_DIM ** -0.5),
        'ple_gate_w': nrm((DEPTH, D_MODEL, D_MODEL), D_MODEL ** -0.5),
        'ple_gate_b': nrm((DEPTH, D_MODEL), 0.02),
    }


def reference(x, p, positions, rel_bias, ev_w_in, ev_w_uq, ev_w_ukv, ev_q_norm, ev_kv_norm,
              ev_lam_q1, ev_lam_k1, ev_lam_q2, ev_lam_k2, ev_subln, ev_w_o, od_w_in, od_w_o,
              ln1_g, ln1_b, ln2_g, ln2_b, peer_w_q, peer_k1, peer_k2, peer_u, peer_v,
              ple_w, ple_gate_w, ple_gate_b):
    bias_b = rel_bias[:, :H_B]
    bias_c = rel_bias[:, H_B:]
    h = x
    for i in range(DEPTH):
        j = i // 2
        if i % 2 == 0:
            lam_init = 0.8 - 0.6 * math.exp(-0.3 * i)
            mix = even_mixer(h, positions, ev_w_in[j], ev_w_uq[j], ev_w_ukv[j], ev_q_norm[j],
                             ev_kv_norm[j], ev_lam_q1[j], ev_lam_k1[j], ev_lam_q2[j], ev_lam_k2[j],
                             ev_subln[j], ev_w_o[j], bias_b, lam_init)
        else:
            mix = odd_mixer(h, positions, od_w_in[j], od_w_o[j], bias_c)
        h = layer_norm(DN_ALPHA * h + mix, ln1_g[i], ln1_b[i])
        ffn = peer(h, peer_w_q[i], peer_k1[i], peer_k2[i], peer_u[i], peer_v[i])
        h = layer_norm(DN_ALPHA * h + ffn, ln2_g[i], ln2_b[i])
        gate = jax.nn.sigmoid(h @ ple_gate_w[i] + ple_gate_b[i])
        h = h + gate * (p[i] @ ple_w[i])
    return h
```

```python
import math
import numpy as np
import concourse.bass as bass
import concourse.mybir as mybir
from concourse.bass_utils import run_bass_kernel_spmd

F32 = mybir.dt.float32
BF16 = mybir.dt.bfloat16
I32 = mybir.dt.int32
ALU = mybir.AluOpType
ACT = mybir.ActivationFunctionType
AX = mybir.AxisListType

D = 1024
PLE = 256
LN_EPS = 1e-5
RMS_EPS = 1e-6
A_SCALE = 96 ** -0.5
B_SCALE = 32 ** -0.5
C_SCALE = 64 ** -0.5
IDX_SCALE = 64 ** -0.5
DEPTH_FULL = 4
DN_ALPHA = (2 * DEPTH_FULL) ** 0.25
NEG = -3.0e38
RREL = 1279
REL0 = 767


class Em:
    NDMA = 24

    def __init__(self, nc):
        self.nc = nc
        self.eng = {'pe': nc.tensor, 'act': nc.scalar, 'dve': nc.vector,
                    'pool': nc.gpsimd, 'sp': nc.sync}
        self.sems = {}
        self.cnt = {}
        self.seen = {e: {} for e in self.eng}
        self.lastw = {}
        self.readers = {}
        for e in ('pe', 'act', 'dve', 'pool'):
            self.sems[e] = nc.alloc_semaphore("s_" + e)
            self.cnt[e] = 0
        for i in range(self.NDMA):
            s = 'd%d' % i
            self.sems[s] = nc.alloc_semaphore("s_" + s)
            self.cnt[s] = 0
        self.dma_rr = 0
        self.nins = 0

    def _wait(self, eng, deps):
        need = {}
        for (s, v) in deps:
            if s == 'pe' and eng == 'pe':
                continue
            if v > need.get(s, 0):
                need[s] = v
        for s, v in need.items():
            if self.seen[eng].get(s, 0) < v:
                self.eng[eng].wait_ge(self.sems[s], v)
                self.seen[eng][s] = v

    def _deps(self, r, w):
        deps = []
        for x in r:
            t = self.lastw.get(x)
            if t:
                deps.append(t)
        for x in w:
            t = self.lastw.get(x)
            if t:
                deps.append(t)
            rd = self.readers.get(x)
            if rd:
                deps.extend(rd.items())
        return deps

    def _update(self, tok, r, w):
        for x in w:
            self.lastw[x] = tok
            self.readers[x] = {}
        for x in r:
            d = self.readers.setdefault(x, {})
            if tok[1] > d.get(tok[0], 0):
                d[tok[0]] = tok[1]

    def op(self, eng, fn, r=(), w=(), inc=True):
        self._wait(eng, self._deps(r, w))
        ins = fn(self.eng[eng])
        self.nins += 1
        if inc:
            self.cnt[eng] += 1
            ins.then_inc(self.sems[eng], 1)
            tok = (eng, self.cnt[eng])
        else:
            tok = (eng, self.cnt[eng] + 1)
        self._update(tok, r, w)
        return ins

    def dma(self, q, out, in_, r=(), w=(), **kw):
        i = self.dma_rr
        self.dma_rr = (i + 1) % self.NDMA
        s = 'd%d' % i
        deps = self._deps(r, w)
        if self.cnt[s] > 0:
            deps.append((s, self.cnt[s]))
        self._wait(q, deps)
        ins = self.eng[q].dma_start(out=out, in_=in_, **kw)
        ins.then_inc(self.sems[s], 16)
        self.nins += 1
        self.cnt[s] += 16
        tok = (s, self.cnt[s])
        self._update(tok, r, w)
        return ins

    def barrier(self):
        allv = [(s, v) for s, v in self.cnt.items() if v > 0]
        for e in ('pe', 'act', 'dve', 'pool', 'sp'):
            self._wait(e, allv)
        self.lastw.clear()
        self.readers.clear()


def fap(ap, dims):
    return bass.AP(ap.tensor, ap.offset, [list(ap.ap[0])] + [list(d) for d in dims])


class Prog:
    def __init__(self, S, L, dbg=None):
        self.S, self.L = S, L
        self.NB = S // 128
        self.NG = S // 512
        self.NE = (L + 1) // 2
        self.NO = L // 2
        self.dbg = dbg
        nc = self.nc = bass.Bass("TRN2", target_bir_lowering=False)
        self.em = Em(nc)
        self.ps = [nc.alloc_psum_tensor("ps%d" % i, [128, 512], F32) for i in range(8)]
        self.psn = 0
        self.uid = 0
        self.resv = set()
        self.io()
        self.consts()
        for l in range(L):
            if l % 2 == 0:
                self.even_proj(l)
                self.attn_even(l)
            else:
                self.odd_proj(l)
                self.dsa(l)
            self.outproj_ln1(l)
            self.peer_scores(l)
            self.peer_main(l)
        self.finalize()

    def din(self, name, shape, dt=F32):
        return self.nc.dram_tensor(name, list(shape), dt, kind="ExternalInput").ap()

    def dscr(self, name, shape, dt=F32):
        return self.nc.dram_tensor(name, list(shape), dt, kind="Internal").ap()

    def io(self):
        S, L, NE, NO = self.S, self.L, self.NE, max(self.NO, 1)
        self.xT = self.din("xT", [D, S])
        self.pT = self.din("pT", [L, PLE, S])
        self.pos = self.din("pos", [1, S], I32)
        self.rel_bias = self.din("rel_bias", [32, 24])
        self.ev_w_in = self.din("ev_w_in", [NE, D, 2336])
        self.ev_w_uq = self.din("ev_w_uq", [NE, 512, 768])
        self.ev_w_ukv = self.din("ev_w_ukv", [NE, 256, 1024])
        self.ev_q_norm = self.din("ev_q_norm", [NE, 128, 4])
        self.ev_kv_norm = self.din("ev_kv_norm", [NE, 128, 2])
        self.ev_lam = self.din("ev_lam", [NE, 32, 4])
        self.ev_subln = self.din("ev_subln", [NE, 64, 1])
        self.ev_w_o = self.din("ev_w_o", [NE, D, D])
        self.od_w_in = self.din("od_w_in", [NO, D, 2640])
        self.od_w_o = self.din("od_w_o", [NO, D, D])
        self.ln = {k: self.din(k, [L, 128, 8]) for k in ("ln1_g", "ln1_b", "ln2_g", "ln2_b", "ple_gate_b")}
        self.peer_w_q = self.din("peer_w_q", [L, D, 2048])
        self.peer_k1T = self.din("peer_k1T", [L, 128, 128])
        self.peer_k2T = self.din("peer_k2T", [L, 128, 128])
        self.peer_uT = self.din("peer_uT", [L, D, 16384])
        self.peer_v = self.din("peer_v", [L, 16384, D])
        self.ple_w = self.din("ple_w", [L, PLE, D])
        self.ple_gate_w = self.din("ple_gate_w", [L, D, D])
        self.outT = self.nc.dram_tensor("outT", [D, S], F32, kind="ExternalOutput").ap()
        self.hT = self.dscr("hT", [D, S])
        self.QR = self.dscr("QR", [256, S], BF16)
        self.QN = self.dscr("QN", [512, S], BF16)
        self.KR = self.dscr("KR", [32, S], BF16)
        self.KN = self.dscr("KN", [512, S], BF16)
        self.VA = self.dscr("VA", [S, 512], BF16)
        self.QD = self.dscr("QD", [512, S], BF16)
        self.KD = self.dscr("KD", [512, S], BF16)
        self.VD = self.dscr("VD", [S, 512], BF16)
        self.OT = self.dscr("OT", [D, S], BF16)
        self.EB = self.dscr("EB", [24, 6, 128, 512], BF16)
        self.fvec = self.dscr("fvec", [24, RREL])
        self.QC = self.dscr("QC", [1024, S], BF16)
        self.KC = self.dscr("KC", [256, S], BF16)
        self.VC = self.dscr("VC", [S, 256], BF16)
        self.QI = self.dscr("QI", [1024, S], BF16)
        self.KI = self.dscr("KI", [64, S], BF16)
        self.WI = self.dscr("WI", [S, 16])
        self.E1 = self.dscr("E1", [S, 1024])
        self.E2 = self.dscr("E2", [S, 1024])
        self.TH = self.dscr("TH", [S, 8])
        self.FT = self.dscr("FT", [D, S])
        self.COS = self.dscr("COS", [128, S])
        self.SIN = self.dscr("SIN", [128, S])
        self.COSI = self.dscr("COSI", [128, S])
        self.SINI = self.dscr("SINI", [128, S])
        if self.dbg:
            self.dbg_out = self.nc.dram_tensor("dbg", list(self.dbg[1]), self.dbg[2], kind="ExternalOutput").ap()

    def sb(self, name, shape, dt=F32):
        self.uid += 1
        return self.nc.alloc_sbuf_tensor("%s_%d" % (name, self.uid), list(shape), dt)

    def nps(self):
        i = self.psn
        self.psn = (i + 1) % 8
        return i

    def phase(self):
        prog = self

        class _P:
            def __enter__(s2):
                prog.em.barrier()
                s2.snap = []
                s2.orig = prog.sb

                def sb(name, shape, dt=F32):
                    prog.uid += 1
                    cm = prog.nc.sbuf_tensor("%s_%d" % (name, prog.uid), list(shape), dt)
                    t = cm.__enter__()
                    s2.snap.append(cm)
                    return t
                prog.sb = sb
                return s2

            def __exit__(s2, *a):
                prog.em.barrier()
                for cm in reversed(s2.snap):
                    cm.__exit__(None, None, None)
                prog.sb = s2.orig
                return False
        return _P()

    def mm(self, psi, rows, lhsT, rhs, start, stop, r, cols=None, inc=None):
        out = self.ps[psi][0:rows, :] if cols is None else self.ps[psi][0:rows, cols[0]:cols[1]]
        self.em.op('pe', lambda e: e.matmul(out, lhsT=lhsT, rhs=rhs, start=start, stop=stop),
                   r=r, w=['ps%d' % psi], inc=stop if inc is None else inc)

    def dbg_dump(self, key, ap_src_dram):
        if self.dbg and self.dbg[0] == key:
            self.em.barrier()
            self.em.dma('sp', self.dbg_out, ap_src_dram)
            self.em.barrier()

    def consts(self):
        em, nc, S = self.em, self.nc, self.S
        sb = self.sb
        self.ident_bf = sb("identb", [128, 128], BF16)
        self.ident_f = sb("identf", [128, 128])
        self.ones_f = sb("onesf", [128, 128])
        self.sel65 = sb("sel65", [128, 64])
        self.cb = sb("cb", [128, 24])
        self.ncb = sb("ncb", [128, 24])
        self.msk = sb("msk", [128, 4, 512], BF16)
        with self.phase():
            tmp = self.sb("ctmp", [128, 128])
            em.op('pool', lambda e: e.iota(tmp[:], [[1, 128]], base=0, channel_multiplier=-1,
                                           allow_small_or_imprecise_dtypes=True), w=['ctmp'])
            em.op('dve', lambda e: e.tensor_single_scalar(out=self.ident_bf[:], in_=tmp[:], scalar=0.0, op=ALU.is_equal), r=['ctmp'], w=['idb'])
            em.op('dve', lambda e: e.tensor_single_scalar(out=self.ident_f[:], in_=tmp[:], scalar=0.0, op=ALU.is_equal), r=['ctmp'], w=['idf'])
            em.op('pool', lambda e: e.memset(self.ones_f[:], 1.0), w=['ones'])
            em.op('pool', lambda e: e.memset(self.sel65[:], 0.0), w=['sel'])
            em.op('pool', lambda e: e.memset(self.sel65[64:65, :], 1.0), r=['sel'], w=['sel'])
            qrow = self.sb("qrow", [128, 512])
            em.op('pool', lambda e: e.iota(qrow[:], [[1, 512]], base=0, channel_multiplier=0,
                                           allow_small_or_imprecise_dtypes=True), w=['qrow'])
            for o in range(4):
                em.op('dve', lambda e, o=o: e.tensor_single_scalar(out=self.msk[0:64, o, :], in_=qrow[0:64, :], scalar=float(128 * o), op=ALU.is_ge), r=['qrow'], w=['msk'])
                em.op('dve', lambda e, o=o: e.tensor_single_scalar(out=self.msk[64:128, o, :], in_=qrow[64:128, :], scalar=float(128 * o + 64), op=ALU.is_ge), r=['qrow'], w=['msk'])
            em.dma('sp', self.cb[:], bass.AP(self.rel_bias.tensor, 15 * 24, [[0, 128], [1, 24]]), w=['cb'])
            em.op('dve', lambda e: e.tensor_scalar(out=self.ncb[:], in0=self.cb[:], scalar1=-1.0, scalar2=None, op0=ALU.mult), r=['cb'], w=['ncb'])
            self.rope_tables()
            self.bias_tiles()

    def rope_tables(self):
        em, S, NB = self.em, self.S, self.NB
        posi = self.sb("posi", [128, NB], I32)
        posf = self.sb("posf", [128, NB])
        frow = self.sb("frow", [128, 128])
        sgn = self.sb("sgn", [128, 128])
        ang = self.sb("ang", [128, 128])
        tr = self.sb("trg", [128, 128])
        kf = self.sb("kf", [128, 128])
        ki = self.sb("ki", [128, 128], I32)
        cosT = self.sb("cosT", [128, S])
        sinT = self.sb("sinT", [128, S])
        em.dma('sp', posi[:], bass.AP(self.pos.tensor, 0, [[1, 128], [128, NB]]), w=['posi'], allow_slow_non_contiguous=True)
        em.op('dve', lambda e: e.tensor_copy(out=posf[:], in_=posi[:]), r=['posi'], w=['posf'])
        for j in range(16):
            fr = float(np.float32(10000.0) ** np.float32(-j / 16.0))
            em.op('pool', lambda e, j=j, fr=fr: e.memset(fap(frow[:, j:j + 1], [[16, 8], [1, 1]]), fr), w=['frow'])
        em.op('pool', lambda e: e.memset(sgn[:], 1.0), w=['sgn'])
        em.op('pool', lambda e: e.memset(fap(sgn[:, 0:1], [[32, 4], [1, 16]]), -1.0), r=['sgn'], w=['sgn'])
        TWO_PI = 2.0 * math.pi
        for tb in range(NB):
            for which, tab in ((0, sinT), (1, cosT)):
                em.op('dve', lambda e, tb=tb: e.tensor_scalar(out=ang[:], in0=frow[:], scalar1=posf[:, tb:tb + 1], scalar2=None, op0=ALU.mult), r=['frow', 'posf'], w=['ang'])
                if which == 1:
                    em.op('dve', lambda e: e.tensor_scalar(out=ang[:], in0=ang[:], scalar1=math.pi / 2, scalar2=None, op0=ALU.add), r=['ang'], w=['ang'])
                em.op('dve', lambda e: e.tensor_scalar(out=kf[:], in0=ang[:], scalar1=1.0 / TWO_PI, scalar2=None, op0=ALU.mult), r=['ang'], w=['kf'])
                em.op('dve', lambda e: e.tensor_copy(out=ki[:], in_=kf[:]), r=['kf'], w=['ki'])
                em.op('dve', lambda e: e.tensor_copy(out=kf[:], in_=ki[:]), r=['ki'], w=['kf'])
                em.op('dve', lambda e: e.scalar_tensor_tensor(out=ang[:], in0=kf[:], scalar=-TWO_PI, in1=ang[:], op0=ALU.mult, op1=ALU.add), r=['kf', 'ang'], w=['ang'])
                em.op('dve', lambda e: e.tensor_scalar(out=kf[:], in0=ang[:], scalar1=math.pi, scalar2=-TWO_PI, op0=ALU.is_gt, op1=ALU.mult), r=['ang'], w=['kf'])
                em.op('dve', lambda e: e.tensor_tensor(out=ang[:], in0=ang[:], in1=kf[:], op=ALU.add), r=['kf', 'ang'], w=['ang'])
                em.op('dve', lambda e: e.tensor_scalar(out=kf[:], in0=ang[:], scalar1=-math.pi, scalar2=TWO_PI, op0=ALU.is_lt, op1=ALU.mult), r=['ang'], w=['kf'])
                em.op('dve', lambda e: e.tensor_tensor(out=ang[:], in0=ang[:], in1=kf[:], op=ALU.add), r=['kf', 'ang'], w=['ang'])
                em.op('dve', lambda e: e.tensor_scalar(out=ang[:], in0=ang[:], scalar1=-3.14159, scalar2=3.14159, op0=ALU.max, op1=ALU.min), r=['ang'], w=['ang'])
                em.op('act', lambda e: e.activation(out=tr[:], in_=ang[:], func=ACT.Sin), r=['ang'], w=['trg'])
                if which == 0:
                    em.op('dve', lambda e: e.tensor_tensor(out=tr[:], in0=tr[:], in1=sgn[:], op=ALU.mult), r=['trg', 'sgn'], w=['trg'])
                pi = self.nps()
                self.mm(pi, 128, tr[:], self.ident_f[:], True, True, ['trg', 'idf'], cols=(0, 128))
                em.op('act', lambda e, pi=pi, tab=tab, tb=tb: e.copy(out=tab[:, tb * 128:(tb + 1) * 128], in_=self.ps[pi][:, 0:128]), r=['ps%d' % pi], w=['ropetab'])
        em.dma('sp', self.COS, cosT[:], r=['ropetab'], w=['COS'])
        em.dma('sp', self.SIN, sinT[:], r=['ropetab'], w=['SIN'])
        for r0 in (32, 96):
            em.op('pool', lambda e, r0=r0: e.memset(cosT[r0:r0 + 32, :], 1.0), r=['ropetab', 'COS'], w=['ropetab'])
            em.op('pool', lambda e, r0=r0: e.memset(sinT[r0:r0 + 32, :], 0.0), r=['ropetab', 'SIN'], w=['ropetab'])
        em.dma('sp', self.COSI, cosT[:], r=['ropetab'], w=['COSI'])
        em.dma('sp', self.SINI, sinT[:], r=['ropetab'], w=['SINI'])

    def bias_tiles(self):
        em = self.em
        rel = self.sb("relrow", [32, RREL])
        bk = self.sb("bk", [32, RREL])
        oh = self.sb("oh", [32, RREL])
        pidx = self.sb("pidx", [32, 1])
        tab = self.sb("tab", [32, 24])
        fv = self.sb("fv", [24, RREL])
        em.dma('sp', tab[:], self.rel_bias, w=['tab'])
        em.op('pool', lambda e: e.iota(rel[:], [[1, RREL]], base=-REL0, channel_multiplier=0, allow_small_or_imprecise_dtypes=True), w=['rel'])
        em.op('pool', lambda e: e.iota(pidx[:], [[1, 1]], base=0, channel_multiplier=1, allow_small_or_imprecise_dtypes=True), w=['pidx'])
        em.op('dve', lambda e: e.memset(bk[:], 15.0), w=['bk'])
        steps = [(t, -1.0) for t in (-165, -107, -69, -45, -29, -19, -12, -7, -6, -5, -4, -3, -2, -1, 0)]
        steps += [(1, 17.0)] + [(t, 1.0) for t in (2, 3, 4, 5, 6, 7, 8, 13, 20, 30, 46, 70, 108, 166)]
        for th, dlt in steps:
            if dlt in (1.0, -1.0):
                op1 = ALU.add if dlt > 0 else ALU.subtract
            em.op('dve', lambda e, th=th, dlt=dlt: e.tensor_scalar(out=oh[:], in0=rel[:], scalar1=float(th), scalar2=float(dlt), op0=ALU.is_ge, op1=ALU.mult), r=['rel'], w=['oh'])
            em.op('dve', lambda e: e.tensor_tensor(out=bk[:], in0=bk[:], in1=oh[:], op=ALU.add), r=['oh', 'bk'], w=['bk'])
        em.op('dve', lambda e: e.tensor_scalar(out=oh[:], in0=bk[:], scalar1=pidx[:, 0:1], scalar2=None, op0=ALU.is_equal), r=['bk', 'pidx'], w=['oh'])
        for c0 in range(0, RREL, 512):
            n = min(512, RREL - c0)
            pi = self.nps()
            self.mm(pi, 24, tab[:], oh[:, c0:c0 + n], True, True, ['tab', 'oh'], cols=(0, n))
            em.op('act', lambda e, pi=pi, c0=c0, n=n: e.copy(out=fv[:, c0:c0 + n], in_=self.ps[pi][0:24, 0:n]), r=['ps%d' % pi], w=['fv'])
        em.dma('sp', self.fvec, fv[:], r=['fv'], w=['fvec'])
        tt = [self.sb("toep%d" % i, [128, 512]) for i in range(2)]
        te = [self.sb("toepb%d" % i, [128, 512], BF16) for i in range(2)]
        n = 0
        for h in range(24):
            for oi in range(6):
                o = oi - 2
                b = n % 2
                n += 1
                src = bass.AP(self.fvec.tensor, h * RREL + 128 * o + REL0, [[1, 128], [-1, 512]])
                em.dma('sp', tt[b][:], src, r=['fvec'], w=['toep%d' % b], allow_slow_non_contiguous=True)
                if h < 8 and o >= 0:
                    em.op('act', lambda e, b=b, h=h: e.activation(out=tt[b][:], in_=tt[b][:], func=ACT.Exp, bias=self.ncb[:, h:h + 1], scale=1.0), r=['toep%d' % b, 'ncb'], w=['toep%d' % b])
                    em.op('dve', lambda e, b=b, o=o: e.tensor_tensor(out=te[b][:], in0=tt[b][:], in1=self.msk[:, o, :], op=ALU.mult), r=['toep%d' % b, 'msk'], w=['toepb%d' % b])
                else:
                    em.op('act', lambda e, b=b, h=h: e.activation(out=te[b][:], in_=tt[b][:], func=ACT.Exp, bias=self.ncb[:, h:h + 1], scale=1.0), r=['toep%d' % b, 'ncb'], w=['toepb%d' % b])
                em.dma('sp', self.EB[h, oi], te[b][:], r=['toepb%d' % b], w=['EB'])
        self.dbg_dump('fvec', self.fvec)

    def rms_fm(self, src, C, F, gcol, dst, tag):
        em = self.em
        sq = self.t_sq
        pi = self.nps()
        for c in range(C):
            em.op('act', lambda e, c=c: e.activation(out=sq[:, c, :], in_=src[:, c, :], func=ACT.Square), r=[tag], w=['sq%d' % c])
            self.mm(pi, 128, self.ones_f[:], sq[:, c, :], c == 0, c == C - 1, ['ones', 'sq%d' % c])
        rs = self.t_rs
        em.op('act', lambda e: e.activation(out=rs[:], in_=self.ps[pi][:], func=ACT.Sqrt, bias=self.eps_rms[:, 0:1], scale=1.0 / F), r=['ps%d' % pi, 'eps'], w=['rs'])
        em.op('dve', lambda e: e.reciprocal(out=rs[:], in_=rs[:]), r=['rs'], w=['rs'])
        for c in range(C):
            em.op('dve', lambda e, c=c: e.scalar_tensor_tensor(out=dst[:, c, :], in0=src[:, c, :], scalar=gcol[:, c:c + 1], in1=rs[:], op0=ALU.mult, op1=ALU.mult), r=[tag, 'rs', 'gcols'], w=[tag + 'n'])

    def ln_fm(self, y, gcol, bcol, outf, tag, otag, outb=None):
        em = self.em
        sq = self.t_sq
        p1, p2 = self.nps(), self.nps()
        for c in range(8):
            self.mm(p1, 128, self.ones_f[:], y[:, c, :], c == 0, c == 7, [tag])
        for c in range(8):
            em.op('act', lambda e, c=c: e.activation(out=sq[:, c, :], in_=y[:, c, :], func=ACT.Square), r=[tag], w=['sq%d' % c])
            self.mm(p2, 128, self.ones_f[:], sq[:, c, :], c == 0, c == 7, ['ones', 'sq%d' % c])
        mean, rs, msq = self.t_mean, self.t_rs, self.t_msq
        em.op('act', lambda e: e.activation(out=mean[:], in_=self.ps[p1][:], func=ACT.Copy, scale=1.0 / D), r=['ps%d' % p1], w=['mean'])
        em.op('dve', lambda e: e.tensor_tensor(out=msq[:], in0=mean[:], in1=mean[:], op=ALU.mult), r=['mean'], w=['msq'])
        em.op('dve', lambda e: e.scalar_tensor_tensor(out=rs[:], in0=self.ps[p2][:], scalar=1.0 / D, in1=msq[:], op0=ALU.mult, op1=ALU.subtract), r=['ps%d' % p2, 'msq'], w=['rs'])
        em.op('act', lambda e: e.activation(out=rs[:], in_=rs[:], func=ACT.Sqrt, bias=self.eps_ln[:, 0:1], scale=1.0), r=['rs', 'eps'], w=['rs'])
        em.op('dve', lambda e: e.reciprocal(out=rs[:], in_=rs[:]), r=['rs'], w=['rs'])
        for c in range(8):
            em.op('dve', lambda e, c=c: e.tensor_tensor(out=y[:, c, :], in0=y[:, c, :], in1=mean[:], op=ALU.subtract), r=[tag, 'mean'], w=[tag])
            em.op('dve', lambda e, c=c: e.tensor_tensor(out=y[:, c, :], in0=y[:, c, :], in1=rs[:], op=ALU.mult), r=[tag, 'rs'], w=[tag])
            em.op('act', lambda e, c=c: e.activation(out=outf[:, c, :], in_=y[:, c, :], func=ACT.Identity, bias=bcol[:, c:c + 1], scale=gcol[:, c:c + 1]), r=[tag, 'gcols'], w=[otag])
            if outb is not None:
                em.op('pool', lambda e, c=c: e.tensor_copy(out=outb[:, c, :], in_=outf[:, c, :]), r=[otag], w=[otag + 'b'])

    def norm_tmps(self):
        self.t_sq = self.sb("sq", [128, 8, 512])
        self.t_rs = self.sb("rs", [128, 512])
        self.t_mean = self.sb("mean", [128, 512])
        self.t_msq = self.sb("msq", [128, 512])
        self.eps_rms = self.sb("epsr", [128, 1])
        self.eps_ln = self.sb("epsl", [128, 1])
        self.em.op('pool', lambda e: e.memset(self.eps_rms[:], RMS_EPS), w=['eps'])
        self.em.op('pool', lambda e: e.memset(self.eps_ln[:], LN_EPS), r=['eps'], w=['eps'])

    def wload(self, dst, src2d, K, c0, c1, tag, dcol=0):
        for k in range(K):
            self.em.dma('pool', dst[:, k, dcol:dcol + (c1 - c0)], src2d[k * 128:(k + 1) * 128, c0:c1], w=[tag])

    def hsrc(self, l):
        return self.xT if l == 0 else self.hT

    def even_proj(self, l):
        em, S = self.em, self.S
        j = l // 2
        with self.phase():
            self.norm_tmps()
            win = self.sb("win", [128, 8, 2336], BF16)
            wkrs = self.sb("wkrs", [128, 8, 32], BF16)
            wqn = self.sb("wqn", [128, 4, 512], BF16)
            wqr = self.sb("wqr", [128, 4, 256], BF16)
            wqs = self.sb("wqs", [128, 4, 256], BF16)
            wkn = self.sb("wkn", [128, 2, 512], BF16)
            wv = self.sb("wv", [128, 2, 512], BF16)
            gq = self.sb("gq", [128, 4])
            gkv = self.sb("gkv", [128, 2])
            W = self.ev_w_in[j]
            self.wload(win, W, 8, 0, 2336, 'win')
            self.wload(wkrs, W, 8, 768 + 16, 768 + 32, 'wkrs', 0)
            self.wload(wkrs, W, 8, 768, 768 + 16, 'wkrs', 16)
            UQ, UKV = self.ev_w_uq[j], self.ev_w_ukv[j]
            for h in range(8):
                self.wload(wqn, UQ, 4, h * 96, h * 96 + 64, 'wqn', h * 64)
                self.wload(wqr, UQ, 4, h * 96 + 64, h * 96 + 96, 'wqr', h * 32)
                self.wload(wqs, UQ, 4, h * 96 + 80, h * 96 + 96, 'wqs', h * 32)
                self.wload(wqs, UQ, 4, h * 96 + 64, h * 96 + 80, 'wqs', h * 32 + 16)
                self.wload(wkn, UKV, 2, h * 128, h * 128 + 64, 'wkn', h * 64)
                self.wload(wv, UKV, 2, h * 128 + 64, h * 128 + 128, 'wv', h * 64)
            em.dma('sp', gq[:], self.ev_q_norm[j], w=['gcols'])
            em.dma('sp', gkv[:], self.ev_kv_norm[j], w=['gcols'])
            hsrc = self.hsrc(l).rearrange("(k p) s -> p k s", p=128)
            hb = [self.sb("hb%d" % i, [128, 8, 512], BF16) for i in range(2)]
            cq = self.sb("cq", [128, 4, 512])
            ckv = self.sb("ckv", [128, 2, 512])
            cqn = self.sb("cqn", [128, 4, 512], BF16)
            ckvn = self.sb("ckvn", [128, 2, 512], BF16)
            t1 = self.sb("t1", [128, 512])
            t2 = self.sb("t2", [128, 512])
            ob = [self.sb("ob%d" % i, [128, 512], BF16) for i in range(4)]
            obn = [0]

            def evac_store(pi, rows, dst, eng=None):
                i = obn[0] % 4
                obn[0] += 1
                eng = eng or ('act' if i % 2 == 0 else 'dve')
                if eng == 'act':
                    em.op('act', lambda e: e.copy(out=ob[i][0:rows, :], in_=self.ps[pi][0:rows, :]), r=['ps%d' % pi], w=['ob%d' % i])
                else:
                    em.op('dve', lambda e: e.tensor_copy(out=ob[i][0:rows, :], in_=self.ps[pi][0:rows, :]), r=['ps%d' % pi], w=['ob%d' % i])
                em.dma('sp', dst, ob[i][0:rows, :], r=['ob%d' % i], w=['scr'])

            cs = [self.sb("cs%d" % i, [128, 512]) for i in range(2)]
            sn = [self.sb("sn%d" % i, [128, 512]) for i in range(2)]

            def rope_store(pa, pb, rows, g, dst):
                cg, sg_ = cs[g % 2], sn[g % 2]
                em.op('dve', lambda e: e.tensor_tensor(out=t1[0:rows, :], in0=self.ps[pa][0:rows, :], in1=cg[0:rows, :], op=ALU.mult), r=['ps%d' % pa, 'cs%d' % (g % 2)], w=['t1'])
                em.op('dve', lambda e: e.tensor_tensor(out=t2[0:rows, :], in0=self.ps[pb][0:rows, :], in1=sg_[0:rows, :], op=ALU.mult), r=['ps%d' % pb, 'sn%d' % (g % 2)], w=['t2'])
                i = obn[0] % 4
                obn[0] += 1
                em.op('pool', lambda e: e.tensor_tensor(out=ob[i][0:rows, :], in0=t1[0:rows, :], in1=t2[0:rows, :], op=ALU.add), r=['t1', 't2'], w=['ob%d' % i])
                em.dma('sp', dst, ob[i][0:rows, :], r=['ob%d' % i], w=['scr'])

            for g in range(self.NG):
                gs = slice(g * 512, (g + 1) * 512)
                H = hb[g % 2]
                ht = 'hb%d' % (g % 2)
                em.dma('pool', H[:], hsrc[:, :, gs], w=[ht])
                em.dma('sp', cs[g % 2][:], self.COS[:, gs], w=['cs%d' % (g % 2)])
                em.dma('sp', sn[g % 2][:], self.SIN[:, gs], w=['sn%d' % (g % 2)])
                for c in range(4):
                    pi = self.nps()
                    for k in range(8):
                        self.mm(pi, 128, win[:, k, c * 128:(c + 1) * 128], H[:, k, :], k == 0, k == 7, ['win', ht])
                    em.op('act', lambda e, c=c, pi=pi: e.copy(out=cq[:, c, :], in_=self.ps[pi][:]), r=['ps%d' % pi], w=['cq'])
                for c in range(2):
                    pi = self.nps()
                    for k in range(8):
                        self.mm(pi, 128, win[:, k, 512 + c * 128:512 + (c + 1) * 128], H[:, k, :], k == 0, k == 7, ['win', ht])
                    em.op('dve', lambda e, c=c, pi=pi: e.tensor_copy(out=ckv[:, c, :], in_=self.ps[pi][:]), r=['ps%d' % pi], w=['ckv'])
                self.rms_fm(cq, 4, 512, gq, cqn, 'cq')
                self.rms_fm(ckv, 2, 256, gkv, ckvn, 'ckv')
                pa, pb = self.nps(), self.nps()
                for k in range(8):
                    self.mm(pa, 32, win[:, k, 768:800], H[:, k, :], k == 0, k == 7, ['win', ht])
                for k in range(8):
                    self.mm(pb, 32, wkrs[:, k, :], H[:, k, :], k == 0, k == 7, ['wkrs', ht])
                rope_store(pa, pb, 32, g, self.KR[:, gs])
                for c in range(4):
                    for base, dst in ((800, self.QD), (1312, self.KD)):
                        pi = self.nps()
                        for k in range(8):
                            self.mm(pi, 128, win[:, k, base + c * 128:base + (c + 1) * 128], H[:, k, :], k == 0, k == 7, ['win', ht])
                        evac_store(pi, 128, dst[c * 128:(c + 1) * 128, gs])
                for tb in range(4):
                    pi = self.nps()
                    for k in range(8):
                        self.mm(pi, 128, H[:, k, tb * 128:(tb + 1) * 128], win[:, k, 1824:2336], k == 0, k == 7, ['win', ht])
                    evac_store(pi, 128, self.VD[g * 512 + tb * 128:g * 512 + (tb + 1) * 128, :])
                for rc in range(2):
                    pa, pb = self.nps(), self.nps()
                    for k in range(4):
                        self.mm(pa, 128, wqr[:, k, rc * 128:(rc + 1) * 128], cqn[:, k, :], k == 0, k == 3, ['wqr', 'cqn'])
                    for k in range(4):
                        self.mm(pb, 128, wqs[:, k, rc * 128:(rc + 1) * 128], cqn[:, k, :], k == 0, k == 3, ['wqs', 'cqn'])
                    rope_store(pa, pb, 128, g, self.QR[rc * 128:(rc + 1) * 128, gs])
                for c in range(4):
                    pi = self.nps()
                    for k in range(4):
                        self.mm(pi, 128, wqn[:, k, c * 128:(c + 1) * 128], cqn[:, k, :], k == 0, k == 3, ['wqn', 'cqn'])
                    evac_store(pi, 128, self.QN[c * 128:(c + 1) * 128, gs])
                    pi = self.nps()
                    for k in range(2):
                        self.mm(pi, 128, wkn[:, k, c * 128:(c + 1) * 128], ckvn[:, k, :], k == 0, k == 1, ['wkn', 'ckvn'])
                    evac_store(pi, 128, self.KN[c * 128:(c + 1) * 128, gs])
                for tb in range(4):
                    pi = self.nps()
                    for k in range(2):
                        self.mm(pi, 128, ckvn[:, k, tb * 128:(tb + 1) * 128], wv[:, k, :], k == 0, k == 1, ['wv', 'ckvn'])
                    evac_store(pi, 128, self.VA[g * 512 + tb * 128:g * 512 + (tb + 1) * 128, :])
        self.dbg_dump('QR', self.QR)
        self.dbg_dump('QN', self.QN)
        self.dbg_dump('KR', self.KR)
        self.dbg_dump('VA', self.VA)
        self.dbg_dump('QD', self.QD)

    def attn_pass(self, qt, kt, vt, rows, g, scale, bias_col, mult_fn, tags, pso, ptb, ptn, qoff=None):
        em = self.em
        last = 4 * g + 3
        for kb in range(last + 1):
            pi = self.nps()
            while pi in self.resv:
                pi = self.nps()
            q0 = g * 512 if qoff is None else qoff
            self.mm(pi, 128, kt[0:rows, kb * 128:(kb + 1) * 128], qt[0:rows, q0:q0 + 512], True, True, tags)
            i = ptn[0] % len(ptb)
            ptn[0] += 1
            P = ptb[i]
            pt = 'pt%d' % i
            if bias_col is None:
                em.op('act', lambda e, pi=pi, P=P: e.activation(out=P[:], in_=self.ps[pi][:], func=ACT.Exp, scale=scale), r=['ps%d' % pi], w=[pt])
            else:
                em.op('act', lambda e, pi=pi, P=P: e.activation(out=P[:], in_=self.ps[pi][:], func=ACT.Exp, bias=bias_col, scale=scale), r=['ps%d' % pi, 'cb'], w=[pt])
            mult_fn(kb, P, pt)
            self.mm(pso, 65, vt[:, kb, :], P[:], kb == 0, kb == last, [pt] + tags)

    def attn_norm(self, pso, osb, rc, tag):
        em = self.em
        em.op('act', lambda e: e.copy(out=osb[0:65, :], in_=self.ps[pso][0:65, :]), r=['ps%d' % pso], w=[tag])
        pi = self.nps()
        while pi in self.resv:
            pi = self.nps()
        self.mm(pi, 64, self.sel65[0:65, :], osb[0:65, :], True, True, ['sel', tag])
        em.op('dve', lambda e: e.reciprocal(out=rc[0:64, :], in_=self.ps[pi][0:64, :]), r=['ps%d' % pi], w=['rc'])
        em.op('dve', lambda e: e.tensor_tensor(out=osb[0:64, :], in0=osb[0:64, :], in1=rc[0:64, :], op=ALU.mult), r=['rc', tag], w=[tag])

    def attn_even(self, l):
        em, S, NB = self.em, self.S, self.NB
        j = l // 2
        lam_init = 0.8 - 0.6 * math.exp(-0.3 * l)
        with self.phase():
            ptb = [self.sb("pt%d" % i, [128, 512], BF16) for i in range(3)]
            ptn = [0]
            self.resv = {6, 7}
            osb = [self.sb("osb%d" % i, [128, 512]) for i in range(2)]
            rc = self.sb("rc", [128, 512])
            o16 = [self.sb("o16%d" % i, [64, 512], BF16) for i in range(2)]
            qt = [self.sb("qt%d" % i, [96, S], BF16) for i in range(2)]
            kt = [self.sb("kt%d" % i, [96, S], BF16) for i in range(2)]
            vt = [self.sb("vt%d" % i, [128, NB, 65], BF16) for i in range(2)]
            for i in range(2):
                em.op('pool', lambda e, i=i: e.memset(vt[i][:, :, 64:65], 1.0), w=['vt%d' % i])

            def mult_a(g):
                def f(kb, P, pt):
                    if kb >= 4 * g:
                        em.op('dve', lambda e: e.tensor_tensor(out=P[:], in0=P[:], in1=self.msk[:, kb - 4 * g, :], op=ALU.mult), r=[pt, 'msk'], w=[pt])
                return f
            n = 0
            for h in range(8):
                b = h % 2
                tg = ['qt%d' % b, 'kt%d' % b, 'vt%d' % b]
                em.dma('sp', qt[b][0:32, :], self.QR[h * 32:(h + 1) * 32, :], w=[tg[0]])
                em.dma('sp', qt[b][32:96, :], self.QN[h * 64:(h + 1) * 64, :], w=[tg[0]])
                em.dma('sp', kt[b][0:32, :], self.KR[:, :], w=[tg[1]])
                em.dma('sp', kt[b][32:96, :], self.KN[h * 64:(h + 1) * 64, :], w=[tg[1]])
                em.dma('sp', vt[b][:, :, 0:64], self.VA[:, h * 64:(h + 1) * 64].rearrange("(nb p) d -> p nb d", p=128), w=[tg[2]])
                for g in range(self.NG):
                    pso = 6 + n % 2
                    ob = n % 2
                    n += 1
                    self.attn_pass(qt[b], kt[b], vt[b], 96, g, A_SCALE, None, mult_a(g), tg, pso, ptb, ptn)
                    self.attn_norm(pso, osb[ob], rc, 'osb%d' % ob)
                    em.op('pool', lambda e, ob=ob: e.tensor_copy(out=o16[ob][:], in_=osb[ob][0:64, :]), r=['osb%d' % ob], w=['o16%d' % ob])
                    em.dma('sp', self.OT[h * 64:(h + 1) * 64, g * 512:(g + 1) * 512], o16[ob][:], r=['o16%d' % ob], w=['OT'])
        self.dbg_dump('OTA', self.OT)
        with self.phase():
            ptb = [self.sb("pt%d" % i, [128, 512], BF16) for i in range(3)]
            ptn = [0]
            self.resv = {4, 5, 6, 7}
            osb = [self.sb("osb%d" % i, [128, 512]) for i in range(2)]
            rc = self.sb("rc", [128, 512])
            od = self.sb("od", [64, 512])
            sq = self.sb("sq", [64, 512])
            o16 = [self.sb("o16%d" % i, [64, 512], BF16) for i in range(2)]
            qt = [self.sb("qt%d" % i, [32, S], BF16) for i in range(4)]
            kt = [self.sb("kt%d" % i, [32, S], BF16) for i in range(4)]
            vt = [self.sb("vt%d" % i, [128, NB, 65], BF16) for i in range(2)]
            eb = [self.sb("eb%d" % i, [128, 6, 512], BF16) for i in range(2)]
            for i in range(2):
                em.op('pool', lambda e, i=i: e.memset(vt[i][:, :, 64:65], 1.0), w=['vt%d' % i])
            lam4 = self.sb("lam4", [32, 4])
            lamp = self.sb("lamp", [32, 2])
            lamc = self.sb("lamc", [64, 4])
            sg = self.sb("sg", [64, 1])
            epsr = self.sb("epsr2", [64, 1])
            em.op('pool', lambda e: e.memset(epsr[:], RMS_EPS), w=['epsr2'])
            em.dma('sp', lam4[:], self.ev_lam[j], w=['lam4'])
            em.dma('sp', sg[:], self.ev_subln[j], w=['sg'])
            em.op('dve', lambda e: e.tensor_tensor(out=lamp[:, 0:1], in0=lam4[:, 0:1], in1=lam4[:, 1:2], op=ALU.mult), r=['lam4'], w=['lamp'])
            em.op('dve', lambda e: e.tensor_tensor(out=lamp[:, 1:2], in0=lam4[:, 2:3], in1=lam4[:, 3:4], op=ALU.mult), r=['lam4', 'lamp'], w=['lamp'])
            pi = 0
            self.mm(pi, 64, self.ones_f[0:32, 0:64], lamp[:, 0:2], True, True, ['ones', 'lamp'], cols=(0, 2))
            em.op('act', lambda e: e.activation(out=lamc[:, 0:2], in_=self.ps[pi][0:64, 0:2], func=ACT.Exp), r=['ps%d' % pi], w=['lamc'])
            em.op('dve', lambda e: e.tensor_tensor(out=lamc[:, 2:3], in0=lamc[:, 1:2], in1=lamc[:, 0:1], op=ALU.subtract), r=['lamc'], w=['lamc'])
            em.op('dve', lambda e: e.tensor_scalar(out=lamc[:, 3:4], in0=lamc[:, 2:3], scalar1=-lam_init, scalar2=None, op0=ALU.add), r=['lamc'], w=['lamc'])
            em.op('dve', lambda e: e.tensor_scalar(out=sg[:], in0=sg[:], scalar1=1.0 - lam_init, scalar2=None, op0=ALU.mult), r=['sg'], w=['sg'])

            def mult_b(g, E, et):
                def f(kb, P, pt):
                    if kb >= 4 * g - 2:
                        em.op('dve', lambda e: e.tensor_tensor(out=P[:], in0=P[:], in1=E[:, kb - 4 * g + 2, :], op=ALU.mult), r=[pt, et], w=[pt])
                return f
            n = 0
            for h in range(8):
                b = h % 2
                em.dma('sp', eb[b][:], self.EB[h].rearrange("o p q -> p o q"), w=['eb%d' % b])
                em.dma('sp', vt[b][:, :, 0:64], self.VD[:, h * 64:(h + 1) * 64].rearrange("(nb p) d -> p nb d", p=128), w=['vt%d' % b])
                for m in range(2):
                    i = b * 2 + m
                    r0 = (h * 2 + m) * 32
                    em.dma('sp', qt[i][:], self.QD[r0:r0 + 32, :], w=['qt%d' % i])
                    em.dma('sp', kt[i][:], self.KD[r0:r0 + 32, :], w=['kt%d' % i])
                for g in range(self.NG):
                    for m in range(2):
                        i = b * 2 + m
                        pso = 4 + m + 2 * (n % 2)
                        tg = ['qt%d' % i, 'kt%d' % i, 'vt%d' % b]
                        self.attn_pass(qt[i], kt[i], vt[b], 32, g, B_SCALE, self.cb[:, h:h + 1], mult_b(g, eb[b], 'eb%d' % b), tg, pso, ptb, ptn)
                        self.attn_norm(pso, osb[m], rc, 'osb%d' % m)
                    ob = n % 2
                    n += 1
                    em.op('dve', lambda e: e.scalar_tensor_tensor(out=od[:], in0=osb[1][0:64, :], scalar=lamc[:, 3:4], in1=osb[0][0:64, :], op0=ALU.mult, op1=ALU.add), r=['osb0', 'osb1', 'lamc'], w=['od'])
                    em.op('act', lambda e: e.activation(out=sq[:], in_=od[:], func=ACT.Square), r=['od'], w=['sqd'])
                    pi = 0
                    self.mm(pi, 64, self.ones_f[0:64, 0:64], sq[:], True, True, ['ones', 'sqd'])
                    em.op('act', lambda e: e.activation(out=sq[:], in_=self.ps[pi][0:64, :], func=ACT.Sqrt, bias=epsr[:, 0:1], scale=1.0 / 64), r=['ps%d' % pi, 'epsr2'], w=['sqd'])
                    em.op('dve', lambda e: e.reciprocal(out=sq[:], in_=sq[:]), r=['sqd'], w=['sqd'])
                    em.op('dve', lambda e, ob=ob: e.scalar_tensor_tensor(out=o16[ob][:], in0=od[:], scalar=sg[:, 0:1], in1=sq[:], op0=ALU.mult, op1=ALU.mult), r=['od', 'sqd', 'sg'], w=['o16%d' % ob])
                    em.dma('sp', self.OT[512 + h * 64:512 + (h + 1) * 64, g * 512:(g + 1) * 512], o16[ob][:], r=['o16%d' % ob], w=['OT'])
        self.resv = set()
        self.dbg_dump('OT', self.OT)

    def outproj_ln1(self, l):
        em = self.em
        j = l // 2
        WO = self.ev_w_o[j] if l % 2 == 0 else self.od_w_o[j]
        with self.phase():
            self.norm_tmps()
            wo = self.sb("wo", [128, 8, D], BF16)
            self.wload(wo, WO, 8, 0, D, 'wo')
            g1 = self.sb("g1", [128, 8])
            b1 = self.sb("b1", [128, 8])
            em.dma('sp', g1[:], self.ln["ln1_g"][l], w=['gcols'])
            em.dma('sp', b1[:], self.ln["ln1_b"][l], w=['gcols'])
            hsrc = self.hsrc(l).rearrange("(k p) s -> p k s", p=128)
            hdst = self.hT.rearrange("(k p) s -> p k s", p=128)
            otv = self.OT.rearrange("(k p) s -> p k s", p=128)
            ot = [self.sb("ot%d" % i, [128, 8, 512], BF16) for i in range(2)]
            hr = [self.sb("hr%d" % i, [128, 8, 512]) for i in range(2)]
            y = self.sb("y", [128, 8, 512])
            ho = [self.sb("ho%d" % i, [128, 8, 512]) for i in range(2)]
            for g in range(self.NG):
                gs = slice(g * 512, (g + 1) * 512)
                b = g % 2
                em.dma('sp', ot[b][:], otv[:, :, gs], w=['ot%d' % b])
                em.dma('sp', hr[b][:], hsrc[:, :, gs], w=['hr%d' % b])
                for n in range(8):
                    pi = self.nps()
                    for k in range(8):
                        self.mm(pi, 128, wo[:, k, n * 128:(n + 1) * 128], ot[b][:, k, :], k == 0, k == 7, ['wo', 'ot%d' % b])
                    em.op('dve', lambda e, n=n, pi=pi: e.scalar_tensor_tensor(out=y[:, n, :], in0=hr[b][:, n, :], scalar=DN_ALPHA, in1=self.ps[pi][:], op0=ALU.mult, op1=ALU.add), r=['ps%d' % pi, 'hr%d' % b], w=['y'])
                self.ln_fm(y, g1, b1, ho[b], 'y', 'ho%d' % b)
                em.dma('sp', hdst[:, :, gs], ho[b][:], r=['ho%d' % b], w=['hT'])
        self.dbg_dump('h1', self.hT)

    def odd_proj(self, l):
        em, S = self.em, self.S
        j = l // 2
        with self.phase():
            win = self.sb("win", [128, 8, 2640], BF16)
            wsw = self.sb("wsw", [128, 8, 1088], BF16)
            W = self.od_w_in[j]
            self.wload(win, W, 8, 0, 2640, 'win')
            self.wload(wsw, W, 8, 1536, 2624, 'wsw')
            for hh in range(17):
                c0 = 1536 + hh * 64
                self.wload(wsw, W, 8, c0 + 16, c0 + 32, 'wsw', hh * 64)
                self.wload(wsw, W, 8, c0, c0 + 16, 'wsw', hh * 64 + 16)
            hsrc = self.hsrc(l).rearrange("(k p) s -> p k s", p=128)
            hb = [self.sb("hb%d" % i, [128, 8, 512], BF16) for i in range(2)]
            cs = [self.sb("cs%d" % i, [128, 512]) for i in range(2)]
            sn = [self.sb("sn%d" % i, [128, 512]) for i in range(2)]
            t1 = self.sb("t1", [128, 512])
            t2 = self.sb("t2", [128, 512])
            ob = [self.sb("ob%d" % i, [128, 512], BF16) for i in range(4)]
            wo_ = [self.sb("wio%d" % i, [128, 16]) for i in range(2)]
            obn = [0]

            def evac_store(pi, rows, dst, ncol=512):
                i = obn[0] % 4
                obn[0] += 1
                if i % 2 == 0:
                    em.op('act', lambda e: e.copy(out=ob[i][0:rows, 0:ncol], in_=self.ps[pi][0:rows, 0:ncol]), r=['ps%d' % pi], w=['ob%d' % i])
                else:
                    em.op('dve', lambda e: e.tensor_copy(out=ob[i][0:rows, 0:ncol], in_=self.ps[pi][0:rows, 0:ncol]), r=['ps%d' % pi], w=['ob%d' % i])
                em.dma('sp', dst, ob[i][0:rows, 0:ncol], r=['ob%d' % i], w=['scr'])

            def rope_store(pa, pb, rows, g, dst):
                cg, sg_ = cs[g % 2], sn[g % 2]
                em.op('dve', lambda e: e.tensor_tensor(out=t1[0:rows, :], in0=self.ps[pa][0:rows, :], in1=cg[0:rows, :], op=ALU.mult), r=['ps%d' % pa, 'cs%d' % (g % 2)], w=['t1'])
                em.op('dve', lambda e: e.tensor_tensor(out=t2[0:rows, :], in0=self.ps[pb][0:rows, :], in1=sg_[0:rows, :], op=ALU.mult), r=['ps%d' % pb, 'sn%d' % (g % 2)], w=['t2'])
                i = obn[0] % 4
                obn[0] += 1
                em.op('pool', lambda e: e.tensor_tensor(out=ob[i][0:rows, :], in0=t1[0:rows, :], in1=t2[0:rows, :], op=ALU.add), r=['t1', 't2'], w=['ob%d' % i])
                em.dma('sp', dst, ob[i][0:rows, :], r=['ob%d' % i], w=['scr'])

            nw = 0
            for g in range(self.NG):
                gs = slice(g * 512, (g + 1) * 512)
                H = hb[g % 2]
                ht = 'hb%d' % (g % 2)
                em.dma('pool', H[:], hsrc[:, :, gs], w=[ht])
                em.dma('sp', cs[g % 2][:], self.COSI[:, gs], w=['cs%d' % (g % 2)])
                em.dma('sp', sn[g % 2][:], self.SINI[:, gs], w=['sn%d' % (g % 2)])
                for c in range(8):
                    pi = self.nps()
                    for k in range(8):
                        self.mm(pi, 128, win[:, k, c * 128:(c + 1) * 128], H[:, k, :], k == 0, k == 7, ['win', ht])
                    evac_store(pi, 128, self.QC[c * 128:(c + 1) * 128, gs])
                for c in range(2):
                    pi = self.nps()
                    for k in range(8):
                        self.mm(pi, 128, win[:, k, 1024 + c * 128:1024 + (c + 1) * 128], H[:, k, :], k == 0, k == 7, ['win', ht])
                    evac_store(pi, 128, self.KC[c * 128:(c + 1) * 128, gs])
                for tb in range(4):
                    pi = self.nps()
                    for k in range(8):
                        self.mm(pi, 128, H[:, k, tb * 128:(tb + 1) * 128], win[:, k, 1280:1536], k == 0, k == 7, ['win', ht], cols=(0, 256))
                    r0 = g * 512 + tb * 128
                    evac_store(pi, 128, self.VC[r0:r0 + 128, :], ncol=256)
                    pi = self.nps()
                    for k in range(8):
                        self.mm(pi, 128, H[:, k, tb * 128:(tb + 1) * 128], win[:, k, 2624:2640], k == 0, k == 7, ['win', ht], cols=(0, 16))
                    wb = nw % 2
                    nw += 1
                    em.op('act', lambda e, pi=pi, wb=wb: e.activation(out=wo_[wb][:], in_=self.ps[pi][:, 0:16], func=ACT.Copy, scale=0.25 * IDX_SCALE), r=['ps%d' % pi], w=['wio%d' % wb])
                    em.dma('sp', self.WI[r0:r0 + 128, :], wo_[wb][:], r=['wio%d' % wb], w=['scr'])
                for c in range(8):
                    pa, pb = self.nps(), self.nps()
                    for k in range(8):
                        self.mm(pa, 128, win[:, k, 1536 + c * 128:1536 + (c + 1) * 128], H[:, k, :], k == 0, k == 7, ['win', ht])
                    for k in range(8):
                        self.mm(pb, 128, wsw[:, k, c * 128:(c + 1) * 128], H[:, k, :], k == 0, k == 7, ['wsw', ht])
                    rope_store(pa, pb, 128, g, self.QI[c * 128:(c + 1) * 128, gs])
                pa, pb = self.nps(), self.nps()
                for k in range(8):
                    self.mm(pa, 64, win[:, k, 2560:2624], H[:, k, :], k == 0, k == 7, ['win', ht])
                for k in range(8):
                    self.mm(pb, 64, wsw[:, k, 1024:1088], H[:, k, :], k == 0, k == 7, ['wsw', ht])
                rope_store(pa, pb, 64, g, self.KI[:, gs])
        self.dbg_dump('QC', self.QC)
        self.dbg_dump('QI', self.QI)
        self.dbg_dump('WI', self.WI)

    def dsa(self, l):
        em, S, NB = self.em, self.S, self.NB
        NIT = 30
        NSEL = float(min(256, S // 4))
        with self.phase():
            ki = self.sb("ki", [64, S], BF16)
            kc = [self.sb("kc%d" % i, [64, S], BF16) for i in range(4)]
            vc = [self.sb("vc%d" % i, [128, NB, 65], BF16) for i in range(4)]
            em.dma('sp', ki[:], self.KI, w=['ki'])
            for i in range(4):
                em.dma('sp', kc[i][:], self.KC[i * 64:(i + 1) * 64, :], w=['kc%d' % i])
                em.op('pool', lambda e, i=i: e.memset(vc[i][:, :, 64:65], 1.0), w=['vc%d' % i])
                em.dma('sp', vc[i][:, :, 0:64], self.VC[:, i * 64:(i + 1) * 64].rearrange("(nb p) d -> p nb d", p=128), w=['vc%d' % i])
            qg = self.sb("qg", [64, 16, 512], BF16)
            wig = self.sb("wig", [128, 4, 16])
            acc = self.sb("acc", [128, S])
            mq = self.sb("mq", [128, S], BF16)
            maskT = self.sb("maskT", [128, NB, 512], BF16)
            rl = [self.sb("rl%d" % i, [128, 512]) for i in range(2)]
            ec = [self.sb("ec%d" % i, [128, 6, 512], BF16) for i in range(2)]
            ptb = [self.sb("pt%d" % i, [128, 512], BF16) for i in range(3)]
            ptn = [0]
            osb = [self.sb("osb%d" % i, [128, 512]) for i in range(2)]
            rc = self.sb("rc", [128, 512])
            o16 = [self.sb("o16%d" % i, [64, 512], BF16) for i in range(2)]
            bs = self.sb("bs", [128, 8])
            wt = self.sb("wt", [128, NIT])
            pw2 = self.sb("pw2", [128, NIT])
            cntT = self.sb("cntT", [128, NIT])
            for i in range(NIT):
                em.op('pool', lambda e, i=i: e.memset(pw2[:, i:i + 1], 2.0 ** -(i + 1)), w=['pw2'])
            qiv = self.QI.rearrange("(h d) s -> d h s", d=64)
            qcv = self.QC.rearrange("(h d) s -> d h s", d=64)
            nrl = 0
            nec = 0
            no = 0
            for g in range(self.NG):
                gs = slice(g * 512, (g + 1) * 512)
                em.dma('sp', qg[:], qiv[:, :, gs], w=['qg'])
                em.dma('sp', wig[:], self.WI[g * 512:(g + 1) * 512, :].rearrange("(t p) h -> p t h", p=128), w=['wig'])
                em.op('pool', lambda e: e.memset(maskT[:], 0.0), w=['maskT'])
                for qt in range(4):
                    T = 4 * g + qt
                    Lk = 128 * (T + 1)
                    nck = (Lk + 511) // 512
                    for h in range(16):
                        for kcn in range(nck):
                            n = min(512, Lk - kcn * 512)
                            pi = self.nps()
                            self.mm(pi, 128, qg[:, h, qt * 128:(qt + 1) * 128], ki[:, kcn * 512:kcn * 512 + n], True, True, ['qg', 'ki'], cols=(0, n))
                            ri = nrl % 2
                            nrl += 1
                            em.op('act', lambda e, pi=pi, ri=ri, n=n: e.activation(out=rl[ri][:, 0:n], in_=self.ps[pi][:, 0:n], func=ACT.Relu), r=['ps%d' % pi], w=['rl%d' % ri])
                            ksl = slice(kcn * 512, kcn * 512 + n)
                            if h == 0:
                                em.op('dve', lambda e, ri=ri, n=n, ksl=ksl, qt=qt: e.tensor_scalar(out=acc[:, ksl], in0=rl[ri][:, 0:n], scalar1=wig[:, qt, 0:1], scalar2=None, op0=ALU.mult), r=['rl%d' % ri, 'wig'], w=['acc'])
                            else:
                                em.op('dve', lambda e, ri=ri, n=n, ksl=ksl, qt=qt, h=h: e.scalar_tensor_tensor(out=acc[:, ksl], in0=rl[ri][:, 0:n], scalar=wig[:, qt, h:h + 1], in1=acc[:, ksl], op0=ALU.mult, op1=ALU.add), r=['rl%d' % ri, 'wig', 'acc'], w=['acc'])
                    if T >= 2:
                        em.op('dve', lambda e, Lk=Lk: e.tensor_reduce(out=bs[:, 0:1], in_=acc[:, 0:Lk], axis=AX.X, op=ALU.max), r=['acc'], w=['bs'])
                        em.op('dve', lambda e, Lk=Lk: e.tensor_reduce(out=bs[:, 1:2], in_=acc[:, 0:Lk], axis=AX.X, op=ALU.min), r=['acc', 'bs'], w=['bs'])
                    em.op('dve', lambda e, Lk=Lk: e.memset(acc[0:64, Lk - 64:Lk], NEG), r=['acc'], w=['acc'])
                    if T >= 2:
                        em.op('dve', lambda e: e.tensor_tensor(out=bs[:, 2:3], in0=bs[:, 0:1], in1=bs[:, 1:2], op=ALU.subtract), r=['bs'], w=['bs'])
                        em.op('dve', lambda e: e.tensor_scalar(out=wt[:], in0=pw2[:], scalar1=bs[:, 2:3], scalar2=None, op0=ALU.mult), r=['bs', 'pw2'], w=['wt'])
                        em.op('dve', lambda e: e.tensor_copy(out=bs[:, 3:4], in_=bs[:, 1:2]), r=['bs'], w=['bs'])
                        em.op('dve', lambda e: e.memset(cntT[:], 0.0), w=['cntT'])
                        for i in range(NIT):
                            em.op('dve', lambda e, i=i: e.tensor_tensor(out=bs[:, 4:5], in0=bs[:, 3:4], in1=wt[:, i:i + 1], op=ALU.add), r=['bs', 'wt'], w=['bs'])
                            em.op('dve', lambda e, i=i, Lk=Lk: e.tensor_scalar(out=mq[:, 0:Lk], in0=acc[:, 0:Lk], scalar1=bs[:, 4:5], scalar2=0.0, op0=ALU.is_ge, op1=ALU.add, accum_out=cntT[:, i:i + 1]), r=['acc', 'bs', 'cntT'], w=['mq', 'cntT'])
                            em.op('dve', lambda e, i=i: e.scalar_tensor_tensor(out=bs[:, 5:6], in0=cntT[:, i:i + 1], scalar=NSEL, in1=wt[:, i:i + 1], op0=ALU.is_ge, op1=ALU.mult), r=['cntT', 'wt', 'bs'], w=['bs'])
                            em.op('dve', lambda e: e.tensor_tensor(out=bs[:, 3:4], in0=bs[:, 3:4], in1=bs[:, 5:6], op=ALU.add), r=['bs'], w=['bs'])
                        em.op('dve', lambda e, Lk=Lk: e.tensor_scalar(out=mq[:, 0:Lk], in0=acc[:, 0:Lk], scalar1=bs[:, 3:4], scalar2=None, op0=ALU.is_ge), r=['acc', 'bs'], w=['mq'])
                    else:
                        em.op('dve', lambda e, Lk=Lk: e.tensor_scalar(out=mq[:, 0:Lk], in0=acc[:, 0:Lk], scalar1=-1.0e30, scalar2=None, op0=ALU.is_ge), r=['acc'], w=['mq'])
                    for kb0 in range(0, T + 1, 4):
                        nk = min(4, T + 1 - kb0)
                        pi = self.nps()
                        for q_ in range(nk):
                            kb = kb0 + q_
                            self.mm(pi, 128, mq[:, kb * 128:(kb + 1) * 128], self.ident_bf[:], True, True, ['mq'], cols=(q_ * 128, (q_ + 1) * 128), inc=(q_ == nk - 1))
                        src = fap(self.ps[pi][:, 0:nk * 128], [[128, nk], [1, 128]])
                        em.op('act', lambda e, src=src, kb0=kb0, nk=nk, qt=qt: e.copy(out=maskT[:, kb0:kb0 + nk, qt * 128:(qt + 1) * 128], in_=src), r=['ps%d' % pi], w=['maskT'])
                em.dma('sp', qg[:], qcv[:, :, gs], w=['qg'])
                self.resv = {6, 7}
                for h in range(16):
                    kv = h // 4
                    eb_ = nec % 2
                    nec += 1
                    em.dma('sp', ec[eb_][:], self.EB[8 + h].rearrange("o p q -> p o q"), w=['ec%d' % eb_])

                    def mult_c(kb, P, pt, g=g, eb_=eb_):
                        em.op('dve', lambda e: e.tensor_tensor(out=P[:], in0=P[:], in1=maskT[:, kb, :], op=ALU.mult), r=[pt, 'maskT'], w=[pt])
                        if kb >= 4 * g - 2:
                            em.op('dve', lambda e: e.tensor_tensor(out=P[:], in0=P[:], in1=ec[eb_][:, kb - 4 * g + 2, :], op=ALU.mult), r=[pt, 'ec%d' % eb_], w=[pt])
                    pso = 6 + no % 2
                    ob = no % 2
                    no += 1
                    self.attn_pass(qg[:, h, :], kc[kv], vc[kv], 64, g, C_SCALE, self.cb[:, 8 + h:9 + h], mult_c, ['qg', 'kc%d' % kv, 'vc%d' % kv], pso, ptb, ptn, qoff=0)
                    self.attn_norm(pso, osb[ob], rc, 'osb%d' % ob)
                    em.op('pool', lambda e, ob=ob: e.tensor_copy(out=o16[ob][:], in_=osb[ob][0:64, :]), r=['osb%d' % ob], w=['o16%d' % ob])
                    em.dma('sp', self.OT[h * 64:(h + 1) * 64, gs], o16[ob][:], r=['o16%d' % ob], w=['OT'])
                self.resv = set()
        self.dbg_dump('OT', self.OT)

    def peer_scores(self, l):
        em, S = self.em, self.S
        with self.phase():
            k1 = self.sb("k1", [128, 128])
            k2 = self.sb("k2", [128, 128])
            em.dma('sp', k1[:], self.peer_k1T[l], w=['k1'])
            em.dma('sp', k2[:], self.peer_k2T[l], w=['k2'])
            hsrc = self.hT.rearrange("(k p) s -> p k s", p=128)
            wqv = self.peer_w_q[l].rearrange("(k p) f -> p k f", p=128)
            xf = self.sb("xf", [128, 8, 512])
            wqc = [self.sb("wqc%d" % i, [128, 8, 128]) for i in range(2)]
            qT = [self.sb("qT%d" % i, [128, 512]) for i in range(2)]
            sall = [self.sb("sall%d" % i, [128, 16, 128]) for i in range(4)]
            m16 = self.sb("m16", [128, 16, 24])
            wk = self.sb("wk", [128, 576])
            wk2 = self.sb("wk2", [128, 576])
            cand = self.sb("cand", [128, 576])
            c24 = self.sb("c24", [128, 8, 24])
            ez = self.sb("ez", [128, 8, 16])
            st = self.sb("st", [128, 8, 6])
            eall = self.sb("eall", [128, 16, 128])
            e1o = [self.sb("e1o%d" % i, [128, 8, 128]) for i in range(2)]
            e2o = [self.sb("e2o%d" % i, [128, 8, 128]) for i in range(2)]
            tho = [self.sb("tho%d" % i, [128, 8]) for i in range(2)]
            nt = 0
            for g in range(self.NG):
                gs = slice(g * 512, (g + 1) * 512)
                em.dma('sp', xf[:], hsrc[:, :, gs], w=['xf'])
                for c in range(16):
                    b = c % 2
                    em.dma('sp', wqc[b][:], wqv[:, :, c * 128:(c + 1) * 128], w=['wqc%d' % b])
                    pi = self.nps()
                    for k in range(8):
                        self.mm(pi, 128, wqc[b][:, k, :], xf[:, k, :], k == 0, k == 7, ['wqc%d' % b, 'xf'])
                    em.op('act', lambda e, pi=pi, b=b: e.copy(out=qT[b][:], in_=self.ps[pi][:]), r=['ps%d' % pi], w=['qT%d' % b])
                    kk, kt = (k1, 'k1') if c % 2 == 0 else (k2, 'k2')
                    for tb in range(4):
                        pj = self.nps()
                        self.mm(pj, 128, qT[b][:, tb * 128:(tb + 1) * 128], kk[:], True, True, ['qT%d' % b, kt], cols=(0, 128))
                        em.op('dve' if tb % 2 else 'act', lambda e, pj=pj, tb=tb, c=c: (e.tensor_copy if tb % 2 else e.copy)(out=sall[tb][:, c, :], in_=self.ps[pj][:, 0:128]), r=['ps%d' % pj], w=['sall%d' % tb])
                for tb in range(4):
                    sa = sall[tb]
                    sat = 'sall%d' % tb
                    ob = nt % 2
                    nt += 1
                    for c in range(16):
                        em.op('dve', lambda e, c=c: e.max(out=m16[:, c, 0:8], in_=sa[:, c, :]), r=[sat], w=['m16'])
                        em.op('dve', lambda e, c=c: e.match_replace(out=wk[:, 0:128], in_to_replace=m16[:, c, 0:8], in_values=sa[:, c, :], imm_value=NEG), r=[sat, 'm16'], w=['wk'])
                        em.op('dve', lambda e, c=c: e.max(out=m16[:, c, 8:16], in_=wk[:, 0:128]), r=['wk'], w=['m16'])
                        em.op('dve', lambda e, c=c: e.match_replace(out=wk2[:, 0:128], in_to_replace=m16[:, c, 8:16], in_values=wk[:, 0:128], imm_value=NEG), r=['wk', 'm16'], w=['wk2'])
                        em.op('dve', lambda e, c=c: e.max(out=m16[:, c, 16:24], in_=wk2[:, 0:128]), r=['wk2'], w=['m16'])
                    for h in range(8):
                        a_ap = fap(m16[:, 2 * h, :], [[1, 24], [0, 24]])
                        b_ap = fap(m16[:, 2 * h + 1, :], [[0, 24], [1, 24]])
                        o_ap = fap(cand[:, 0:576], [[24, 24], [1, 24]])
                        em.op('dve', lambda e, a_ap=a_ap, b_ap=b_ap, o_ap=o_ap: e.tensor_tensor(out=o_ap, in0=a_ap, in1=b_ap, op=ALU.add), r=['m16'], w=['cand'])
                        em.op('dve', lambda e, h=h: e.max(out=c24[:, h, 0:8], in_=cand[:]), r=['cand'], w=['c24'])
                        em.op('dve', lambda e, h=h: e.match_replace(out=wk[:], in_to_replace=c24[:, h, 0:8], in_values=cand[:], imm_value=NEG), r=['cand', 'c24'], w=['wk'])
                        em.op('dve', lambda e, h=h: e.max(out=c24[:, h, 8:16], in_=wk[:]), r=['wk'], w=['c24'])
                        em.op('dve', lambda e, h=h: e.match_replace(out=wk2[:], in_to_replace=c24[:, h, 8:16], in_values=wk[:], imm_value=NEG), r=['wk', 'c24'], w=['wk2'])
                        em.op('dve', lambda e, h=h: e.max(out=c24[:, h, 16:24], in_=wk2[:]), r=['wk2'], w=['c24'])
                    mx_ap = fap(m16[:, 0, 0:1], [[24, 16], [0, 128]])
                    em.op('dve', lambda e, mx_ap=mx_ap: e.tensor_tensor(out=eall[:], in0=sa[:], in1=mx_ap, op=ALU.subtract), r=[sat, 'm16'], w=['eall'])
                    em.op('act', lambda e: e.activation(out=eall[:], in_=eall[:], func=ACT.Exp), r=['eall'], w=['eall'])
                    cm_ap = fap(c24[:, 0, 0:1], [[24, 8], [0, 16]])
                    em.op('dve', lambda e, cm_ap=cm_ap: e.tensor_tensor(out=ez[:], in0=c24[:, :, 0:16], in1=cm_ap, op=ALU.subtract), r=['c24'], w=['ez'])
                    em.op('act', lambda e: e.activation(out=ez[:], in_=ez[:], func=ACT.Exp), r=['ez'], w=['ez'])
                    em.op('dve', lambda e: e.tensor_reduce(out=st[:, :, 0], in_=ez[:], axis=AX.X, op=ALU.add), r=['ez'], w=['st'])
                    em.op('dve', lambda e: e.reciprocal(out=st[:, :, 1], in_=st[:, :, 0]), r=['st'], w=['st'])
                    em.op('dve', lambda e: e.tensor_tensor(out=st[:, :, 2], in0=c24[:, :, 15], in1=c24[:, :, 16], op=ALU.add), r=['c24', 'st'], w=['st'])
                    em.op('dve', lambda e: e.scalar_tensor_tensor(out=st[:, :, 3], in0=st[:, :, 2], scalar=0.5, in1=c24[:, :, 0], op0=ALU.mult, op1=ALU.subtract), r=['c24', 'st'], w=['st'])
                    em.op('act', lambda e: e.activation(out=st[:, :, 4], in_=st[:, :, 3], func=ACT.Exp), r=['st'], w=['st'])
                    em.op('dve', lambda e, ob=ob: e.tensor_tensor(out=tho[ob][:], in0=st[:, :, 4], in1=st[:, :, 1], op=ALU.mult), r=['st'], w=['tho%d' % ob])
                    rz_ap = fap(st[:, 0, 1:2], [[6, 8], [0, 128]])
                    e1_ap = fap(eall[:, 0, :], [[256, 8], [1, 128]])
                    e2_ap = fap(eall[:, 1, :], [[256, 8], [1, 128]])
                    em.op('dve', lambda e, ob=ob, rz_ap=rz_ap, e1_ap=e1_ap: e.tensor_tensor(out=e1o[ob][:], in0=e1_ap, in1=rz_ap, op=ALU.mult), r=['eall', 'st'], w=['e1o%d' % ob])
                    em.op('pool', lambda e, ob=ob, e2_ap=e2_ap: e.tensor_copy(out=e2o[ob][:], in_=e2_ap), r=['eall'], w=['e2o%d' % ob])
                    r0 = g * 512 + tb * 128
                    em.dma('sp', self.E1[r0:r0 + 128, :], e1o[ob][:].rearrange("p h n -> p (h n)"), r=['e1o%d' % ob], w=['E1'])
                    em.dma('sp', self.E2[r0:r0 + 128, :], e2o[ob][:].rearrange("p h n -> p (h n)"), r=['e2o%d' % ob], w=['E2'])
                    em.dma('sp', self.TH[r0:r0 + 128, :], tho[ob][:], r=['tho%d' % ob], w=['TH'])
        self.dbg_dump('E1', self.E1)
        self.dbg_dump('TH', self.TH)

    def peer_main(self, l):
        em, S = self.em, self.S
        NEG_ = 32
        with self.phase():
            hsrc = self.hT.rearrange("(k p) s -> p k s", p=128)
            uTv = self.peer_uT[l].rearrange("(k p) e -> p k e", p=128)
            vv = self.peer_v[l].rearrange("(g c p) d -> g p c d", p=128, c=4)
            ftv = self.FT.rearrange("(k p) s -> p k s", p=128)
            xb = self.sb("xb", [128, 8, 512], BF16)
            acc = self.sb("acc", [128, 8, 512])
            u16 = [self.sb("u16%d" % i, [128, 8, 512], BF16) for i in range(2)]
            v16 = [self.sb("v16%d" % i, [128, 4, 1024], BF16) for i in range(2)]
            e1t = [self.sb("e1t%d" % i, [128, 8, 128]) for i in range(4)]
            e2t = [self.sb("e2t%d" % i, [128, 8, 128]) for i in range(4)]
            tht = [self.sb("tht%d" % i, [128, 8]) for i in range(4)]
            glT = self.sb("glT", [128, 4, 512])
            AT = self.sb("AT", [128, 4, 512], BF16)
            yb = [self.sb("yb%d" % i, [128, 512]) for i in range(4)]
            gb = [self.sb("gb%d" % i, [128, 512], BF16) for i in range(6)]
            ny = ng = 0
            for g in range(self.NG):
                gs = slice(g * 512, (g + 1) * 512)
                em.dma('pool', xb[:], hsrc[:, :, gs], w=['xb'])
                for tt in range(4):
                    r0 = g * 512 + tt * 128
                    em.dma('sp', e1t[tt][:].rearrange("p h n -> p (h n)"), self.E1[r0:r0 + 128, :], w=['e1t%d' % tt])
                    em.dma('sp', e2t[tt][:].rearrange("p h n -> p (h n)"), self.E2[r0:r0 + 128, :], w=['e2t%d' % tt])
                    em.dma('sp', tht[tt][:], self.TH[r0:r0 + 128, :], w=['tht%d' % tt])
                for eg in range(NEG_):
                    wb = eg % 2
                    ut, vt_ = 'u16%d' % wb, 'v16%d' % wb
                    em.dma('pool', u16[wb][:], uTv[:, :, eg * 512:(eg + 1) * 512], w=[ut])
                    em.dma('pool', v16[wb][:], vv[eg], w=[vt_])
                    for c in range(4):
                        for k in range(8):
                            self.mm(c, 128, u16[wb][:, k, c * 128:(c + 1) * 128], xb[:, k, :], k == 0, k == 7, [ut, 'xb'])
                        em.op('act', lambda e, c=c: e.activation(out=glT[:, c, :], in_=self.ps[c][:], func=ACT.Gelu_apprx_tanh), r=['ps%d' % c], w=['glT'])
                    for tt in range(4):
                        for h in range(8):
                            yi = ny % 4
                            ny += 1
                            gi = ng % 6
                            ng += 1
                            a_ap = fap(e1t[tt][:, h, eg * 4:eg * 4 + 4], [[1, 4], [0, 128]])
                            b_ap = fap(e2t[tt][:, h, :], [[0, 4], [1, 128]])
                            o_ap = fap(yb[yi][:, :], [[128, 4], [1, 128]])
                            em.op('pool', lambda e, a_ap=a_ap, b_ap=b_ap, o_ap=o_ap: e.tensor_tensor(out=o_ap, in0=a_ap, in1=b_ap, op=ALU.mult), r=['e1t%d' % tt, 'e2t%d' % tt], w=['yb%d' % yi])
                            em.op('dve', lambda e, yi=yi, gi=gi, tt=tt, h=h: e.scalar_tensor_tensor(out=gb[gi][:], in0=yb[yi][:], scalar=tht[tt][:, h:h + 1], in1=yb[yi][:], op0=ALU.is_ge, op1=ALU.mult), r=['yb%d' % yi, 'tht%d' % tt], w=['gb%d' % gi])
                            for c in range(4):
                                self.mm(4 + c, 128, gb[gi][:, c * 128:(c + 1) * 128], self.ident_bf[:], h == 0, h == 7, ['gb%d' % gi], cols=(tt * 128, (tt + 1) * 128), inc=(c == 3))
                    for c in range(4):
                        em.op('dve', lambda e, c=c: e.tensor_tensor(out=AT[:, c, :], in0=glT[:, c, :], in1=self.ps[4 + c][:], op=ALU.mult), r=['glT', 'ps%d' % (4 + c)], w=['AT'])
                    for dch in range(8):
                        pi = dch % 4
                        for c in range(4):
                            self.mm(pi, 128, v16[wb][:, c, dch * 128:(dch + 1) * 128], AT[:, c, :], c == 0, c == 3, [vt_, 'AT'])
                        if eg == 0:
                            em.op('dve', lambda e, dch=dch, pi=pi: e.tensor_copy(out=acc[:, dch, :], in_=self.ps[pi][:]), r=['ps%d' % pi], w=['acc'])
                        else:
                            em.op('dve', lambda e, dch=dch, pi=pi: e.tensor_tensor(out=acc[:, dch, :], in0=acc[:, dch, :], in1=self.ps[pi][:], op=ALU.add), r=['ps%d' % pi, 'acc'], w=['acc'])
                em.dma('sp', ftv[:, :, gs], acc[:], r=['acc'], w=['FT'])
        self.dbg_dump('FT', self.FT)
        self.peer_post(l)

    def peer_post(self, l):
        em = self.em
        with self.phase():
            self.norm_tmps()
            wg = self.sb("wg", [128, 8, D], BF16)
            pw = self.sb("pw", [128, 2, D], BF16)
            self.wload(wg, self.ple_gate_w[l], 8, 0, D, 'wg')
            self.wload(pw, self.ple_w[l], 2, 0, D, 'pw')
            g2 = self.sb("g2", [128, 8])
            b2 = self.sb("b2", [128, 8])
            bg = self.sb("bg", [128, 8])
            em.dma('sp', g2[:], self.ln["ln2_g"][l], w=['gcols'])
            em.dma('sp', b2[:], self.ln["ln2_b"][l], w=['gcols'])
            em.dma('sp', bg[:], self.ln["ple_gate_b"][l], w=['gcols'])
            hv = self.hT.rearrange("(k p) s -> p k s", p=128)
            ftv = self.FT.rearrange("(k p) s -> p k s", p=128)
            pv = self.pT[l].rearrange("(k p) s -> p k s", p=128)
            hr = [self.sb("hr%d" % i, [128, 8, 512]) for i in range(1)]
            ft = [self.sb("ft%d" % i, [128, 8, 512]) for i in range(1)]
            pb = [self.sb("pb%d" % i, [128, 2, 512], BF16) for i in range(1)]
            y = self.sb("y", [128, 8, 512])
            h2 = self.sb("h2", [128, 8, 512])
            h2b = self.sb("h2b", [128, 8, 512], BF16)
            gt = self.sb("gt", [128, 512])
            ho = [self.sb("ho%d" % i, [128, 8, 512]) for i in range(1)]
            for g in range(self.NG):
                gs = slice(g * 512, (g + 1) * 512)
                b = 0
                em.dma('sp', hr[b][:], hv[:, :, gs], w=['hr%d' % b])
                em.dma('sp', ft[b][:], ftv[:, :, gs], w=['ft%d' % b])
                em.dma('pool', pb[b][:], pv[:, :, gs], w=['pb%d' % b])
                for n in range(8):
                    em.op('dve', lambda e, n=n: e.scalar_tensor_tensor(out=y[:, n, :], in0=hr[b][:, n, :], scalar=DN_ALPHA, in1=ft[b][:, n, :], op0=ALU.mult, op1=ALU.add), r=['hr%d' % b, 'ft%d' % b], w=['y'])
                self.ln_fm(y, g2, b2, h2, 'y', 'h2', outb=h2b)
                for n in range(8):
                    pi, pj = self.nps(), self.nps()
                    for k in range(8):
                        self.mm(pi, 128, wg[:, k, n * 128:(n + 1) * 128], h2b[:, k, :], k == 0, k == 7, ['wg', 'h2b'])
                    for k in range(2):
                        self.mm(pj, 128, pw[:, k, n * 128:(n + 1) * 128], pb[b][:, k, :], k == 0, k == 1, ['pw', 'pb%d' % b])
                    em.op('act', lambda e, n=n, pi=pi: e.activation(out=gt[:], in_=self.ps[pi][:], func=ACT.Sigmoid, bias=bg[:, n:n + 1], scale=1.0), r=['ps%d' % pi, 'gcols'], w=['gt'])
                    em.op('dve', lambda e, pj=pj: e.tensor_tensor(out=gt[:], in0=gt[:], in1=self.ps[pj][:], op=ALU.mult), r=['gt', 'ps%d' % pj], w=['gt'])
                    em.op('dve', lambda e, n=n: e.tensor_tensor(out=ho[b][:, n, :], in0=gt[:], in1=h2[:, n, :], op=ALU.add), r=['gt', 'h2'], w=['ho%d' % b])
                em.dma('sp', hv[:, :, gs], ho[b][:], r=['ho%d' % b], w=['hT%d' % g])
        self.dbg_dump('h2', self.hT)

    def finalize(self):
        em = self.em
        em.barrier()
        if not self.dbg:
            em.dma('sp', self.outT, self.hT)
        em.barrier()


def _cols(v, C):
    return np.ascontiguousarray(np.asarray(v, np.float32).reshape(C, 128).T)


def prep_inputs(inp, b, S, L):
    NE, NO = (L + 1) // 2, max(L // 2, 1)
    f = lambda a: np.ascontiguousarray(np.asarray(a, np.float32))
    m = {}
    m["xT"] = f(np.asarray(inp["x"])[b, :S].T)
    m["pT"] = f(np.transpose(np.asarray(inp["p"])[:L, b, :S], (0, 2, 1)))
    m["pos"] = np.ascontiguousarray(np.asarray(inp["positions"])[b, :S].reshape(1, S).astype(np.int32))
    m["rel_bias"] = f(inp["rel_bias"])
    m["ev_w_in"] = f(np.asarray(inp["ev_w_in"])[:NE])
    m["ev_w_uq"] = f(np.asarray(inp["ev_w_uq"])[:NE])
    m["ev_w_ukv"] = f(np.asarray(inp["ev_w_ukv"])[:NE])
    m["ev_q_norm"] = np.stack([_cols(np.asarray(inp["ev_q_norm"])[i], 4) for i in range(NE)])
    m["ev_kv_norm"] = np.stack([_cols(np.asarray(inp["ev_kv_norm"])[i], 2) for i in range(NE)])
    m["ev_lam"] = f(np.stack([np.asarray(inp[k])[:NE] for k in ("ev_lam_q1", "ev_lam_k1", "ev_lam_q2", "ev_lam_k2")], axis=-1))
    m["ev_subln"] = f(np.asarray(inp["ev_subln"])[:NE].reshape(NE, 64, 1))
    m["ev_w_o"] = f(np.asarray(inp["ev_w_o"])[:NE])
    m["od_w_in"] = f(np.asarray(inp["od_w_in"])[:NO])
    m["od_w_o"] = f(np.asarray(inp["od_w_o"])[:NO])
    for k in ("ln1_g", "ln1_b", "ln2_g", "ln2_b", "ple_gate_b"):
        m[k] = np.stack([_cols(np.asarray(inp[k])[i], 8) for i in range(L)])
    m["peer_w_q"] = f(np.asarray(inp["peer_w_q"])[:L])
    m["peer_k1T"] = f(np.transpose(np.asarray(inp["peer_k1"])[:L], (0, 2, 1)))
    m["peer_k2T"] = f(np.transpose(np.asarray(inp["peer_k2"])[:L], (0, 2, 1)))
    m["peer_uT"] = f(np.transpose(np.asarray(inp["peer_u"])[:L], (0, 2, 1)))
    m["peer_v"] = f(np.asarray(inp["peer_v"])[:L])
    m["ple_w"] = f(np.asarray(inp["ple_w"])[:L])
    m["ple_gate_w"] = f(np.asarray(inp["ple_gate_w"])[:L])
    return m


def kernel(**inputs):
    B, S, L = 8, 4096, 4
    prog = Prog(S, L)
    shared = None
    in_maps = []
    for b in range(B):
        m = prep_inputs(inputs, b, S, L)
        if shared is None:
            shared = m
        else:
            for k in m:
                if k not in ("xT", "pT", "pos"):
                    m[k] = shared[k]
        in_maps.append(m)
    res = run_bass_kernel_spmd(prog.nc, in_maps, core_ids=list(range(B)))
    out = np.stack([np.ascontiguousarray(res.results[b]["outT"].T) for b in range(B)])
    return out.astype(np.float32)
```

```python
import math
import numpy as np
import concourse.bass as bass
import concourse.mybir as mybir
from concourse.bass_utils import run_bass_kernel_spmd

F32 = mybir.dt.float32
BF16 = mybir.dt.bfloat16
I32 = mybir.dt.int32
ALU = mybir.AluOpType
ACT = mybir.ActivationFunctionType
AX = mybir.AxisListType

D = 1024
PLE = 256
LN_EPS = 1e-5
RMS_EPS = 1e-6
A_SCALE = 96 ** -0.5
B_SCALE = 32 ** -0.5
C_SCALE = 64 ** -0.5
IDX_SCALE = 64 ** -0.5
DEPTH_FULL = 4
DN_ALPHA = (2 * DEPTH_FULL) ** 0.25
NEG = -3.0e38
RREL = 1279
REL0 = 767


class Em:
    NDMA = 24

    def __init__(self, nc):
        self.nc = nc
        self.eng = {'pe': nc.tensor, 'act': nc.scalar, 'dve': nc.vector,
                    'pool': nc.gpsimd, 'sp': nc.sync}
        self.sems = {}
        self.cnt = {}
        self.seen = {e: {} for e in self.eng}
        self.lastw = {}
        self.readers = {}
        for e in ('pe', 'act', 'dve', 'pool'):
            self.sems[e] = nc.alloc_semaphore("s_" + e)
            self.cnt[e] = 0
        for i in range(self.NDMA):
            s = 'd%d' % i
            self.sems[s] = nc.alloc_semaphore("s_" + s)
            self.cnt[s] = 0
        self.dma_rr = 0
        self.nins = 0

    def _wait(self, eng, deps):
        need = {}
        for (s, v) in deps:
            if s == 'pe' and eng == 'pe':
                continue
            if v > need.get(s, 0):
                need[s] = v
        for s, v in need.items():
            if self.seen[eng].get(s, 0) < v:
                self.eng[eng].wait_ge(self.sems[s], v)
                self.seen[eng][s] = v

    def _deps(self, r, w):
        deps = []
        for x in r:
            t = self.lastw.get(x)
            if t:
                deps.append(t)
        for x in w:
            t = self.lastw.get(x)
            if t:
                deps.append(t)
            rd = self.readers.get(x)
            if rd:
                deps.extend(rd.items())
        return deps

    def _update(self, tok, r, w):
        for x in w:
            self.lastw[x] = tok
            self.readers[x] = {}
        for x in r:
            d = self.readers.setdefault(x, {})
            if tok[1] > d.get(tok[0], 0):
                d[tok[0]] = tok[1]

    def op(self, eng, fn, r=(), w=(), inc=True):
        self._wait(eng, self._deps(r, w))
        ins = fn(self.eng[eng])
        self.nins += 1
        if inc:
            self.cnt[eng] += 1
            ins.then_inc(self.sems[eng], 1)
            tok = (eng, self.cnt[eng])
        else:
            tok = (eng, self.cnt[eng] + 1)
        self._update(tok, r, w)
        return ins

    def dma(self, q, out, in_, r=(), w=(), **kw):
        i = self.dma_rr
        self.dma_rr = (i + 1) % self.NDMA
        s = 'd%d' % i
        deps = self._deps(r, w)
        if self.cnt[s] > 0:
            deps.append((s, self.cnt[s]))
        self._wait(q, deps)
        ins = self.eng[q].dma_start(out=out, in_=in_, **kw)
        ins.then_inc(self.sems[s], 16)
        self.nins += 1
        self.cnt[s] += 16
        tok = (s, self.cnt[s])
        self._update(tok, r, w)
        return ins

    def barrier(self):
        allv = [(s, v) for s, v in self.cnt.items() if v > 0]
        for e in ('pe', 'act', 'dve', 'pool', 'sp'):
            self._wait(e, allv)
        self.lastw.clear()
        self.readers.clear()


def fap(ap, dims):
    return bass.AP(ap.tensor, ap.offset, [list(ap.ap[0])] + [list(d) for d in dims])


class Prog:
    def __init__(self, S, L, dbg=None):
        self.S, self.L = S, L
        self.NB = S // 128
        self.NG = S // 512
        self.NE = (L + 1) // 2
        self.NO = L // 2
        self.dbg = dbg
        nc = self.nc = bass.Bass("TRN2", target_bir_lowering=False)
        self.em = Em(nc)
        self.ps = [nc.alloc_psum_tensor("ps%d" % i, [128, 512], F32) for i in range(8)]
        self.psn = 0
        self.uid = 0
        self.resv = set()
        self.io()
        self.consts()
        for l in range(L):
            if l % 2 == 0:
                self.even_proj(l)
                self.attn_even(l)
            else:
                self.odd_proj(l)
                self.dsa(l)
            self.outproj_ln1(l)
            self.peer_scores(l)
            self.peer_main(l)
        self.finalize()

    def din(self, name, shape, dt=F32):
        return self.nc.dram_tensor(name, list(shape), dt, kind="ExternalInput").ap()

    def dscr(self, name, shape, dt=F32):
        return self.nc.dram_tensor(name, list(shape), dt, kind="Internal").ap()

    def io(self):
        S, L, NE, NO = self.S, self.L, self.NE, max(self.NO, 1)
        self.xT = self.din("xT", [D, S])
        self.pT = self.din("pT", [L, PLE, S])
        self.pos = self.din("pos", [1, S], I32)
        self.rel_bias = self.din("rel_bias", [32, 24])
        self.ev_w_in = self.din("ev_w_in", [NE, D, 2336])
        self.ev_w_uq = self.din("ev_w_uq", [NE, 512, 768])
        self.ev_w_ukv = self.din("ev_w_ukv", [NE, 256, 1024])
        self.ev_q_norm = self.din("ev_q_norm", [NE, 128, 4])
        self.ev_kv_norm = self.din("ev_kv_norm", [NE, 128, 2])
        self.ev_lam = self.din("ev_lam", [NE, 32, 4])
        self.ev_subln = self.din("ev_subln", [NE, 64, 1])
        self.ev_w_o = self.din("ev_w_o", [NE, D, D])
        self.od_w_in = self.din("od_w_in", [NO, D, 2640])
        self.od_w_o = self.din("od_w_o", [NO, D, D])
        self.ln = {k: self.din(k, [L, 128, 8]) for k in ("ln1_g", "ln1_b", "ln2_g", "ln2_b", "ple_gate_b")}
        self.peer_w_q = self.din("peer_w_q", [L, D, 2048])
        self.peer_k1T = self.din("peer_k1T", [L, 128, 128])
        self.peer_k2T = self.din("peer_k2T", [L, 128, 128])
        self.peer_uT = self.din("peer_uT", [L, D, 16384])
        self.peer_v = self.din("peer_v", [L, 16384, D])
        self.ple_w = self.din("ple_w", [L, PLE, D])
        self.ple_gate_w = self.din("ple_gate_w", [L, D, D])
        self.outT = self.nc.dram_tensor("outT", [D, S], F32, kind="ExternalOutput").ap()
        self.hT = self.dscr("hT", [D, S])
        self.QR = self.dscr("QR", [256, S], BF16)
        self.QN = self.dscr("QN", [512, S], BF16)
        self.KR = self.dscr("KR", [32, S], BF16)
        self.KN = self.dscr("KN", [512, S], BF16)
        self.VA = self.dscr("VA", [S, 512], BF16)
        self.QD = self.dscr("QD", [512, S], BF16)
        self.KD = self.dscr("KD", [512, S], BF16)
        self.VD = self.dscr("VD", [S, 512], BF16)
        self.OT = self.dscr("OT", [D, S], BF16)
        self.EB = self.dscr("EB", [24, 6, 128, 512], BF16)
        self.fvec = self.dscr("fvec", [24, RREL])
        self.QC = self.dscr("QC", [1024, S], BF16)
        self.KC = self.dscr("KC", [256, S], BF16)
        self.VC = self.dscr("VC", [S, 256], BF16)
        self.QI = self.dscr("QI", [1024, S], BF16)
        self.KI = self.dscr("KI", [64, S], BF16)
        self.WI = self.dscr("WI", [S, 16])
        self.E1 = self.dscr("E1", [S, 1024])
        self.E2 = self.dscr("E2", [S, 1024])
        self.TH = self.dscr("TH", [S, 8])
        self.FT = self.dscr("FT", [D, S])
        self.COS = self.dscr("COS", [128, S])
        self.SIN = self.dscr("SIN", [128, S])
        self.COSI = self.dscr("COSI", [128, S])
        self.SINI = self.dscr("SINI", [128, S])
        if self.dbg:
            self.dbg_out = self.nc.dram_tensor("dbg", list(self.dbg[1]), self.dbg[2], kind="ExternalOutput").ap()

    def sb(self, name, shape, dt=F32):
        self.uid += 1
        return self.nc.alloc_sbuf_tensor("%s_%d" % (name, self.uid), list(shape), dt)

    def nps(self):
        i = self.psn
        self.psn = (i + 1) % 8
        return i

    def phase(self, name=None):
        prog = self
        prog.phn = getattr(prog, 'phn', 0) + 1
        name = "%s_%d" % (name or "ph", prog.phn)

        class _P:
            def __enter__(s2):
                prog.em.barrier()
                s2.ns = prog.nc.named_scope(name)
                s2.ns.__enter__()
                s2.snap = []
                s2.orig = prog.sb

                def sb(name, shape, dt=F32):
                    prog.uid += 1
                    cm = prog.nc.sbuf_tensor("%s_%d" % (name, prog.uid), list(shape), dt)
                    t = cm.__enter__()
                    s2.snap.append(cm)
                    return t
                prog.sb = sb
                return s2

            def __exit__(s2, *a):
                prog.em.barrier()
                for cm in reversed(s2.snap):
                    cm.__exit__(None, None, None)
                prog.sb = s2.orig
                s2.ns.__exit__(None, None, None)
                return False
        return _P()

    def mm(self, psi, rows, lhsT, rhs, start, stop, r, cols=None, inc=None):
        out = self.ps[psi][0:rows, :] if cols is None else self.ps[psi][0:rows, cols[0]:cols[1]]
        self.em.op('pe', lambda e: e.matmul(out, lhsT=lhsT, rhs=rhs, start=start, stop=stop),
                   r=r, w=['ps%d' % psi], inc=stop if inc is None else inc)

    def dbg_dump(self, key, ap_src_dram):
        if self.dbg and self.dbg[0] == key:
            self.em.barrier()
            self.em.dma('sp', self.dbg_out, ap_src_dram)
            self.em.barrier()

    def consts(self):
        em, nc, S = self.em, self.nc, self.S
        sb = self.sb
        self.ident_bf = sb("identb", [128, 128], BF16)
        self.ident_f = sb("identf", [128, 128])
        self.ones_f = sb("onesf", [128, 128])
        self.sel65 = sb("sel65", [128, 64])
        self.cb = sb("cb", [128, 24])
        self.ncb = sb("ncb", [128, 24])
        self.msk = sb("msk", [128, 4, 512], BF16)
        with self.phase("consts"):
            tmp = self.sb("ctmp", [128, 128])
            em.op('pool', lambda e: e.iota(tmp[:], [[1, 128]], base=0, channel_multiplier=-1,
                                           allow_small_or_imprecise_dtypes=True), w=['ctmp'])
            em.op('dve', lambda e: e.tensor_single_scalar(out=self.ident_bf[:], in_=tmp[:], scalar=0.0, op=ALU.is_equal), r=['ctmp'], w=['idb'])
            em.op('dve', lambda e: e.tensor_single_scalar(out=self.ident_f[:], in_=tmp[:], scalar=0.0, op=ALU.is_equal), r=['ctmp'], w=['idf'])
            em.op('pool', lambda e: e.memset(self.ones_f[:], 1.0), w=['ones'])
            em.op('pool', lambda e: e.memset(self.sel65[:], 0.0), w=['sel'])
            em.op('pool', lambda e: e.memset(self.sel65[64:65, :], 1.0), r=['sel'], w=['sel'])
            qrow = self.sb("qrow", [128, 512])
            em.op('pool', lambda e: e.iota(qrow[:], [[1, 512]], base=0, channel_multiplier=0,
                                           allow_small_or_imprecise_dtypes=True), w=['qrow'])
            for o in range(4):
                em.op('dve', lambda e, o=o: e.tensor_single_scalar(out=self.msk[0:64, o, :], in_=qrow[0:64, :], scalar=float(128 * o), op=ALU.is_ge), r=['qrow'], w=['msk'])
                em.op('dve', lambda e, o=o: e.tensor_single_scalar(out=self.msk[64:128, o, :], in_=qrow[64:128, :], scalar=float(128 * o + 64), op=ALU.is_ge), r=['qrow'], w=['msk'])
            em.dma('sp', self.cb[:], bass.AP(self.rel_bias.tensor, 15 * 24, [[0, 128], [1, 24]]), w=['cb'])
            em.op('dve', lambda e: e.tensor_scalar(out=self.ncb[:], in0=self.cb[:], scalar1=-1.0, scalar2=None, op0=ALU.mult), r=['cb'], w=['ncb'])
            self.rope_tables()
            self.bias_tiles()

    def rope_tables(self):
        em, S, NB = self.em, self.S, self.NB
        posi = self.sb("posi", [128, NB], I32)
        posf = self.sb("posf", [128, NB])
        frow = self.sb("frow", [128, 128])
        sgn = self.sb("sgn", [128, 128])
        ang = self.sb("ang", [128, 128])
        tr = self.sb("trg", [128, 128])
        kf = self.sb("kf", [128, 128])
        ki = self.sb("ki", [128, 128], I32)
        cosT = self.sb("cosT", [128, S])
        sinT = self.sb("sinT", [128, S])
        em.dma('sp', posi[:], bass.AP(self.pos.tensor, 0, [[1, 128], [128, NB]]), w=['posi'], allow_slow_non_contiguous=True)
        em.op('dve', lambda e: e.tensor_copy(out=posf[:], in_=posi[:]), r=['posi'], w=['posf'])
        for j in range(16):
            fr = float(np.float32(10000.0) ** np.float32(-j / 16.0))
            em.op('pool', lambda e, j=j, fr=fr: e.memset(fap(frow[:, j:j + 1], [[16, 8], [1, 1]]), fr), w=['frow'])
        em.op('pool', lambda e: e.memset(sgn[:], 1.0), w=['sgn'])
        em.op('pool', lambda e: e.memset(fap(sgn[:, 0:1], [[32, 4], [1, 16]]), -1.0), r=['sgn'], w=['sgn'])
        TWO_PI = 2.0 * math.pi
        for tb in range(NB):
            for which, tab in ((0, sinT), (1, cosT)):
                em.op('dve', lambda e, tb=tb: e.tensor_scalar(out=ang[:], in0=frow[:], scalar1=posf[:, tb:tb + 1], scalar2=None, op0=ALU.mult), r=['frow', 'posf'], w=['ang'])
                if which == 1:
                    em.op('dve', lambda e: e.tensor_scalar(out=ang[:], in0=ang[:], scalar1=math.pi / 2, scalar2=None, op0=ALU.add), r=['ang'], w=['ang'])
                em.op('dve', lambda e: e.tensor_scalar(out=kf[:], in0=ang[:], scalar1=1.0 / TWO_PI, scalar2=None, op0=ALU.mult), r=['ang'], w=['kf'])
                em.op('dve', lambda e: e.tensor_copy(out=ki[:], in_=kf[:]), r=['kf'], w=['ki'])
                em.op('dve', lambda e: e.tensor_copy(out=kf[:], in_=ki[:]), r=['ki'], w=['kf'])
                em.op('dve', lambda e: e.scalar_tensor_tensor(out=ang[:], in0=kf[:], scalar=-TWO_PI, in1=ang[:], op0=ALU.mult, op1=ALU.add), r=['kf', 'ang'], w=['ang'])
                em.op('dve', lambda e: e.tensor_scalar(out=kf[:], in0=ang[:], scalar1=math.pi, scalar2=-TWO_PI, op0=ALU.is_gt, op1=ALU.mult), r=['ang'], w=['kf'])
                em.op('dve', lambda e: e.tensor_tensor(out=ang[:], in0=ang[:], in1=kf[:], op=ALU.add), r=['kf', 'ang'], w=['ang'])
                em.op('dve', lambda e: e.tensor_scalar(out=kf[:], in0=ang[:], scalar1=-math.pi, scalar2=TWO_PI, op0=ALU.is_lt, op1=ALU.mult), r=['ang'], w=['kf'])
                em.op('dve', lambda e: e.tensor_tensor(out=ang[:], in0=ang[:], in1=kf[:], op=ALU.add), r=['kf', 'ang'], w=['ang'])
                em.op('dve', lambda e: e.tensor_scalar(out=ang[:], in0=ang[:], scalar1=-3.14159, scalar2=3.14159, op0=ALU.max, op1=ALU.min), r=['ang'], w=['ang'])
                em.op('act', lambda e: e.activation(out=tr[:], in_=ang[:], func=ACT.Sin), r=['ang'], w=['trg'])
                if which == 0:
                    em.op('dve', lambda e: e.tensor_tensor(out=tr[:], in0=tr[:], in1=sgn[:], op=ALU.mult), r=['trg', 'sgn'], w=['trg'])
                pi = self.nps()
                self.mm(pi, 128, tr[:], self.ident_f[:], True, True, ['trg', 'idf'], cols=(0, 128))
                em.op('act', lambda e, pi=pi, tab=tab, tb=tb: e.copy(out=tab[:, tb * 128:(tb + 1) * 128], in_=self.ps[pi][:, 0:128]), r=['ps%d' % pi], w=['ropetab'])
        em.dma('sp', self.COS, cosT[:], r=['ropetab'], w=['COS'])
        em.dma('sp', self.SIN, sinT[:], r=['ropetab'], w=['SIN'])
        for r0 in (32, 96):
            em.op('pool', lambda e, r0=r0: e.memset(cosT[r0:r0 + 32, :], 1.0), r=['ropetab', 'COS'], w=['ropetab'])
            em.op('pool', lambda e, r0=r0: e.memset(sinT[r0:r0 + 32, :], 0.0), r=['ropetab', 'SIN'], w=['ropetab'])
        em.dma('sp', self.COSI, cosT[:], r=['ropetab'], w=['COSI'])
        em.dma('sp', self.SINI, sinT[:], r=['ropetab'], w=['SINI'])

    def bias_tiles(self):
        em = self.em
        rel = self.sb("relrow", [32, RREL])
        bk = self.sb("bk", [32, RREL])
        oh = self.sb("oh", [32, RREL])
        pidx = self.sb("pidx", [32, 1])
        tab = self.sb("tab", [32, 24])
        fv = self.sb("fv", [24, RREL])
        em.dma('sp', tab[:], self.rel_bias, w=['tab'])
        em.op('pool', lambda e: e.iota(rel[:], [[1, RREL]], base=-REL0, channel_multiplier=0, allow_small_or_imprecise_dtypes=True), w=['rel'])
        em.op('pool', lambda e: e.iota(pidx[:], [[1, 1]], base=0, channel_multiplier=1, allow_small_or_imprecise_dtypes=True), w=['pidx'])
        em.op('dve', lambda e: e.memset(bk[:], 15.0), w=['bk'])
        steps = [(t, -1.0) for t in (-165, -107, -69, -45, -29, -19, -12, -7, -6, -5, -4, -3, -2, -1, 0)]
        steps += [(1, 17.0)] + [(t, 1.0) for t in (2, 3, 4, 5, 6, 7, 8, 13, 20, 30, 46, 70, 108, 166)]
        for th, dlt in steps:
            if dlt in (1.0, -1.0):
                op1 = ALU.add if dlt > 0 else ALU.subtract
            em.op('dve', lambda e, th=th, dlt=dlt: e.tensor_scalar(out=oh[:], in0=rel[:], scalar1=float(th), scalar2=float(dlt), op0=ALU.is_ge, op1=ALU.mult), r=['rel'], w=['oh'])
            em.op('dve', lambda e: e.tensor_tensor(out=bk[:], in0=bk[:], in1=oh[:], op=ALU.add), r=['oh', 'bk'], w=['bk'])
        em.op('dve', lambda e: e.tensor_scalar(out=oh[:], in0=bk[:], scalar1=pidx[:, 0:1], scalar2=None, op0=ALU.is_equal), r=['bk', 'pidx'], w=['oh'])
        for c0 in range(0, RREL, 512):
            n = min(512, RREL - c0)
            pi = self.nps()
            self.mm(pi, 24, tab[:], oh[:, c0:c0 + n], True, True, ['tab', 'oh'], cols=(0, n))
            em.op('act', lambda e, pi=pi, c0=c0, n=n: e.copy(out=fv[:, c0:c0 + n], in_=self.ps[pi][0:24, 0:n]), r=['ps%d' % pi], w=['fv'])
        em.dma('sp', self.fvec, fv[:], r=['fv'], w=['fvec'])
        antid = self.sb("antid", [128, 128])
        em.op('pool', lambda e: e.iota(antid[:], [[1, 128]], base=-127, channel_multiplier=1, allow_small_or_imprecise_dtypes=True), w=['antid'])
        em.op('dve', lambda e: e.tensor_single_scalar(out=antid[:], in_=antid[:], scalar=0.0, op=ALU.is_equal), r=['antid'], w=['antid'])
        tt = [self.sb("toep%d" % i, [128, 4, 128]) for i in range(2)]
        tx = [self.sb("toepx%d" % i, [128, 512]) for i in range(2)]
        te = [self.sb("toepb%d" % i, [128, 512], BF16) for i in range(2)]
        n = 0
        for h in range(24):
            for oi in range(6):
                o = oi - 2
                b = n % 2
                n += 1
                src = bass.AP(self.fvec.tensor, h * RREL + 128 * o + REL0 - 127, [[1, 128], [-128, 4], [1, 128]])
                em.dma('sp', tt[b][:], src, r=['fvec'], w=['toep%d' % b])
                pi = self.nps()
                for j in range(4):
                    self.mm(pi, 128, tt[b][:, j, :], antid[:], True, True, ['toep%d' % b, 'antid'], cols=(j * 128, (j + 1) * 128), inc=(j == 3))
                if h < 8 and o >= 0:
                    em.op('act', lambda e, b=b, h=h, pi=pi: e.activation(out=tx[b][:], in_=self.ps[pi][:], func=ACT.Exp, bias=self.ncb[:, h:h + 1], scale=1.0), r=['ps%d' % pi, 'ncb'], w=['toepx%d' % b])
                    em.op('dve', lambda e, b=b, o=o: e.tensor_tensor(out=te[b][:], in0=tx[b][:], in1=self.msk[:, o, :], op=ALU.mult), r=['toepx%d' % b, 'msk'], w=['toepb%d' % b])
                else:
                    em.op('act', lambda e, b=b, h=h, pi=pi: e.activation(out=te[b][:], in_=self.ps[pi][:], func=ACT.Exp, bias=self.ncb[:, h:h + 1], scale=1.0), r=['ps%d' % pi, 'ncb'], w=['toepb%d' % b])
                em.dma('sp', self.EB[h, oi], te[b][:], r=['toepb%d' % b], w=['EB'])
        self.dbg_dump('fvec', self.fvec)

    def rms_fm(self, src, C, F, gcol, dst, tag):
        em = self.em
        sq = self.t_sq
        pi = self.nps()
        for c in range(C):
            em.op('act', lambda e, c=c: e.activation(out=sq[:, c, :], in_=src[:, c, :], func=ACT.Square), r=[tag], w=['sq%d' % c])
            self.mm(pi, 128, self.ones_f[:], sq[:, c, :], c == 0, c == C - 1, ['ones', 'sq%d' % c])
        rs = self.t_rs
        em.op('act', lambda e: e.activation(out=rs[:], in_=self.ps[pi][:], func=ACT.Sqrt, bias=self.eps_rms[:, 0:1], scale=1.0 / F), r=['ps%d' % pi, 'eps'], w=['rs'])
        em.op('dve', lambda e: e.reciprocal(out=rs[:], in_=rs[:]), r=['rs'], w=['rs'])
        for c in range(C):
            em.op('dve', lambda e, c=c: e.scalar_tensor_tensor(out=dst[:, c, :], in0=src[:, c, :], scalar=gcol[:, c:c + 1], in1=rs[:], op0=ALU.mult, op1=ALU.mult), r=[tag, 'rs', 'gcols'], w=[tag + 'n'])

    def ln_fm(self, y, gcol, bcol, outf, tag, otag, outb=None):
        em = self.em
        sq = self.t_sq
        p1, p2 = self.nps(), self.nps()
        for c in range(8):
            self.mm(p1, 128, self.ones_f[:], y[:, c, :], c == 0, c == 7, [tag])
        for c in range(8):
            em.op('act', lambda e, c=c: e.activation(out=sq[:, c, :], in_=y[:, c, :], func=ACT.Square), r=[tag], w=['sq%d' % c])
            self.mm(p2, 128, self.ones_f[:], sq[:, c, :], c == 0, c == 7, ['ones', 'sq%d' % c])
        mean, rs, msq = self.t_mean, self.t_rs, self.t_msq
        em.op('act', lambda e: e.activation(out=mean[:], in_=self.ps[p1][:], func=ACT.Copy, scale=1.0 / D), r=['ps%d' % p1], w=['mean'])
        em.op('dve', lambda e: e.tensor_tensor(out=msq[:], in0=mean[:], in1=mean[:], op=ALU.mult), r=['mean'], w=['msq'])
        em.op('dve', lambda e: e.scalar_tensor_tensor(out=rs[:], in0=self.ps[p2][:], scalar=1.0 / D, in1=msq[:], op0=ALU.mult, op1=ALU.subtract), r=['ps%d' % p2, 'msq'], w=['rs'])
        em.op('act', lambda e: e.activation(out=rs[:], in_=rs[:], func=ACT.Sqrt, bias=self.eps_ln[:, 0:1], scale=1.0), r=['rs', 'eps'], w=['rs'])
        em.op('dve', lambda e: e.reciprocal(out=rs[:], in_=rs[:]), r=['rs'], w=['rs'])
        for c in range(8):
            em.op('dve', lambda e, c=c: e.tensor_tensor(out=y[:, c, :], in0=y[:, c, :], in1=mean[:], op=ALU.subtract), r=[tag, 'mean'], w=[tag])
            em.op('dve', lambda e, c=c: e.tensor_tensor(out=y[:, c, :], in0=y[:, c, :], in1=rs[:], op=ALU.mult), r=[tag, 'rs'], w=[tag])
            em.op('act', lambda e, c=c: e.activation(out=outf[:, c, :], in_=y[:, c, :], func=ACT.Identity, bias=bcol[:, c:c + 1], scale=gcol[:, c:c + 1]), r=[tag, 'gcols'], w=[otag])
            if outb is not None:
                em.op('pool', lambda e, c=c: e.tensor_copy(out=outb[:, c, :], in_=outf[:, c, :]), r=[otag], w=[otag + 'b'])

    def norm_tmps(self):
        self.t_sq = self.sb("sq", [128, 8, 512])
        self.t_rs = self.sb("rs", [128, 512])
        self.t_mean = self.sb("mean", [128, 512])
        self.t_msq = self.sb("msq", [128, 512])
        self.eps_rms = self.sb("epsr", [128, 1])
        self.eps_ln = self.sb("epsl", [128, 1])
        self.em.op('pool', lambda e: e.memset(self.eps_rms[:], RMS_EPS), w=['eps'])
        self.em.op('pool', lambda e: e.memset(self.eps_ln[:], LN_EPS), r=['eps'], w=['eps'])

    def wload(self, dst, src2d, K, c0, c1, tag, dcol=0):
        for k in range(K):
            self.em.dma('pool', dst[:, k, dcol:dcol + (c1 - c0)], src2d[k * 128:(k + 1) * 128, c0:c1], w=[tag])

    def hsrc(self, l):
        return self.xT if l == 0 else self.hT

    def even_proj(self, l):
        em, S = self.em, self.S
        j = l // 2
        with self.phase("even_proj"):
            self.norm_tmps()
            win = self.sb("win", [128, 8, 2336], BF16)
            wkrs = self.sb("wkrs", [128, 8, 32], BF16)
            wqn = self.sb("wqn", [128, 4, 512], BF16)
            wqr = self.sb("wqr", [128, 4, 256], BF16)
            wqs = self.sb("wqs", [128, 4, 256], BF16)
            wkn = self.sb("wkn", [128, 2, 512], BF16)
            wv = self.sb("wv", [128, 2, 512], BF16)
            gq = self.sb("gq", [128, 4])
            gkv = self.sb("gkv", [128, 2])
            W = self.ev_w_in[j]
            self.wload(win, W, 8, 0, 2336, 'win')
            self.wload(wkrs, W, 8, 768 + 16, 768 + 32, 'wkrs', 0)
            self.wload(wkrs, W, 8, 768, 768 + 16, 'wkrs', 16)
            UQ, UKV = self.ev_w_uq[j], self.ev_w_ukv[j]
            for h in range(8):
                self.wload(wqn, UQ, 4, h * 96, h * 96 + 64, 'wqn', h * 64)
                self.wload(wqr, UQ, 4, h * 96 + 64, h * 96 + 96, 'wqr', h * 32)
                self.wload(wqs, UQ, 4, h * 96 + 80, h * 96 + 96, 'wqs', h * 32)
                self.wload(wqs, UQ, 4, h * 96 + 64, h * 96 + 80, 'wqs', h * 32 + 16)
                self.wload(wkn, UKV, 2, h * 128, h * 128 + 64, 'wkn', h * 64)
                self.wload(wv, UKV, 2, h * 128 + 64, h * 128 + 128, 'wv', h * 64)
            em.dma('sp', gq[:], self.ev_q_norm[j], w=['gcols'])
            em.dma('sp', gkv[:], self.ev_kv_norm[j], w=['gcols'])
            hsrc = self.hsrc(l).rearrange("(k p) s -> p k s", p=128)
            hb = [self.sb("hb%d" % i, [128, 8, 512], BF16) for i in range(2)]
            cq = self.sb("cq", [128, 4, 512])
            ckv = self.sb("ckv", [128, 2, 512])
            cqn = self.sb("cqn", [128, 4, 512], BF16)
            ckvn = self.sb("ckvn", [128, 2, 512], BF16)
            t1 = self.sb("t1", [128, 512])
            t2 = self.sb("t2", [128, 512])
            ob = [self.sb("ob%d" % i, [128, 512], BF16) for i in range(4)]
            obn = [0]

            def evac_store(pi, rows, dst, eng=None):
                i = obn[0] % 4
                obn[0] += 1
                eng = eng or ('act' if i % 2 == 0 else 'dve')
                if eng == 'act':
                    em.op('act', lambda e: e.copy(out=ob[i][0:rows, :], in_=self.ps[pi][0:rows, :]), r=['ps%d' % pi], w=['ob%d' % i])
                else:
                    em.op('dve', lambda e: e.tensor_copy(out=ob[i][0:rows, :], in_=self.ps[pi][0:rows, :]), r=['ps%d' % pi], w=['ob%d' % i])
                em.dma('sp', dst, ob[i][0:rows, :], r=['ob%d' % i], w=['scr'])

            cs = [self.sb("cs%d" % i, [128, 512]) for i in range(2)]
            sn = [self.sb("sn%d" % i, [128, 512]) for i in range(2)]

            def rope_store(pa, pb, rows, g, dst):
                cg, sg_ = cs[g % 2], sn[g % 2]
                em.op('dve', lambda e: e.tensor_tensor(out=t1[0:rows, :], in0=self.ps[pa][0:rows, :], in1=cg[0:rows, :], op=ALU.mult), r=['ps%d' % pa, 'cs%d' % (g % 2)], w=['t1'])
                em.op('dve', lambda e: e.tensor_tensor(out=t2[0:rows, :], in0=self.ps[pb][0:rows, :], in1=sg_[0:rows, :], op=ALU.mult), r=['ps%d' % pb, 'sn%d' % (g % 2)], w=['t2'])
                i = obn[0] % 4
                obn[0] += 1
                em.op('pool', lambda e: e.tensor_tensor(out=ob[i][0:rows, :], in0=t1[0:rows, :], in1=t2[0:rows, :], op=ALU.add), r=['t1', 't2'], w=['ob%d' % i])
                em.dma('sp', dst, ob[i][0:rows, :], r=['ob%d' % i], w=['scr'])

            for g in range(self.NG):
                gs = slice(g * 512, (g + 1) * 512)
                H = hb[g % 2]
                ht = 'hb%d' % (g % 2)
                em.dma('pool', H[:], hsrc[:, :, gs], w=[ht])
                em.dma('sp', cs[g % 2][:], self.COS[:, gs], w=['cs%d' % (g % 2)])
                em.dma('sp', sn[g % 2][:], self.SIN[:, gs], w=['sn%d' % (g % 2)])
                for c in range(4):
                    pi = self.nps()
                    for k in range(8):
                        self.mm(pi, 128, win[:, k, c * 128:(c + 1) * 128], H[:, k, :], k == 0, k == 7, ['win', ht])
                    em.op('act', lambda e, c=c, pi=pi: e.copy(out=cq[:, c, :], in_=self.ps[pi][:]), r=['ps%d' % pi], w=['cq'])
                for c in range(2):
                    pi = self.nps()
                    for k in range(8):
                        self.mm(pi, 128, win[:, k, 512 + c * 128:512 + (c + 1) * 128], H[:, k, :], k == 0, k == 7, ['win', ht])
                    em.op('dve', lambda e, c=c, pi=pi: e.tensor_copy(out=ckv[:, c, :], in_=self.ps[pi][:]), r=['ps%d' % pi], w=['ckv'])
                self.rms_fm(cq, 4, 512, gq, cqn, 'cq')
                self.rms_fm(ckv, 2, 256, gkv, ckvn, 'ckv')
                pa, pb = self.nps(), self.nps()
                for k in range(8):
                    self.mm(pa, 32, win[:, k, 768:800], H[:, k, :], k == 0, k == 7, ['win', ht])
                for k in range(8):
                    self.mm(pb, 32, wkrs[:, k, :], H[:, k, :], k == 0, k == 7, ['wkrs', ht])
                rope_store(pa, pb, 32, g, self.KR[:, gs])
                for c in range(4):
                    for base, dst in ((800, self.QD), (1312, self.KD)):
                        pi = self.nps()
                        for k in range(8):
                            self.mm(pi, 128, win[:, k, base + c * 128:base + (c + 1) * 128], H[:, k, :], k == 0, k == 7, ['win', ht])
                        evac_store(pi, 128, dst[c * 128:(c + 1) * 128, gs])
                for tb in range(4):
                    pi = self.nps()
                    for k in range(8):
                        self.mm(pi, 128, H[:, k, tb * 128:(tb + 1) * 128], win[:, k, 1824:2336], k == 0, k == 7, ['win', ht])
                    evac_store(pi, 128, self.VD[g * 512 + tb * 128:g * 512 + (tb + 1) * 128, :])
                for rc in range(2):
                    pa, pb = self.nps(), self.nps()
                    for k in range(4):
                        self.mm(pa, 128, wqr[:, k, rc * 128:(rc + 1) * 128], cqn[:, k, :], k == 0, k == 3, ['wqr', 'cqn'])
                    for k in range(4):
                        self.mm(pb, 128, wqs[:, k, rc * 128:(rc + 1) * 128], cqn[:, k, :], k == 0, k == 3, ['wqs', 'cqn'])
                    rope_store(pa, pb, 128, g, self.QR[rc * 128:(rc + 1) * 128, gs])
                for c in range(4):
                    pi = self.nps()
                    for k in range(4):
                        self.mm(pi, 128, wqn[:, k, c * 128:(c + 1) * 128], cqn[:, k, :], k == 0, k == 3, ['wqn', 'cqn'])
                    evac_store(pi, 128, self.QN[c * 128:(c + 1) * 128, gs])
                    pi = self.nps()
                    for k in range(2):
                        self.mm(pi, 128, wkn[:, k, c * 128:(c + 1) * 128], ckvn[:, k, :], k == 0, k == 1, ['wkn', 'ckvn'])
                    evac_store(pi, 128, self.KN[c * 128:(c + 1) * 128, gs])
                for tb in range(4):
                    pi = self.nps()
                    for k in range(2):
                        self.mm(pi, 128, ckvn[:, k, tb * 128:(tb + 1) * 128], wv[:, k, :], k == 0, k == 1, ['wv', 'ckvn'])
                    evac_store(pi, 128, self.VA[g * 512 + tb * 128:g * 512 + (tb + 1) * 128, :])
        self.dbg_dump('QR', self.QR)
        self.dbg_dump('QN', self.QN)
        self.dbg_dump('KR', self.KR)
        self.dbg_dump('VA', self.VA)
        self.dbg_dump('QD', self.QD)

    def attn_begin(self, p, LA=2):
        p['q'] = []
        p['n'] = 4 * p['g'] + 4
        for kb in range(min(LA, p['n'])):
            p['q'].append(self.attn_qk(p, kb))

    def attn_qk(self, p, kb):
        pi = self.nps()
        while pi in self.resv:
            pi = self.nps()
        rows = p['rows']
        q0 = p['g'] * 512 if p.get('qoff') is None else p['qoff']
        self.mm(pi, 128, p['kt'][0:rows, kb * 128:(kb + 1) * 128], p['qt'][0:rows, q0:q0 + 512], True, True, p['tags'])
        return pi

    def attn_body(self, p, ptb, ptn, LA=2):
        em = self.em
        n = p['n']
        scale, bias_col = p['scale'], p['bias']
        for kb in range(n):
            pi = p['q'][kb]
            i = ptn[0] % len(ptb)
            ptn[0] += 1
            P = ptb[i]
            pt = 'pt%d' % i
            if bias_col is None:
                em.op('act', lambda e, pi=pi, P=P: e.activation(out=P[:], in_=self.ps[pi][:], func=ACT.Exp, scale=scale), r=['ps%d' % pi], w=[pt])
            else:
                em.op('act', lambda e, pi=pi, P=P: e.activation(out=P[:], in_=self.ps[pi][:], func=ACT.Exp, bias=bias_col, scale=scale), r=['ps%d' % pi, 'cb'], w=[pt])
            p['mult'](kb, P, pt)
            if kb + LA < n:
                p['q'].append(self.attn_qk(p, kb + LA))
            self.mm(p['pso'], 65, p['vt'][:, kb, :], P[:], kb == 0, kb == n - 1, [pt] + p['tags'])

    def attn_run(self, passes, ptb, ptn):
        if not passes:
            return
        self.attn_begin(passes[0])
        for i, p in enumerate(passes):
            if 'pre' in p:
                pass
            self.attn_body(p, ptb, ptn)
            if i + 1 < len(passes):
                if 'load' in passes[i + 1]:
                    passes[i + 1]['load']()
                self.attn_begin(passes[i + 1])
            p['finish']()

    def attn_norm(self, pso, osb, rc, tag):
        em = self.em
        em.op('act', lambda e: e.copy(out=osb[0:65, :], in_=self.ps[pso][0:65, :]), r=['ps%d' % pso], w=[tag])
        pi = self.nps()
        while pi in self.resv:
            pi = self.nps()
        self.mm(pi, 64, self.sel65[0:65, :], osb[0:65, :], True, True, ['sel', tag])
        em.op('dve', lambda e: e.reciprocal(out=rc[0:64, :], in_=self.ps[pi][0:64, :]), r=['ps%d' % pi], w=['rc'])
        em.op('dve', lambda e: e.tensor_tensor(out=osb[0:64, :], in0=osb[0:64, :], in1=rc[0:64, :], op=ALU.mult), r=['rc', tag], w=[tag])

    def attn_even(self, l):
        em, S, NB = self.em, self.S, self.NB
        j = l // 2
        lam_init = 0.8 - 0.6 * math.exp(-0.3 * l)
        with self.phase("attn_even"):
            ptb = [self.sb("pt%d" % i, [128, 512], BF16) for i in range(4)]
            ptn = [0]
            self.resv = {6, 7}
            osb = [self.sb("osb%d" % i, [128, 512]) for i in range(2)]
            rc = self.sb("rc", [128, 512])
            o16 = [self.sb("o16%d" % i, [64, 512], BF16) for i in range(2)]
            qt = [self.sb("qt%d" % i, [96, S], BF16) for i in range(2)]
            kt = [self.sb("kt%d" % i, [96, S], BF16) for i in range(2)]
            vt = [self.sb("vt%d" % i, [128, NB, 65], BF16) for i in range(2)]
            for i in range(2):
                em.op('pool', lambda e, i=i: e.memset(vt[i][:, :, 64:65], 1.0), w=['vt%d' % i])

            def mult_a(g):
                def f(kb, P, pt):
                    if kb >= 4 * g:
                        em.op('dve', lambda e: e.tensor_tensor(out=P[:], in0=P[:], in1=self.msk[:, kb - 4 * g, :], op=ALU.mult), r=[pt, 'msk'], w=[pt])
                return f
            passes = []
            n = 0
            for h in range(8):
                b = h % 2
                tg = ['qt%d' % b, 'kt%d' % b, 'vt%d' % b]

                def load(h=h, b=b, tg=tg):
                    em.dma('sp', qt[b][0:32, :], self.QR[h * 32:(h + 1) * 32, :], w=[tg[0]])
                    em.dma('sp', qt[b][32:96, :], self.QN[h * 64:(h + 1) * 64, :], w=[tg[0]])
                    em.dma('sp', kt[b][0:32, :], self.KR[:, :], w=[tg[1]])
                    em.dma('sp', kt[b][32:96, :], self.KN[h * 64:(h + 1) * 64, :], w=[tg[1]])
                    em.dma('sp', vt[b][:, :, 0:64], self.VA[:, h * 64:(h + 1) * 64].rearrange("(nb p) d -> p nb d", p=128), w=[tg[2]])
                for g in range(self.NG):
                    pso = 6 + n % 2
                    ob = n % 2
                    n += 1

                    def finish(h=h, g=g, pso=pso, ob=ob):
                        self.attn_norm(pso, osb[ob], rc, 'osb%d' % ob)
                        em.op('pool', lambda e: e.tensor_copy(out=o16[ob][:], in_=osb[ob][0:64, :]), r=['osb%d' % ob], w=['o16%d' % ob])
                        em.dma('sp', self.OT[h * 64:(h + 1) * 64, g * 512:(g + 1) * 512], o16[ob][:], r=['o16%d' % ob], w=['OT'])
                    p = dict(qt=qt[b], kt=kt[b], vt=vt[b], rows=96, g=g, scale=A_SCALE, bias=None, mult=mult_a(g), tags=tg, pso=pso, finish=finish)
                    if g == 0:
                        p['load'] = load
                    passes.append(p)
            passes[0]['load']()
            self.attn_run(passes, ptb, ptn)
        self.dbg_dump('OTA', self.OT)
        with self.phase("attn_even"):
            ptb = [self.sb("pt%d" % i, [128, 512], BF16) for i in range(4)]
            ptn = [0]
            self.resv = {4, 5, 6, 7}
            osb = [self.sb("osb%d" % i, [128, 512]) for i in range(2)]
            rc = self.sb("rc", [128, 512])
            od = self.sb("od", [64, 512])
            sq = self.sb("sq", [64, 512])
            o16 = [self.sb("o16%d" % i, [64, 512], BF16) for i in range(2)]
            qt = [self.sb("qt%d" % i, [32, S], BF16) for i in range(4)]
            kt = [self.sb("kt%d" % i, [32, S], BF16) for i in range(4)]
            vt = [self.sb("vt%d" % i, [128, NB, 65], BF16) for i in range(2)]
            eb = [self.sb("eb%d" % i, [128, 6, 512], BF16) for i in range(2)]
            for i in range(2):
                em.op('pool', lambda e, i=i: e.memset(vt[i][:, :, 64:65], 1.0), w=['vt%d' % i])
            lam4 = self.sb("lam4", [32, 4])
            lamp = self.sb("lamp", [32, 2])
            lamc = self.sb("lamc", [64, 4])
            sg = self.sb("sg", [64, 1])
            epsr = self.sb("epsr2", [64, 1])
            em.op('pool', lambda e: e.memset(epsr[:], RMS_EPS), w=['epsr2'])
            em.dma('sp', lam4[:], self.ev_lam[j], w=['lam4'])
            em.dma('sp', sg[:], self.ev_subln[j], w=['sg'])
            em.op('dve', lambda e: e.tensor_tensor(out=lamp[:, 0:1], in0=lam4[:, 0:1], in1=lam4[:, 1:2], op=ALU.mult), r=['lam4'], w=['lamp'])
            em.op('dve', lambda e: e.tensor_tensor(out=lamp[:, 1:2], in0=lam4[:, 2:3], in1=lam4[:, 3:4], op=ALU.mult), r=['lam4', 'lamp'], w=['lamp'])
            pi = 0
            self.mm(pi, 64, self.ones_f[0:32, 0:64], lamp[:, 0:2], True, True, ['ones', 'lamp'], cols=(0, 2))
            em.op('act', lambda e: e.activation(out=lamc[:, 0:2], in_=self.ps[pi][0:64, 0:2], func=ACT.Exp), r=['ps%d' % pi], w=['lamc'])
            em.op('dve', lambda e: e.tensor_tensor(out=lamc[:, 2:3], in0=lamc[:, 1:2], in1=lamc[:, 0:1], op=ALU.subtract), r=['lamc'], w=['lamc'])
            em.op('dve', lambda e: e.tensor_scalar(out=lamc[:, 3:4], in0=lamc[:, 2:3], scalar1=-lam_init, scalar2=None, op0=ALU.add), r=['lamc'], w=['lamc'])
            em.op('dve', lambda e: e.tensor_scalar(out=sg[:], in0=sg[:], scalar1=1.0 - lam_init, scalar2=None, op0=ALU.mult), r=['sg'], w=['sg'])

            def mult_b(g, E, et):
                def f(kb, P, pt):
                    if kb >= 4 * g - 2:
                        em.op('dve', lambda e: e.tensor_tensor(out=P[:], in0=P[:], in1=E[:, kb - 4 * g + 2, :], op=ALU.mult), r=[pt, et], w=[pt])
                return f
            passes = []
            n = 0
            for h in range(8):
                b = h % 2

                def load(h=h, b=b):
                    em.dma('sp', eb[b][:], self.EB[h].rearrange("o p q -> p o q"), w=['eb%d' % b])
                    em.dma('sp', vt[b][:, :, 0:64], self.VD[:, h * 64:(h + 1) * 64].rearrange("(nb p) d -> p nb d", p=128), w=['vt%d' % b])
                    for m in range(2):
                        i = b * 2 + m
                        r0 = (h * 2 + m) * 32
                        em.dma('sp', qt[i][:], self.QD[r0:r0 + 32, :], w=['qt%d' % i])
                        em.dma('sp', kt[i][:], self.KD[r0:r0 + 32, :], w=['kt%d' % i])
                for g in range(self.NG):
                    ob = n % 2
                    for m in range(2):
                        i = b * 2 + m
                        pso = 4 + m + 2 * (n % 2)
                        tg = ['qt%d' % i, 'kt%d' % i, 'vt%d' % b]

                        def finish(h=h, g=g, m=m, pso=pso, ob=ob):
                            self.attn_norm(pso, osb[m], rc, 'osb%d' % m)
                            if m == 0:
                                return
                            em.op('dve', lambda e: e.scalar_tensor_tensor(out=od[:], in0=osb[1][0:64, :], scalar=lamc[:, 3:4], in1=osb[0][0:64, :], op0=ALU.mult, op1=ALU.add), r=['osb0', 'osb1', 'lamc'], w=['od'])
                            em.op('act', lambda e: e.activation(out=sq[:], in_=od[:], func=ACT.Square), r=['od'], w=['sqd'])
                            pi = self.nps()
                            while pi in self.resv:
                                pi = self.nps()
                            self.mm(pi, 64, self.ones_f[0:64, 0:64], sq[:], True, True, ['ones', 'sqd'])
                            em.op('act', lambda e: e.activation(out=sq[:], in_=self.ps[pi][0:64, :], func=ACT.Sqrt, bias=epsr[:, 0:1], scale=1.0 / 64), r=['ps%d' % pi, 'epsr2'], w=['sqd'])
                            em.op('dve', lambda e: e.reciprocal(out=sq[:], in_=sq[:]), r=['sqd'], w=['sqd'])
                            em.op('dve', lambda e: e.scalar_tensor_tensor(out=o16[ob][:], in0=od[:], scalar=sg[:, 0:1], in1=sq[:], op0=ALU.mult, op1=ALU.mult), r=['od', 'sqd', 'sg'], w=['o16%d' % ob])
                            em.dma('sp', self.OT[512 + h * 64:512 + (h + 1) * 64, g * 512:(g + 1) * 512], o16[ob][:], r=['o16%d' % ob], w=['OT'])
                        p = dict(qt=qt[i], kt=kt[i], vt=vt[b], rows=32, g=g, scale=B_SCALE, bias=self.cb[:, h:h + 1], mult=mult_b(g, eb[b], 'eb%d' % b), tags=tg, pso=pso, finish=finish)
                        if g == 0 and m == 0:
                            p['load'] = load
                        passes.append(p)
                    n += 1
            passes[0]['load']()
            self.attn_run(passes, ptb, ptn)
        self.resv = set()
        self.dbg_dump('OT', self.OT)

    def outproj_ln1(self, l):
        em = self.em
        j = l // 2
        WO = self.ev_w_o[j] if l % 2 == 0 else self.od_w_o[j]
        with self.phase("outproj_ln1"):
            self.norm_tmps()
            wo = self.sb("wo", [128, 8, D], BF16)
            self.wload(wo, WO, 8, 0, D, 'wo')
            g1 = self.sb("g1", [128, 8])
            b1 = self.sb("b1", [128, 8])
            em.dma('sp', g1[:], self.ln["ln1_g"][l], w=['gcols'])
            em.dma('sp', b1[:], self.ln["ln1_b"][l], w=['gcols'])
            hsrc = self.hsrc(l).rearrange("(k p) s -> p k s", p=128)
            hdst = self.hT.rearrange("(k p) s -> p k s", p=128)
            otv = self.OT.rearrange("(k p) s -> p k s", p=128)
            ot = [self.sb("ot%d" % i, [128, 8, 512], BF16) for i in range(2)]
            hr = [self.sb("hr%d" % i, [128, 8, 512]) for i in range(2)]
            y = self.sb("y", [128, 8, 512])
            ho = [self.sb("ho%d" % i, [128, 8, 512]) for i in range(2)]
            for g in range(self.NG):
                gs = slice(g * 512, (g + 1) * 512)
                b = g % 2
                em.dma('sp', ot[b][:], otv[:, :, gs], w=['ot%d' % b])
                em.dma('sp', hr[b][:], hsrc[:, :, gs], w=['hr%d' % b])
                for n in range(8):
                    pi = self.nps()
                    for k in range(8):
                        self.mm(pi, 128, wo[:, k, n * 128:(n + 1) * 128], ot[b][:, k, :], k == 0, k == 7, ['wo', 'ot%d' % b])
                    em.op('dve', lambda e, n=n, pi=pi: e.scalar_tensor_tensor(out=y[:, n, :], in0=hr[b][:, n, :], scalar=DN_ALPHA, in1=self.ps[pi][:], op0=ALU.mult, op1=ALU.add), r=['ps%d' % pi, 'hr%d' % b], w=['y'])
                self.ln_fm(y, g1, b1, ho[b], 'y', 'ho%d' % b)
                em.dma('sp', hdst[:, :, gs], ho[b][:], r=['ho%d' % b], w=['hT'])
        self.dbg_dump('h1', self.hT)

    def odd_proj(self, l):
        em, S = self.em, self.S
        j = l // 2
        with self.phase("odd_proj"):
            win = self.sb("win", [128, 8, 2640], BF16)
            wsw = self.sb("wsw", [128, 8, 1088], BF16)
            W = self.od_w_in[j]
            self.wload(win, W, 8, 0, 2640, 'win')
            self.wload(wsw, W, 8, 1536, 2624, 'wsw')
            for hh in range(17):
                c0 = 1536 + hh * 64
                self.wload(wsw, W, 8, c0 + 16, c0 + 32, 'wsw', hh * 64)
                self.wload(wsw, W, 8, c0, c0 + 16, 'wsw', hh * 64 + 16)
            hsrc = self.hsrc(l).rearrange("(k p) s -> p k s", p=128)
            hb = [self.sb("hb%d" % i, [128, 8, 512], BF16) for i in range(2)]
            cs = [self.sb("cs%d" % i, [128, 512]) for i in range(2)]
            sn = [self.sb("sn%d" % i, [128, 512]) for i in range(2)]
            t1 = self.sb("t1", [128, 512])
            t2 = self.sb("t2", [128, 512])
            ob = [self.sb("ob%d" % i, [128, 512], BF16) for i in range(4)]
            wo_ = [self.sb("wio%d" % i, [128, 16]) for i in range(2)]
            obn = [0]

            def evac_store(pi, rows, dst, ncol=512):
                i = obn[0] % 4
                obn[0] += 1
                if i % 2 == 0:
                    em.op('act', lambda e: e.copy(out=ob[i][0:rows, 0:ncol], in_=self.ps[pi][0:rows, 0:ncol]), r=['ps%d' % pi], w=['ob%d' % i])
                else:
                    em.op('dve', lambda e: e.tensor_copy(out=ob[i][0:rows, 0:ncol], in_=self.ps[pi][0:rows, 0:ncol]), r=['ps%d' % pi], w=['ob%d' % i])
                em.dma('sp', dst, ob[i][0:rows, 0:ncol], r=['ob%d' % i], w=['scr'])

            def rope_store(pa, pb, rows, g, dst):
                cg, sg_ = cs[g % 2], sn[g % 2]
                em.op('dve', lambda e: e.tensor_tensor(out=t1[0:rows, :], in0=self.ps[pa][0:rows, :], in1=cg[0:rows, :], op=ALU.mult), r=['ps%d' % pa, 'cs%d' % (g % 2)], w=['t1'])
                em.op('dve', lambda e: e.tensor_tensor(out=t2[0:rows, :], in0=self.ps[pb][0:rows, :], in1=sg_[0:rows, :], op=ALU.mult), r=['ps%d' % pb, 'sn%d' % (g % 2)], w=['t2'])
                i = obn[0] % 4
                obn[0] += 1
                em.op('pool', lambda e: e.tensor_tensor(out=ob[i][0:rows, :], in0=t1[0:rows, :], in1=t2[0:rows, :], op=ALU.add), r=['t1', 't2'], w=['ob%d' % i])
                em.dma('sp', dst, ob[i][0:rows, :], r=['ob%d' % i], w=['scr'])

            nw = 0
            for g in range(self.NG):
                gs = slice(g * 512, (g + 1) * 512)
                H = hb[g % 2]
                ht = 'hb%d' % (g % 2)
                em.dma('pool', H[:], hsrc[:, :, gs], w=[ht])
                em.dma('sp', cs[g % 2][:], self.COSI[:, gs], w=['cs%d' % (g % 2)])
                em.dma('sp', sn[g % 2][:], self.SINI[:, gs], w=['sn%d' % (g % 2)])
                for c in range(8):
                    pi = self.nps()
                    for k in range(8):
                        self.mm(pi, 128, win[:, k, c * 128:(c + 1) * 128], H[:, k, :], k == 0, k == 7, ['win', ht])
                    evac_store(pi, 128, self.QC[c * 128:(c + 1) * 128, gs])
                for c in range(2):
                    pi = self.nps()
                    for k in range(8):
                        self.mm(pi, 128, win[:, k, 1024 + c * 128:1024 + (c + 1) * 128], H[:, k, :], k == 0, k == 7, ['win', ht])
                    evac_store(pi, 128, self.KC[c * 128:(c + 1) * 128, gs])
                for tb in range(4):
                    pi = self.nps()
                    for k in range(8):
                        self.mm(pi, 128, H[:, k, tb * 128:(tb + 1) * 128], win[:, k, 1280:1536], k == 0, k == 7, ['win', ht], cols=(0, 256))
                    r0 = g * 512 + tb * 128
                    evac_store(pi, 128, self.VC[r0:r0 + 128, :], ncol=256)
                    pi = self.nps()
                    for k in range(8):
                        self.mm(pi, 128, H[:, k, tb * 128:(tb + 1) * 128], win[:, k, 2624:2640], k == 0, k == 7, ['win', ht], cols=(0, 16))
                    wb = nw % 2
                    nw += 1
                    em.op('act', lambda e, pi=pi, wb=wb: e.activation(out=wo_[wb][:], in_=self.ps[pi][:, 0:16], func=ACT.Copy, scale=0.25 * IDX_SCALE), r=['ps%d' % pi], w=['wio%d' % wb])
                    em.dma('sp', self.WI[r0:r0 + 128, :], wo_[wb][:], r=['wio%d' % wb], w=['scr'])
                for c in range(8):
                    pa, pb = self.nps(), self.nps()
                    for k in range(8):
                        self.mm(pa, 128, win[:, k, 1536 + c * 128:1536 + (c + 1) * 128], H[:, k, :], k == 0, k == 7, ['win', ht])
                    for k in range(8):
                        self.mm(pb, 128, wsw[:, k, c * 128:(c + 1) * 128], H[:, k, :], k == 0, k == 7, ['wsw', ht])
                    rope_store(pa, pb, 128, g, self.QI[c * 128:(c + 1) * 128, gs])
                pa, pb = self.nps(), self.nps()
                for k in range(8):
                    self.mm(pa, 64, win[:, k, 2560:2624], H[:, k, :], k == 0, k == 7, ['win', ht])
                for k in range(8):
                    self.mm(pb, 64, wsw[:, k, 1024:1088], H[:, k, :], k == 0, k == 7, ['wsw', ht])
                rope_store(pa, pb, 64, g, self.KI[:, gs])
        self.dbg_dump('QC', self.QC)
        self.dbg_dump('QI', self.QI)
        self.dbg_dump('WI', self.WI)

    def dsa(self, l):
        em, S, NB = self.em, self.S, self.NB
        NIT = 30
        NSEL = float(min(256, S // 4))
        with self.phase("dsa"):
            ki = self.sb("ki", [64, S], BF16)
            kc = [self.sb("kc%d" % i, [64, S], BF16) for i in range(4)]
            vc = [self.sb("vc%d" % i, [128, NB, 65], BF16) for i in range(4)]
            em.dma('sp', ki[:], self.KI, w=['ki'])
            for i in range(4):
                em.dma('sp', kc[i][:], self.KC[i * 64:(i + 1) * 64, :], w=['kc%d' % i])
                em.op('pool', lambda e, i=i: e.memset(vc[i][:, :, 64:65], 1.0), w=['vc%d' % i])
                em.dma('sp', vc[i][:, :, 0:64], self.VC[:, i * 64:(i + 1) * 64].rearrange("(nb p) d -> p nb d", p=128), w=['vc%d' % i])
            qg = self.sb("qg", [64, 16, 512], BF16)
            wig = self.sb("wig", [128, 4, 16])
            acc = self.sb("acc", [128, S])
            mq = self.sb("mq", [128, S], BF16)
            maskT = self.sb("maskT", [128, NB, 512], BF16)
            rl = [self.sb("rl%d" % i, [128, 512]) for i in range(2)]
            ec = [self.sb("ec%d" % i, [128, 6, 512], BF16) for i in range(2)]
            ptb = [self.sb("pt%d" % i, [128, 512], BF16) for i in range(4)]
            ptn = [0]
            osb = [self.sb("osb%d" % i, [128, 512]) for i in range(2)]
            rc = self.sb("rc", [128, 512])
            o16 = [self.sb("o16%d" % i, [64, 512], BF16) for i in range(2)]
            bs = self.sb("bs", [128, 8])
            wt = self.sb("wt", [128, NIT])
            pw2 = self.sb("pw2", [128, NIT])
            cntT = self.sb("cntT", [128, NIT])
            for i in range(NIT):
                em.op('pool', lambda e, i=i: e.memset(pw2[:, i:i + 1], 2.0 ** -(i + 1)), w=['pw2'])
            qiv = self.QI.rearrange("(h d) s -> d h s", d=64)
            qcv = self.QC.rearrange("(h d) s -> d h s", d=64)
            nrl = 0
            nec = 0
            no = 0
            for g in range(self.NG):
                gs = slice(g * 512, (g + 1) * 512)
                em.dma('sp', qg[:], qiv[:, :, gs], w=['qg'])
                em.dma('sp', wig[:], self.WI[g * 512:(g + 1) * 512, :].rearrange("(t p) h -> p t h", p=128), w=['wig'])
                em.op('pool', lambda e: e.memset(maskT[:], 0.0), w=['maskT'])
                for qt in range(4):
                    T = 4 * g + qt
                    Lk = 128 * (T + 1)
                    nck = (Lk + 511) // 512
                    for h in range(16):
                        for kcn in range(nck):
                            n = min(512, Lk - kcn * 512)
                            pi = self.nps()
                            self.mm(pi, 128, qg[:, h, qt * 128:(qt + 1) * 128], ki[:, kcn * 512:kcn * 512 + n], True, True, ['qg', 'ki'], cols=(0, n))
                            ri = nrl % 2
                            nrl += 1
                            em.op('act', lambda e, pi=pi, ri=ri, n=n: e.activation(out=rl[ri][:, 0:n], in_=self.ps[pi][:, 0:n], func=ACT.Relu), r=['ps%d' % pi], w=['rl%d' % ri])
                            ksl = slice(kcn * 512, kcn * 512 + n)
                            if h == 0:
                                em.op('dve', lambda e, ri=ri, n=n, ksl=ksl, qt=qt: e.tensor_scalar(out=acc[:, ksl], in0=rl[ri][:, 0:n], scalar1=wig[:, qt, 0:1], scalar2=None, op0=ALU.mult), r=['rl%d' % ri, 'wig'], w=['acc'])
                            else:
                                em.op('dve', lambda e, ri=ri, n=n, ksl=ksl, qt=qt, h=h: e.scalar_tensor_tensor(out=acc[:, ksl], in0=rl[ri][:, 0:n], scalar=wig[:, qt, h:h + 1], in1=acc[:, ksl], op0=ALU.mult, op1=ALU.add), r=['rl%d' % ri, 'wig', 'acc'], w=['acc'])
                    if T >= 2:
                        em.op('dve', lambda e, Lk=Lk: e.tensor_reduce(out=bs[:, 0:1], in_=acc[:, 0:Lk], axis=AX.X, op=ALU.max), r=['acc'], w=['bs'])
                        em.op('dve', lambda e, Lk=Lk: e.tensor_reduce(out=bs[:, 1:2], in_=acc[:, 0:Lk], axis=AX.X, op=ALU.min), r=['acc', 'bs'], w=['bs'])
                    em.op('dve', lambda e, Lk=Lk: e.memset(acc[0:64, Lk - 64:Lk], NEG), r=['acc'], w=['acc'])
                    if T >= 2:
                        em.op('dve', lambda e: e.tensor_tensor(out=bs[:, 2:3], in0=bs[:, 0:1], in1=bs[:, 1:2], op=ALU.subtract), r=['bs'], w=['bs'])
                        em.op('dve', lambda e: e.tensor_scalar(out=wt[:], in0=pw2[:], scalar1=bs[:, 2:3], scalar2=None, op0=ALU.mult), r=['bs', 'pw2'], w=['wt'])
                        em.op('dve', lambda e: e.tensor_copy(out=bs[:, 3:4], in_=bs[:, 1:2]), r=['bs'], w=['bs'])
                        em.op('dve', lambda e: e.memset(cntT[:], 0.0), w=['cntT'])
                        for i in range(NIT):
                            em.op('dve', lambda e, i=i: e.tensor_tensor(out=bs[:, 4:5], in0=bs[:, 3:4], in1=wt[:, i:i + 1], op=ALU.add), r=['bs', 'wt'], w=['bs'])
                            em.op('dve', lambda e, i=i, Lk=Lk: e.tensor_scalar(out=mq[:, 0:Lk], in0=acc[:, 0:Lk], scalar1=bs[:, 4:5], scalar2=0.0, op0=ALU.is_ge, op1=ALU.add, accum_out=cntT[:, i:i + 1]), r=['acc', 'bs', 'cntT'], w=['mq', 'cntT'])
                            em.op('dve', lambda e, i=i: e.scalar_tensor_tensor(out=bs[:, 5:6], in0=cntT[:, i:i + 1], scalar=NSEL, in1=wt[:, i:i + 1], op0=ALU.is_ge, op1=ALU.mult), r=['cntT', 'wt', 'bs'], w=['bs'])
                            em.op('dve', lambda e: e.tensor_tensor(out=bs[:, 3:4], in0=bs[:, 3:4], in1=bs[:, 5:6], op=ALU.add), r=['bs'], w=['bs'])
                        em.op('dve', lambda e, Lk=Lk: e.tensor_scalar(out=mq[:, 0:Lk], in0=acc[:, 0:Lk], scalar1=bs[:, 3:4], scalar2=None, op0=ALU.is_ge), r=['acc', 'bs'], w=['mq'])
                    else:
                        em.op('dve', lambda e, Lk=Lk: e.tensor_scalar(out=mq[:, 0:Lk], in0=acc[:, 0:Lk], scalar1=-1.0e30, scalar2=None, op0=ALU.is_ge), r=['acc'], w=['mq'])
                    for kb0 in range(0, T + 1, 4):
                        nk = min(4, T + 1 - kb0)
                        pi = self.nps()
                        for q_ in range(nk):
                            kb = kb0 + q_
                            self.mm(pi, 128, mq[:, kb * 128:(kb + 1) * 128], self.ident_bf[:], True, True, ['mq'], cols=(q_ * 128, (q_ + 1) * 128), inc=(q_ == nk - 1))
                        src = fap(self.ps[pi][:, 0:nk * 128], [[128, nk], [1, 128]])
                        em.op('act', lambda e, src=src, kb0=kb0, nk=nk, qt=qt: e.copy(out=maskT[:, kb0:kb0 + nk, qt * 128:(qt + 1) * 128], in_=src), r=['ps%d' % pi], w=['maskT'])
                em.dma('sp', qg[:], qcv[:, :, gs], w=['qg'])
                self.resv = {6, 7}
                passes = []
                for h in range(16):
                    kv = h // 4
                    eb_ = nec % 2
                    nec += 1

                    def load(h=h, eb_=eb_):
                        em.dma('sp', ec[eb_][:], self.EB[8 + h].rearrange("o p q -> p o q"), w=['ec%d' % eb_])

                    def mult_c(kb, P, pt, g=g, eb_=eb_):
                        em.op('dve', lambda e: e.tensor_tensor(out=P[:], in0=P[:], in1=maskT[:, kb, :], op=ALU.mult), r=[pt, 'maskT'], w=[pt])
                        if kb >= 4 * g - 2:
                            em.op('dve', lambda e: e.tensor_tensor(out=P[:], in0=P[:], in1=ec[eb_][:, kb - 4 * g + 2, :], op=ALU.mult), r=[pt, 'ec%d' % eb_], w=[pt])
                    pso = 6 + no % 2
                    ob = no % 2
                    no += 1

                    def finish(h=h, pso=pso, ob=ob, gs=gs):
                        self.attn_norm(pso, osb[ob], rc, 'osb%d' % ob)
                        em.op('pool', lambda e: e.tensor_copy(out=o16[ob][:], in_=osb[ob][0:64, :]), r=['osb%d' % ob], w=['o16%d' % ob])
                        em.dma('sp', self.OT[h * 64:(h + 1) * 64, gs], o16[ob][:], r=['o16%d' % ob], w=['OT'])
                    passes.append(dict(qt=qg[:, h, :], kt=kc[kv], vt=vc[kv], rows=64, g=g, scale=C_SCALE, bias=self.cb[:, 8 + h:9 + h], mult=mult_c, tags=['qg', 'kc%d' % kv, 'vc%d' % kv], pso=pso, finish=finish, qoff=0, load=load))
                passes[0]['load']()
                self.attn_run(passes, ptb, ptn)
                self.resv = set()
        self.dbg_dump('OT', self.OT)

    def peer_scores(self, l):
        em, S = self.em, self.S
        with self.phase("peer_scores"):
            k1 = self.sb("k1", [128, 128])
            k2 = self.sb("k2", [128, 128])
            em.dma('sp', k1[:], self.peer_k1T[l], w=['k1'])
            em.dma('sp', k2[:], self.peer_k2T[l], w=['k2'])
            hsrc = self.hT.rearrange("(k p) s -> p k s", p=128)
            wqv = self.peer_w_q[l].rearrange("(k p) f -> p k f", p=128)
            xf = self.sb("xf", [128, 8, 512])
            wqc = [self.sb("wqc%d" % i, [128, 8, 128]) for i in range(2)]
            qT = [self.sb("qT%d" % i, [128, 512]) for i in range(2)]
            sall = [self.sb("sall%d" % i, [128, 16, 128]) for i in range(4)]
            m16 = self.sb("m16", [128, 16, 24])
            wk = self.sb("wk", [128, 576])
            wk2 = self.sb("wk2", [128, 576])
            cand = self.sb("cand", [128, 576])
            c24 = self.sb("c24", [128, 8, 24])
            ez = self.sb("ez", [128, 8, 16])
            st = self.sb("st", [128, 8, 6])
            eall = self.sb("eall", [128, 16, 128])
            e1o = [self.sb("e1o%d" % i, [128, 8, 128]) for i in range(2)]
            e2o = [self.sb("e2o%d" % i, [128, 8, 128]) for i in range(2)]
            tho = [self.sb("tho%d" % i, [128, 8]) for i in range(2)]
            nt = 0
            for g in range(self.NG):
                gs = slice(g * 512, (g + 1) * 512)
                em.dma('sp', xf[:], hsrc[:, :, gs], w=['xf'])
                for c in range(16):
                    b = c % 2
                    em.dma('sp', wqc[b][:], wqv[:, :, c * 128:(c + 1) * 128], w=['wqc%d' % b])
                    pi = self.nps()
                    for k in range(8):
                        self.mm(pi, 128, wqc[b][:, k, :], xf[:, k, :], k == 0, k == 7, ['wqc%d' % b, 'xf'])
                    em.op('act', lambda e, pi=pi, b=b: e.copy(out=qT[b][:], in_=self.ps[pi][:]), r=['ps%d' % pi], w=['qT%d' % b])
                    kk, kt = (k1, 'k1') if c % 2 == 0 else (k2, 'k2')
                    for tb in range(4):
                        pj = self.nps()
                        self.mm(pj, 128, qT[b][:, tb * 128:(tb + 1) * 128], kk[:], True, True, ['qT%d' % b, kt], cols=(0, 128))
                        em.op('dve' if tb % 2 else 'act', lambda e, pj=pj, tb=tb, c=c: (e.tensor_copy if tb % 2 else e.copy)(out=sall[tb][:, c, :], in_=self.ps[pj][:, 0:128]), r=['ps%d' % pj], w=['sall%d' % tb])
                for tb in range(4):
                    sa = sall[tb]
                    sat = 'sall%d' % tb
                    ob = nt % 2
                    nt += 1
                    for c in range(16):
                        em.op('dve', lambda e, c=c: e.max(out=m16[:, c, 0:8], in_=sa[:, c, :]), r=[sat], w=['m16'])
                        em.op('dve', lambda e, c=c: e.match_replace(out=wk[:, 0:128], in_to_replace=m16[:, c, 0:8], in_values=sa[:, c, :], imm_value=NEG), r=[sat, 'm16'], w=['wk'])
                        em.op('dve', lambda e, c=c: e.max(out=m16[:, c, 8:16], in_=wk[:, 0:128]), r=['wk'], w=['m16'])
                        em.op('dve', lambda e, c=c: e.match_replace(out=wk2[:, 0:128], in_to_replace=m16[:, c, 8:16], in_values=wk[:, 0:128], imm_value=NEG), r=['wk', 'm16'], w=['wk2'])
                        em.op('dve', lambda e, c=c: e.max(out=m16[:, c, 16:24], in_=wk2[:, 0:128]), r=['wk2'], w=['m16'])
                    for h in range(8):
                        a_ap = fap(m16[:, 2 * h, :], [[1, 24], [0, 24]])
                        b_ap = fap(m16[:, 2 * h + 1, :], [[0, 24], [1, 24]])
                        o_ap = fap(cand[:, 0:576], [[24, 24], [1, 24]])
                        em.op('dve', lambda e, a_ap=a_ap, b_ap=b_ap, o_ap=o_ap: e.tensor_tensor(out=o_ap, in0=a_ap, in1=b_ap, op=ALU.add), r=['m16'], w=['cand'])
                        em.op('dve', lambda e, h=h: e.max(out=c24[:, h, 0:8], in_=cand[:]), r=['cand'], w=['c24'])
                        em.op('dve', lambda e, h=h: e.match_replace(out=wk[:], in_to_replace=c24[:, h, 0:8], in_values=cand[:], imm_value=NEG), r=['cand', 'c24'], w=['wk'])
                        em.op('dve', lambda e, h=h: e.max(out=c24[:, h, 8:16], in_=wk[:]), r=['wk'], w=['c24'])
                        em.op('dve', lambda e, h=h: e.match_replace(out=wk2[:], in_to_replace=c24[:, h, 8:16], in_values=wk[:], imm_value=NEG), r=['wk', 'c24'], w=['wk2'])
                        em.op('dve', lambda e, h=h: e.max(out=c24[:, h, 16:24], in_=wk2[:]), r=['wk2'], w=['c24'])
                    mx_ap = fap(m16[:, 0, 0:1], [[24, 16], [0, 128]])
                    em.op('dve', lambda e, mx_ap=mx_ap: e.tensor_tensor(out=eall[:], in0=sa[:], in1=mx_ap, op=ALU.subtract), r=[sat, 'm16'], w=['eall'])
                    em.op('act', lambda e: e.activation(out=eall[:], in_=eall[:], func=ACT.Exp), r=['eall'], w=['eall'])
                    cm_ap = fap(c24[:, 0, 0:1], [[24, 8], [0, 16]])
                    em.op('dve', lambda e, cm_ap=cm_ap: e.tensor_tensor(out=ez[:], in0=c24[:, :, 0:16], in1=cm_ap, op=ALU.subtract), r=['c24'], w=['ez'])
                    em.op('act', lambda e: e.activation(out=ez[:], in_=ez[:], func=ACT.Exp), r=['ez'], w=['ez'])
                    em.op('dve', lambda e: e.tensor_reduce(out=st[:, :, 0], in_=ez[:], axis=AX.X, op=ALU.add), r=['ez'], w=['st'])
                    em.op('dve', lambda e: e.reciprocal(out=st[:, :, 1], in_=st[:, :, 0]), r=['st'], w=['st'])
                    em.op('dve', lambda e: e.tensor_tensor(out=st[:, :, 2], in0=c24[:, :, 15], in1=c24[:, :, 16], op=ALU.add), r=['c24', 'st'], w=['st'])
                    em.op('dve', lambda e: e.scalar_tensor_tensor(out=st[:, :, 3], in0=st[:, :, 2], scalar=0.5, in1=c24[:, :, 0], op0=ALU.mult, op1=ALU.subtract), r=['c24', 'st'], w=['st'])
                    em.op('act', lambda e: e.activation(out=st[:, :, 4], in_=st[:, :, 3], func=ACT.Exp), r=['st'], w=['st'])
                    em.op('dve', lambda e, ob=ob: e.tensor_tensor(out=tho[ob][:], in0=st[:, :, 4], in1=st[:, :, 1], op=ALU.mult), r=['st'], w=['tho%d' % ob])
                    rz_ap = fap(st[:, 0, 1:2], [[6, 8], [0, 128]])
                    e1_ap = fap(eall[:, 0, :], [[256, 8], [1, 128]])
                    e2_ap = fap(eall[:, 1, :], [[256, 8], [1, 128]])
                    em.op('dve', lambda e, ob=ob, rz_ap=rz_ap, e1_ap=e1_ap: e.tensor_tensor(out=e1o[ob][:], in0=e1_ap, in1=rz_ap, op=ALU.mult), r=['eall', 'st'], w=['e1o%d' % ob])
                    em.op('pool', lambda e, ob=ob, e2_ap=e2_ap: e.tensor_copy(out=e2o[ob][:], in_=e2_ap), r=['eall'], w=['e2o%d' % ob])
                    r0 = g * 512 + tb * 128
                    em.dma('sp', self.E1[r0:r0 + 128, :], e1o[ob][:].rearrange("p h n -> p (h n)"), r=['e1o%d' % ob], w=['E1'])
                    em.dma('sp', self.E2[r0:r0 + 128, :], e2o[ob][:].rearrange("p h n -> p (h n)"), r=['e2o%d' % ob], w=['E2'])
                    em.dma('sp', self.TH[r0:r0 + 128, :], tho[ob][:], r=['tho%d' % ob], w=['TH'])
        self.dbg_dump('E1', self.E1)
        self.dbg_dump('TH', self.TH)

    def peer_main(self, l):
        em, S = self.em, self.S
        NEG_ = 32
        with self.phase("peer_main"):
            hsrc = self.hT.rearrange("(k p) s -> p k s", p=128)
            uTv = self.peer_uT[l].rearrange("(k p) e -> p k e", p=128)
            vv = self.peer_v[l].rearrange("(g c p) d -> g p c d", p=128, c=4)
            ftv = self.FT.rearrange("(k p) s -> p k s", p=128)
            xb = self.sb("xb", [128, 8, 512], BF16)
            acc = self.sb("acc", [128, 8, 512])
            u16 = [self.sb("u16%d" % i, [128, 8, 512], BF16) for i in range(2)]
            v16 = [self.sb("v16%d" % i, [128, 4, 1024], BF16) for i in range(2)]
            e1t = [self.sb("e1t%d" % i, [128, 8, 128]) for i in range(4)]
            e2t = [self.sb("e2t%d" % i, [128, 8, 128]) for i in range(4)]
            tht = [self.sb("tht%d" % i, [128, 8]) for i in range(4)]
            glT = self.sb("glT", [128, 4, 512])
            AT = self.sb("AT", [128, 4, 512], BF16)
            yb = [self.sb("yb%d" % i, [128, 512]) for i in range(6)]
            gb = [self.sb("gb%d" % i, [128, 512], BF16) for i in range(6)]
            ny = ng = 0
            for g in range(self.NG):
                gs = slice(g * 512, (g + 1) * 512)
                em.dma('pool', xb[:], hsrc[:, :, gs], w=['xb'])
                for tt in range(4):
                    r0 = g * 512 + tt * 128
                    em.dma('sp', e1t[tt][:].rearrange("p h n -> p (h n)"), self.E1[r0:r0 + 128, :], w=['e1t%d' % tt])
                    em.dma('sp', e2t[tt][:].rearrange("p h n -> p (h n)"), self.E2[r0:r0 + 128, :], w=['e2t%d' % tt])
                    em.dma('sp', tht[tt][:], self.TH[r0:r0 + 128, :], w=['tht%d' % tt])
                for eg in range(NEG_):
                    wb = eg % 2
                    ut, vt_ = 'u16%d' % wb, 'v16%d' % wb
                    em.dma('pool', u16[wb][:], uTv[:, :, eg * 512:(eg + 1) * 512], w=[ut])
                    em.dma('pool', v16[wb][:], vv[eg], w=[vt_])
                    for c in range(4):
                        for k in range(8):
                            self.mm(c, 128, u16[wb][:, k, c * 128:(c + 1) * 128], xb[:, k, :], k == 0, k == 7, [ut, 'xb'])
                        em.op('act', lambda e, c=c: e.activation(out=glT[:, c, :], in_=self.ps[c][:], func=ACT.Gelu_apprx_tanh), r=['ps%d' % c], w=['glT'])
                    for tt in range(4):
                        for h in range(8):
                            yi = ny % 6
                            ny += 1
                            gi = ng % 6
                            ng += 1
                            for i1 in range(2):
                                em.op('act', lambda e, i1=i1, yi=yi, tt=tt, h=h: e.activation(out=yb[yi][:, i1 * 128:(i1 + 1) * 128], in_=e2t[tt][:, h, :], func=ACT.Copy, scale=e1t[tt][:, h, eg * 4 + i1:eg * 4 + i1 + 1]), r=['e1t%d' % tt, 'e2t%d' % tt], w=['yb%d' % yi])
                            a_ap = fap(e1t[tt][:, h, eg * 4 + 2:eg * 4 + 4], [[1, 2], [0, 128]])
                            b_ap = fap(e2t[tt][:, h, :], [[0, 2], [1, 128]])
                            o_ap = fap(yb[yi][:, 256:512], [[128, 2], [1, 128]])
                            em.op('pool', lambda e, a_ap=a_ap, b_ap=b_ap, o_ap=o_ap: e.tensor_tensor(out=o_ap, in0=a_ap, in1=b_ap, op=ALU.mult), r=['e1t%d' % tt, 'e2t%d' % tt], w=['yb%d' % yi])
                            em.op('dve', lambda e, yi=yi, gi=gi, tt=tt, h=h: e.scalar_tensor_tensor(out=gb[gi][:], in0=yb[yi][:], scalar=tht[tt][:, h:h + 1], in1=yb[yi][:], op0=ALU.is_ge, op1=ALU.mult), r=['yb%d' % yi, 'tht%d' % tt], w=['gb%d' % gi])
                            for c in range(4):
                                self.mm(4 + c, 128, gb[gi][:, c * 128:(c + 1) * 128], self.ident_bf[:], h == 0, h == 7, ['gb%d' % gi], cols=(tt * 128, (tt + 1) * 128), inc=(c == 3))
                    for c in range(4):
                        em.op('dve', lambda e, c=c: e.tensor_tensor(out=AT[:, c, :], in0=glT[:, c, :], in1=self.ps[4 + c][:], op=ALU.mult), r=['glT', 'ps%d' % (4 + c)], w=['AT'])
                    for dch in range(8):
                        pi = dch % 4
                        for c in range(4):
                            self.mm(pi, 128, v16[wb][:, c, dch * 128:(dch + 1) * 128], AT[:, c, :], c == 0, c == 3, [vt_, 'AT'])
                        if eg == 0:
                            em.op('dve', lambda e, dch=dch, pi=pi: e.tensor_copy(out=acc[:, dch, :], in_=self.ps[pi][:]), r=['ps%d' % pi], w=['acc'])
                        else:
                            em.op('dve', lambda e, dch=dch, pi=pi: e.tensor_tensor(out=acc[:, dch, :], in0=acc[:, dch, :], in1=self.ps[pi][:], op=ALU.add), r=['ps%d' % pi, 'acc'], w=['acc'])
                em.dma('sp', ftv[:, :, gs], acc[:], r=['acc'], w=['FT'])
        self.dbg_dump('FT', self.FT)
        self.peer_post(l)

    def peer_post(self, l):
        em = self.em
        with self.phase("peer_post"):
            self.norm_tmps()
            wg = self.sb("wg", [128, 8, D], BF16)
            pw = self.sb("pw", [128, 2, D], BF16)
            self.wload(wg, self.ple_gate_w[l], 8, 0, D, 'wg')
            self.wload(pw, self.ple_w[l], 2, 0, D, 'pw')
            g2 = self.sb("g2", [128, 8])
            b2 = self.sb("b2", [128, 8])
            bg = self.sb("bg", [128, 8])
            em.dma('sp', g2[:], self.ln["ln2_g"][l], w=['gcols'])
            em.dma('sp', b2[:], self.ln["ln2_b"][l], w=['gcols'])
            em.dma('sp', bg[:], self.ln["ple_gate_b"][l], w=['gcols'])
            hv = self.hT.rearrange("(k p) s -> p k s", p=128)
            ftv = self.FT.rearrange("(k p) s -> p k s", p=128)
            pv = self.pT[l].rearrange("(k p) s -> p k s", p=128)
            hr = [self.sb("hr%d" % i, [128, 8, 512]) for i in range(1)]
            ft = [self.sb("ft%d" % i, [128, 8, 512]) for i in range(1)]
            pb = [self.sb("pb%d" % i, [128, 2, 512], BF16) for i in range(1)]
            y = self.sb("y", [128, 8, 512])
            h2 = self.sb("h2", [128, 8, 512])
            h2b = self.sb("h2b", [128, 8, 512], BF16)
            gt = self.sb("gt", [128, 512])
            ho = [self.sb("ho%d" % i, [128, 8, 512]) for i in range(1)]
            for g in range(self.NG):
                gs = slice(g * 512, (g + 1) * 512)
                b = 0
                em.dma('sp', hr[b][:], hv[:, :, gs], w=['hr%d' % b])
                em.dma('sp', ft[b][:], ftv[:, :, gs], w=['ft%d' % b])
                em.dma('pool', pb[b][:], pv[:, :, gs], w=['pb%d' % b])
                for n in range(8):
                    em.op('dve', lambda e, n=n: e.scalar_tensor_tensor(out=y[:, n, :], in0=hr[b][:, n, :], scalar=DN_ALPHA, in1=ft[b][:, n, :], op0=ALU.mult, op1=ALU.add), r=['hr%d' % b, 'ft%d' % b], w=['y'])
                self.ln_fm(y, g2, b2, h2, 'y', 'h2', outb=h2b)
                for n in range(8):
                    pi, pj = self.nps(), self.nps()
                    for k in range(8):
                        self.mm(pi, 128, wg[:, k, n * 128:(n + 1) * 128], h2b[:, k, :], k == 0, k == 7, ['wg', 'h2b'])
                    for k in range(2):
                        self.mm(pj, 128, pw[:, k, n * 128:(n + 1) * 128], pb[b][:, k, :], k == 0, k == 1, ['pw', 'pb%d' % b])
                    em.op('act', lambda e, n=n, pi=pi: e.activation(out=gt[:], in_=self.ps[pi][:], func=ACT.Sigmoid, bias=bg[:, n:n + 1], scale=1.0), r=['ps%d' % pi, 'gcols'], w=['gt'])
                    em.op('dve', lambda e, pj=pj: e.tensor_tensor(out=gt[:], in0=gt[:], in1=self.ps[pj][:], op=ALU.mult), r=['gt', 'ps%d' % pj], w=['gt'])
                    em.op('dve', lambda e, n=n: e.tensor_tensor(out=ho[b][:, n, :], in0=gt[:], in1=h2[:, n, :], op=ALU.add), r=['gt', 'h2'], w=['ho%d' % b])
                em.dma('sp', hv[:, :, gs], ho[b][:], r=['ho%d' % b], w=['hT%d' % g])
        self.dbg_dump('h2', self.hT)

    def finalize(self):
        em = self.em
        em.barrier()
        if not self.dbg:
            em.dma('sp', self.outT, self.hT)
        em.barrier()


def _cols(v, C):
    return np.ascontiguousarray(np.asarray(v, np.float32).reshape(C, 128).T)


def prep_inputs(inp, b, S, L):
    NE, NO = (L + 1) // 2, max(L // 2, 1)
    f = lambda a: np.ascontiguousarray(np.asarray(a, np.float32))
    m = {}
    m["xT"] = f(np.asarray(inp["x"])[b, :S].T)
    m["pT"] = f(np.transpose(np.asarray(inp["p"])[:L, b, :S], (0, 2, 1)))
    m["pos"] = np.ascontiguousarray(np.asarray(inp["positions"])[b, :S].reshape(1, S).astype(np.int32))
    m["rel_bias"] = f(inp["rel_bias"])
    m["ev_w_in"] = f(np.asarray(inp["ev_w_in"])[:NE])
    m["ev_w_uq"] = f(np.asarray(inp["ev_w_uq"])[:NE])
    m["ev_w_ukv"] = f(np.asarray(inp["ev_w_ukv"])[:NE])
    m["ev_q_norm"] = np.stack([_cols(np.asarray(inp["ev_q_norm"])[i], 4) for i in range(NE)])
    m["ev_kv_norm"] = np.stack([_cols(np.asarray(inp["ev_kv_norm"])[i], 2) for i in range(NE)])
    m["ev_lam"] = f(np.stack([np.asarray(inp[k])[:NE] for k in ("ev_lam_q1", "ev_lam_k1", "ev_lam_q2", "ev_lam_k2")], axis=-1))
    m["ev_subln"] = f(np.asarray(inp["ev_subln"])[:NE].reshape(NE, 64, 1))
    m["ev_w_o"] = f(np.asarray(inp["ev_w_o"])[:NE])
    m["od_w_in"] = f(np.asarray(inp["od_w_in"])[:NO])
    m["od_w_o"] = f(np.asarray(inp["od_w_o"])[:NO])
    for k in ("ln1_g", "ln1_b", "ln2_g", "ln2_b", "ple_gate_b"):
        m[k] = np.stack([_cols(np.asarray(inp[k])[i], 8) for i in range(L)])
    m["peer_w_q"] = f(np.asarray(inp["peer_w_q"])[:L])
    m["peer_k1T"] = f(np.transpose(np.asarray(inp["peer_k1"])[:L], (0, 2, 1)))
    m["peer_k2T"] = f(np.transpose(np.asarray(inp["peer_k2"])[:L], (0, 2, 1)))
    m["peer_uT"] = f(np.transpose(np.asarray(inp["peer_u"])[:L], (0, 2, 1)))
    m["peer_v"] = f(np.asarray(inp["peer_v"])[:L])
    m["ple_w"] = f(np.asarray(inp["ple_w"])[:L])
    m["ple_gate_w"] = f(np.asarray(inp["ple_gate_w"])[:L])
    return m


def kernel(**inputs):
    B, S, L = 8, 4096, 4
    prog = Prog(S, L)
    shared = None
    in_maps = []
    for b in range(B):
        m = prep_inputs(inputs, b, S, L)
        if shared is None:
            shared = m
        else:
            for k in m:
                if k not in ("xT", "pT", "pos"):
                    m[k] = shared[k]
        in_maps.append(m)
    res = run_bass_kernel_spmd(prog.nc, in_maps, core_ids=list(range(B)))
    out = np.stack([np.ascontiguousarray(res.results[b]["outT"].T) for b in range(B)])
    return out.astype(np.float32)
```

```python
import math
import numpy as np
import concourse.bass as bass
import concourse.mybir as mybir
from concourse.bass_utils import run_bass_kernel_spmd

F32 = mybir.dt.float32
BF16 = mybir.dt.bfloat16
I32 = mybir.dt.int32
ALU = mybir.AluOpType
ACT = mybir.ActivationFunctionType
AX = mybir.AxisListType

D = 1024
PLE = 256
LN_EPS = 1e-5
RMS_EPS = 1e-6
A_SCALE = 96 ** -0.5
B_SCALE = 32 ** -0.5
C_SCALE = 64 ** -0.5
IDX_SCALE = 64 ** -0.5
DEPTH_FULL = 4
DN_ALPHA = (2 * DEPTH_FULL) ** 0.25
NEG = -3.0e38
RREL = 1279
REL0 = 767


class Em:
    NDMA = 24

    def __init__(self, nc):
        self.nc = nc
        self.eng = {'pe': nc.tensor, 'act': nc.scalar, 'dve': nc.vector,
                    'pool': nc.gpsimd, 'sp': nc.sync}
        self.sems = {}
        self.cnt = {}
        self.seen = {e: {} for e in self.eng}
        self.lastw = {}
        self.readers = {}
        for e in ('pe', 'act', 'dve', 'pool'):
            self.sems[e] = nc.alloc_semaphore("s_" + e)
            self.cnt[e] = 0
        for i in range(self.NDMA):
            s = 'd%d' % i
            self.sems[s] = nc.alloc_semaphore("s_" + s)
            self.cnt[s] = 0
        self.dma_rr = 0
        self.nins = 0

    def _wait(self, eng, deps):
        need = {}
        for (s, v) in deps:
            if s == 'pe' and eng == 'pe':
                continue
            if v > need.get(s, 0):
                need[s] = v
        for s, v in need.items():
            if self.seen[eng].get(s, 0) < v:
                self.eng[eng].wait_ge(self.sems[s], v)
                self.seen[eng][s] = v

    def _deps(self, r, w):
        deps = []
        for x in r:
            t = self.lastw.get(x)
            if t:
                deps.append(t)
        for x in w:
            t = self.lastw.get(x)
            if t:
                deps.append(t)
            rd = self.readers.get(x)
            if rd:
                deps.extend(rd.items())
        return deps

    def _update(self, tok, r, w):
        for x in w:
            self.lastw[x] = tok
            self.readers[x] = {}
        for x in r:
            d = self.readers.setdefault(x, {})
            if tok[1] > d.get(tok[0], 0):
                d[tok[0]] = tok[1]

    def op(self, eng, fn, r=(), w=(), inc=True):
        self._wait(eng, self._deps(r, w))
        ins = fn(self.eng[eng])
        self.nins += 1
        if inc:
            self.cnt[eng] += 1
            ins.then_inc(self.sems[eng], 1)
            tok = (eng, self.cnt[eng])
        else:
            tok = (eng, self.cnt[eng] + 1)
        self._update(tok, r, w)
        return ins

    def dma(self, q, out, in_, r=(), w=(), **kw):
        i = self.dma_rr
        self.dma_rr = (i + 1) % self.NDMA
        s = 'd%d' % i
        deps = self._deps(r, w)
        if self.cnt[s] > 0:
            deps.append((s, self.cnt[s]))
        self._wait(q, deps)
        ins = self.eng[q].dma_start(out=out, in_=in_, **kw)
        ins.then_inc(self.sems[s], 16)
        self.nins += 1
        self.cnt[s] += 16
        tok = (s, self.cnt[s])
        self._update(tok, r, w)
        return ins

    def barrier(self):
        allv = [(s, v) for s, v in self.cnt.items() if v > 0]
        for e in ('pe', 'act', 'dve', 'pool', 'sp'):
            self._wait(e, allv)
        self.lastw.clear()
        self.readers.clear()


def fap(ap, dims):
    return bass.AP(ap.tensor, ap.offset, [list(ap.ap[0])] + [list(d) for d in dims])


class Prog:
    def __init__(self, S, L, dbg=None):
        self.S, self.L = S, L
        self.NB = S // 128
        self.NG = S // 512
        self.NE = (L + 1) // 2
        self.NO = L // 2
        self.dbg = dbg
        nc = self.nc = bass.Bass("TRN2", target_bir_lowering=False)
        self.em = Em(nc)
        self.ps = [nc.alloc_psum_tensor("ps%d" % i, [128, 512], F32) for i in range(8)]
        self.psn = 0
        self.uid = 0
        self.resv = set()
        self.io()
        self.consts()
        for l in range(L):
            if l % 2 == 0:
                self.even_proj(l)
                self.attn_even(l)
            else:
                self.odd_proj(l)
                self.dsa(l)
            self.outproj_ln1(l)
            self.peer_scores(l)
            self.peer_main(l)
        self.finalize()

    def din(self, name, shape, dt=F32):
        return self.nc.dram_tensor(name, list(shape), dt, kind="ExternalInput").ap()

    def dscr(self, name, shape, dt=F32):
        return self.nc.dram_tensor(name, list(shape), dt, kind="Internal").ap()

    def io(self):
        S, L, NE, NO = self.S, self.L, self.NE, max(self.NO, 1)
        self.xT = self.din("xT", [D, S])
        self.pT = self.din("pT", [L, PLE, S])
        self.pos = self.din("pos", [1, S], I32)
        self.rel_bias = self.din("rel_bias", [32, 24])
        self.ev_w_in = self.din("ev_w_in", [NE, D, 2336])
        self.ev_w_uq = self.din("ev_w_uq", [NE, 512, 768])
        self.ev_w_ukv = self.din("ev_w_ukv", [NE, 256, 1024])
        self.ev_q_norm = self.din("ev_q_norm", [NE, 128, 4])
        self.ev_kv_norm = self.din("ev_kv_norm", [NE, 128, 2])
        self.ev_lam = self.din("ev_lam", [NE, 32, 4])
        self.ev_subln = self.din("ev_subln", [NE, 64, 1])
        self.ev_w_o = self.din("ev_w_o", [NE, D, D])
        self.od_w_in = self.din("od_w_in", [NO, D, 2640])
        self.od_w_o = self.din("od_w_o", [NO, D, D])
        self.ln = {k: self.din(k, [L, 128, 8]) for k in ("ln1_g", "ln1_b", "ln2_g", "ln2_b", "ple_gate_b")}
        self.peer_w_q = self.din("peer_w_q", [L, D, 2048])
        self.peer_k1T = self.din("peer_k1T", [L, 128, 128])
        self.peer_k2T = self.din("peer_k2T", [L, 128, 128])
        self.peer_uT = self.din("peer_uT", [L, D, 16384])
        self.peer_v = self.din("peer_v", [L, 16384, D])
        self.ple_w = self.din("ple_w", [L, PLE, D])
        self.ple_gate_w = self.din("ple_gate_w", [L, D, D])
        self.outT = self.nc.dram_tensor("outT", [D, S], F32, kind="ExternalOutput").ap()
        self.hT = self.dscr("hT", [D, S])
        self.QR = self.dscr("QR", [256, S], BF16)
        self.QN = self.dscr("QN", [512, S], BF16)
        self.KR = self.dscr("KR", [32, S], BF16)
        self.KN = self.dscr("KN", [512, S], BF16)
        self.VA = self.dscr("VA", [S, 512], BF16)
        self.QD = self.dscr("QD", [512, S], BF16)
        self.KD = self.dscr("KD", [512, S], BF16)
        self.VD = self.dscr("VD", [S, 512], BF16)
        self.OT = self.dscr("OT", [D, S], BF16)
        self.EB = self.dscr("EB", [24, 6, 128, 512], BF16)
        self.fvec = self.dscr("fvec", [24, RREL])
        self.QC = self.dscr("QC", [1024, S], BF16)
        self.KC = self.dscr("KC", [256, S], BF16)
        self.VC = self.dscr("VC", [S, 256], BF16)
        self.QI = self.dscr("QI", [1024, S], BF16)
        self.KI = self.dscr("KI", [64, S], BF16)
        self.WI = self.dscr("WI", [S, 16])
        self.E1 = self.dscr("E1", [S, 1024])
        self.E2 = self.dscr("E2", [S, 1024])
        self.TH = self.dscr("TH", [S, 8])
        self.FT = self.dscr("FT", [D, S])
        self.COS = self.dscr("COS", [128, S])
        self.SIN = self.dscr("SIN", [128, S])
        self.COSI = self.dscr("COSI", [128, S])
        self.SINI = self.dscr("SINI", [128, S])
        if self.dbg:
            self.dbg_out = self.nc.dram_tensor("dbg", list(self.dbg[1]), self.dbg[2], kind="ExternalOutput").ap()

    def sb(self, name, shape, dt=F32):
        self.uid += 1
        return self.nc.alloc_sbuf_tensor("%s_%d" % (name, self.uid), list(shape), dt)

    def nps(self):
        i = self.psn
        self.psn = (i + 1) % 8
        return i

    def phase(self, name=None):
        prog = self
        prog.phn = getattr(prog, 'phn', 0) + 1
        name = "%s_%d" % (name or "ph", prog.phn)

        class _P:
            def __enter__(s2):
                prog.em.barrier()
                s2.ns = prog.nc.named_scope(name)
                s2.ns.__enter__()
                s2.snap = []
                s2.orig = prog.sb

                def sb(name, shape, dt=F32):
                    prog.uid += 1
                    cm = prog.nc.sbuf_tensor("%s_%d" % (name, prog.uid), list(shape), dt)
                    t = cm.__enter__()
                    s2.snap.append(cm)
                    return t
                prog.sb = sb
                return s2

            def __exit__(s2, *a):
                prog.em.barrier()
                for cm in reversed(s2.snap):
                    cm.__exit__(None, None, None)
                prog.sb = s2.orig
                s2.ns.__exit__(None, None, None)
                return False
        return _P()

    def mm(self, psi, rows, lhsT, rhs, start, stop, r, cols=None, inc=None):
        out = self.ps[psi][0:rows, :] if cols is None else self.ps[psi][0:rows, cols[0]:cols[1]]
        self.em.op('pe', lambda e: e.matmul(out, lhsT=lhsT, rhs=rhs, start=start, stop=stop),
                   r=r, w=['ps%d' % psi], inc=stop if inc is None else inc)

    def dbg_dump(self, key, ap_src_dram):
        if self.dbg and self.dbg[0] == key:
            self.em.barrier()
            self.em.dma('sp', self.dbg_out, ap_src_dram)
            self.em.barrier()

    def consts(self):
        em, nc, S = self.em, self.nc, self.S
        sb = self.sb
        self.ident_bf = sb("identb", [128, 128], BF16)
        self.ident_f = sb("identf", [128, 128])
        self.ones_f = sb("onesf", [128, 128])
        self.sel65 = sb("sel65", [128, 64])
        self.cb = sb("cb", [128, 24])
        self.ncb = sb("ncb", [128, 24])
        self.msk = sb("msk", [128, 4, 512], BF16)
        with self.phase("consts"):
            tmp = self.sb("ctmp", [128, 128])
            em.op('pool', lambda e: e.iota(tmp[:], [[1, 128]], base=0, channel_multiplier=-1,
                                           allow_small_or_imprecise_dtypes=True), w=['ctmp'])
            em.op('dve', lambda e: e.tensor_single_scalar(out=self.ident_bf[:], in_=tmp[:], scalar=0.0, op=ALU.is_equal), r=['ctmp'], w=['idb'])
            em.op('dve', lambda e: e.tensor_single_scalar(out=self.ident_f[:], in_=tmp[:], scalar=0.0, op=ALU.is_equal), r=['ctmp'], w=['idf'])
            em.op('pool', lambda e: e.memset(self.ones_f[:], 1.0), w=['ones'])
            em.op('pool', lambda e: e.memset(self.sel65[:], 0.0), w=['sel'])
            em.op('pool', lambda e: e.memset(self.sel65[64:65, :], 1.0), r=['sel'], w=['sel'])
            qrow = self.sb("qrow", [128, 512])
            em.op('pool', lambda e: e.iota(qrow[:], [[1, 512]], base=0, channel_multiplier=0,
                                           allow_small_or_imprecise_dtypes=True), w=['qrow'])
            for o in range(4):
                em.op('dve', lambda e, o=o: e.tensor_single_scalar(out=self.msk[0:64, o, :], in_=qrow[0:64, :], scalar=float(128 * o), op=ALU.is_ge), r=['qrow'], w=['msk'])
                em.op('dve', lambda e, o=o: e.tensor_single_scalar(out=self.msk[64:128, o, :], in_=qrow[64:128, :], scalar=float(128 * o + 64), op=ALU.is_ge), r=['qrow'], w=['msk'])
            em.dma('sp', self.cb[:], bass.AP(self.rel_bias.tensor, 15 * 24, [[0, 128], [1, 24]]), w=['cb'])
            em.op('dve', lambda e: e.tensor_scalar(out=self.ncb[:], in0=self.cb[:], scalar1=-1.0, scalar2=None, op0=ALU.mult), r=['cb'], w=['ncb'])
            self.rope_tables()
            self.bias_tiles()

    def rope_tables(self):
        em, S, NB = self.em, self.S, self.NB
        posi = self.sb("posi", [128, NB], I32)
        posf = self.sb("posf", [128, NB])
        frow = self.sb("frow", [128, 128])
        sgn = self.sb("sgn", [128, 128])
        ang = self.sb("ang", [128, 128])
        tr = self.sb("trg", [128, 128])
        kf = self.sb("kf", [128, 128])
        ki = self.sb("ki", [128, 128], I32)
        cosT = self.sb("cosT", [128, S])
        sinT = self.sb("sinT", [128, S])
        em.dma('sp', posi[:], bass.AP(self.pos.tensor, 0, [[1, 128], [128, NB]]), w=['posi'], allow_slow_non_contiguous=True)
        em.op('dve', lambda e: e.tensor_copy(out=posf[:], in_=posi[:]), r=['posi'], w=['posf'])
        for j in range(16):
            fr = float(np.float32(10000.0) ** np.float32(-j / 16.0))
            em.op('pool', lambda e, j=j, fr=fr: e.memset(fap(frow[:, j:j + 1], [[16, 8], [1, 1]]), fr), w=['frow'])
        em.op('pool', lambda e: e.memset(sgn[:], 1.0), w=['sgn'])
        em.op('pool', lambda e: e.memset(fap(sgn[:, 0:1], [[32, 4], [1, 16]]), -1.0), r=['sgn'], w=['sgn'])
        TWO_PI = 2.0 * math.pi
        for tb in range(NB):
            for which, tab in ((0, sinT), (1, cosT)):
                em.op('dve', lambda e, tb=tb: e.tensor_scalar(out=ang[:], in0=frow[:], scalar1=posf[:, tb:tb + 1], scalar2=None, op0=ALU.mult), r=['frow', 'posf'], w=['ang'])
                if which == 1:
                    em.op('dve', lambda e: e.tensor_scalar(out=ang[:], in0=ang[:], scalar1=math.pi / 2, scalar2=None, op0=ALU.add), r=['ang'], w=['ang'])
                em.op('dve', lambda e: e.tensor_scalar(out=kf[:], in0=ang[:], scalar1=1.0 / TWO_PI, scalar2=None, op0=ALU.mult), r=['ang'], w=['kf'])
                em.op('dve', lambda e: e.tensor_copy(out=ki[:], in_=kf[:]), r=['kf'], w=['ki'])
                em.op('dve', lambda e: e.tensor_copy(out=kf[:], in_=ki[:]), r=['ki'], w=['kf'])
                em.op('dve', lambda e: e.scalar_tensor_tensor(out=ang[:], in0=kf[:], scalar=-TWO_PI, in1=ang[:], op0=ALU.mult, op1=ALU.add), r=['kf', 'ang'], w=['ang'])
                em.op('dve', lambda e: e.tensor_scalar(out=kf[:], in0=ang[:], scalar1=math.pi, scalar2=-TWO_PI, op0=ALU.is_gt, op1=ALU.mult), r=['ang'], w=['kf'])
                em.op('dve', lambda e: e.tensor_tensor(out=ang[:], in0=ang[:], in1=kf[:], op=ALU.add), r=['kf', 'ang'], w=['ang'])
                em.op('dve', lambda e: e.tensor_scalar(out=kf[:], in0=ang[:], scalar1=-math.pi, scalar2=TWO_PI, op0=ALU.is_lt, op1=ALU.mult), r=['ang'], w=['kf'])
                em.op('dve', lambda e: e.tensor_tensor(out=ang[:], in0=ang[:], in1=kf[:], op=ALU.add), r=['kf', 'ang'], w=['ang'])
                em.op('dve', lambda e: e.tensor_scalar(out=ang[:], in0=ang[:], scalar1=-3.14159, scalar2=3.14159, op0=ALU.max, op1=ALU.min), r=['ang'], w=['ang'])
                em.op('act', lambda e: e.activation(out=tr[:], in_=ang[:], func=ACT.Sin), r=['ang'], w=['trg'])
                if which == 0:
                    em.op('dve', lambda e: e.tensor_tensor(out=tr[:], in0=tr[:], in1=sgn[:], op=ALU.mult), r=['trg', 'sgn'], w=['trg'])
                pi = self.nps()
                self.mm(pi, 128, tr[:], self.ident_f[:], True, True, ['trg', 'idf'], cols=(0, 128))
                em.op('act', lambda e, pi=pi, tab=tab, tb=tb: e.copy(out=tab[:, tb * 128:(tb + 1) * 128], in_=self.ps[pi][:, 0:128]), r=['ps%d' % pi], w=['ropetab'])
        em.dma('sp', self.COS, cosT[:], r=['ropetab'], w=['COS'])
        em.dma('sp', self.SIN, sinT[:], r=['ropetab'], w=['SIN'])
        for r0 in (32, 96):
            em.op('pool', lambda e, r0=r0: e.memset(cosT[r0:r0 + 32, :], 1.0), r=['ropetab', 'COS'], w=['ropetab'])
            em.op('pool', lambda e, r0=r0: e.memset(sinT[r0:r0 + 32, :], 0.0), r=['ropetab', 'SIN'], w=['ropetab'])
        em.dma('sp', self.COSI, cosT[:], r=['ropetab'], w=['COSI'])
        em.dma('sp', self.SINI, sinT[:], r=['ropetab'], w=['SINI'])

    def bias_tiles(self):
        em = self.em
        rel = self.sb("relrow", [32, RREL])
        bk = self.sb("bk", [32, RREL])
        oh = self.sb("oh", [32, RREL])
        pidx = self.sb("pidx", [32, 1])
        tab = self.sb("tab", [32, 24])
        fv = self.sb("fv", [24, RREL])
        em.dma('sp', tab[:], self.rel_bias, w=['tab'])
        em.op('pool', lambda e: e.iota(rel[:], [[1, RREL]], base=-REL0, channel_multiplier=0, allow_small_or_imprecise_dtypes=True), w=['rel'])
        em.op('pool', lambda e: e.iota(pidx[:], [[1, 1]], base=0, channel_multiplier=1, allow_small_or_imprecise_dtypes=True), w=['pidx'])
        em.op('dve', lambda e: e.memset(bk[:], 15.0), w=['bk'])
        steps = [(t, -1.0) for t in (-165, -107, -69, -45, -29, -19, -12, -7, -6, -5, -4, -3, -2, -1, 0)]
        steps += [(1, 17.0)] + [(t, 1.0) for t in (2, 3, 4, 5, 6, 7, 8, 13, 20, 30, 46, 70, 108, 166)]
        for th, dlt in steps:
            if dlt in (1.0, -1.0):
                op1 = ALU.add if dlt > 0 else ALU.subtract
            em.op('dve', lambda e, th=th, dlt=dlt: e.tensor_scalar(out=oh[:], in0=rel[:], scalar1=float(th), scalar2=float(dlt), op0=ALU.is_ge, op1=ALU.mult), r=['rel'], w=['oh'])
            em.op('dve', lambda e: e.tensor_tensor(out=bk[:], in0=bk[:], in1=oh[:], op=ALU.add), r=['oh', 'bk'], w=['bk'])
        em.op('dve', lambda e: e.tensor_scalar(out=oh[:], in0=bk[:], scalar1=pidx[:, 0:1], scalar2=None, op0=ALU.is_equal), r=['bk', 'pidx'], w=['oh'])
        for c0 in range(0, RREL, 512):
            n = min(512, RREL - c0)
            pi = self.nps()
            self.mm(pi, 24, tab[:], oh[:, c0:c0 + n], True, True, ['tab', 'oh'], cols=(0, n))
            em.op('act', lambda e, pi=pi, c0=c0, n=n: e.copy(out=fv[:, c0:c0 + n], in_=self.ps[pi][0:24, 0:n]), r=['ps%d' % pi], w=['fv'])
        em.dma('sp', self.fvec, fv[:], r=['fv'], w=['fvec'])
        antid = self.sb("antid", [128, 128])
        em.op('pool', lambda e: e.iota(antid[:], [[1, 128]], base=-127, channel_multiplier=1, allow_small_or_imprecise_dtypes=True), w=['antid'])
        em.op('dve', lambda e: e.tensor_single_scalar(out=antid[:], in_=antid[:], scalar=0.0, op=ALU.is_equal), r=['antid'], w=['antid'])
        tt = [self.sb("toep%d" % i, [128, 4, 128]) for i in range(2)]
        tx = [self.sb("toepx%d" % i, [128, 512]) for i in range(2)]
        te = [self.sb("toepb%d" % i, [128, 512], BF16) for i in range(2)]
        n = 0
        for h in range(24):
            for oi in range(6):
                o = oi - 2
                b = n % 2
                n += 1
                src = bass.AP(self.fvec.tensor, h * RREL + 128 * o + REL0 - 127, [[1, 128], [-128, 4], [1, 128]])
                em.dma('sp', tt[b][:], src, r=['fvec'], w=['toep%d' % b])
                pi = self.nps()
                for j in range(4):
                    self.mm(pi, 128, tt[b][:, j, :], antid[:], True, True, ['toep%d' % b, 'antid'], cols=(j * 128, (j + 1) * 128), inc=(j == 3))
                if h < 8 and o >= 0:
                    em.op('act', lambda e, b=b, h=h, pi=pi: e.activation(out=tx[b][:], in_=self.ps[pi][:], func=ACT.Exp, bias=self.ncb[:, h:h + 1], scale=1.0), r=['ps%d' % pi, 'ncb'], w=['toepx%d' % b])
                    em.op('dve', lambda e, b=b, o=o: e.tensor_tensor(out=te[b][:], in0=tx[b][:], in1=self.msk[:, o, :], op=ALU.mult), r=['toepx%d' % b, 'msk'], w=['toepb%d' % b])
                else:
                    em.op('act', lambda e, b=b, h=h, pi=pi: e.activation(out=te[b][:], in_=self.ps[pi][:], func=ACT.Exp, bias=self.ncb[:, h:h + 1], scale=1.0), r=['ps%d' % pi, 'ncb'], w=['toepb%d' % b])
                em.dma('sp', self.EB[h, oi], te[b][:], r=['toepb%d' % b], w=['EB'])
        self.dbg_dump('fvec', self.fvec)

    def rms_fm(self, src, C, F, gcol, dst, tag):
        em = self.em
        sq = self.t_sq
        pi = self.nps()
        for c in range(C):
            em.op('act', lambda e, c=c: e.activation(out=sq[:, c, :], in_=src[:, c, :], func=ACT.Square), r=[tag], w=['sq%d' % c])
            self.mm(pi, 128, self.ones_f[:], sq[:, c, :], c == 0, c == C - 1, ['ones', 'sq%d' % c])
        rs = self.t_rs
        em.op('act', lambda e: e.activation(out=rs[:], in_=self.ps[pi][:], func=ACT.Sqrt, bias=self.eps_rms[:, 0:1], scale=1.0 / F), r=['ps%d' % pi, 'eps'], w=['rs'])
        em.op('dve', lambda e: e.reciprocal(out=rs[:], in_=rs[:]), r=['rs'], w=['rs'])
        for c in range(C):
            em.op('dve', lambda e, c=c: e.scalar_tensor_tensor(out=dst[:, c, :], in0=src[:, c, :], scalar=gcol[:, c:c + 1], in1=rs[:], op0=ALU.mult, op1=ALU.mult), r=[tag, 'rs', 'gcols'], w=[tag + 'n'])

    def ln_fm(self, y, gcol, bcol, outf, tag, otag, outb=None):
        em = self.em
        sq = self.t_sq
        p1, p2 = self.nps(), self.nps()
        for c in range(8):
            self.mm(p1, 128, self.ones_f[:], y[:, c, :], c == 0, c == 7, [tag])
        for c in range(8):
            em.op('act', lambda e, c=c: e.activation(out=sq[:, c, :], in_=y[:, c, :], func=ACT.Square), r=[tag], w=['sq%d' % c])
            self.mm(p2, 128, self.ones_f[:], sq[:, c, :], c == 0, c == 7, ['ones', 'sq%d' % c])
        mean, rs, msq = self.t_mean, self.t_rs, self.t_msq
        em.op('act', lambda e: e.activation(out=mean[:], in_=self.ps[p1][:], func=ACT.Copy, scale=1.0 / D), r=['ps%d' % p1], w=['mean'])
        em.op('dve', lambda e: e.tensor_tensor(out=msq[:], in0=mean[:], in1=mean[:], op=ALU.mult), r=['mean'], w=['msq'])
        em.op('dve', lambda e: e.scalar_tensor_tensor(out=rs[:], in0=self.ps[p2][:], scalar=1.0 / D, in1=msq[:], op0=ALU.mult, op1=ALU.subtract), r=['ps%d' % p2, 'msq'], w=['rs'])
        em.op('act', lambda e: e.activation(out=rs[:], in_=rs[:], func=ACT.Sqrt, bias=self.eps_ln[:, 0:1], scale=1.0), r=['rs', 'eps'], w=['rs'])
        em.op('dve', lambda e: e.reciprocal(out=rs[:], in_=rs[:]), r=['rs'], w=['rs'])
        for c in range(8):
            em.op('dve', lambda e, c=c: e.tensor_tensor(out=y[:, c, :], in0=y[:, c, :], in1=mean[:], op=ALU.subtract), r=[tag, 'mean'], w=[tag])
            em.op('dve', lambda e, c=c: e.tensor_tensor(out=y[:, c, :], in0=y[:, c, :], in1=rs[:], op=ALU.mult), r=[tag, 'rs'], w=[tag])
            em.op('act', lambda e, c=c: e.activation(out=outf[:, c, :], in_=y[:, c, :], func=ACT.Identity, bias=bcol[:, c:c + 1], scale=gcol[:, c:c + 1]), r=[tag, 'gcols'], w=[otag])
            if outb is not None:
                em.op('pool', lambda e, c=c: e.tensor_copy(out=outb[:, c, :], in_=outf[:, c, :]), r=[otag], w=[otag + 'b'])

    def norm_tmps(self):
        self.t_sq = self.sb("sq", [128, 8, 512])
        self.t_rs = self.sb("rs", [128, 512])
        self.t_mean = self.sb("mean", [128, 512])
        self.t_msq = self.sb("msq", [128, 512])
        self.eps_rms = self.sb("epsr", [128, 1])
        self.eps_ln = self.sb("epsl", [128, 1])
        self.em.op('pool', lambda e: e.memset(self.eps_rms[:], RMS_EPS), w=['eps'])
        self.em.op('pool', lambda e: e.memset(self.eps_ln[:], LN_EPS), r=['eps'], w=['eps'])

    def wload(self, dst, src2d, K, c0, c1, tag, dcol=0):
        for k in range(K):
            self.em.dma('pool', dst[:, k, dcol:dcol + (c1 - c0)], src2d[k * 128:(k + 1) * 128, c0:c1], w=[tag])

    def hsrc(self, l):
        return self.xT if l == 0 else self.hT

    def even_proj(self, l):
        em, S = self.em, self.S
        j = l // 2
        with self.phase("even_proj"):
            self.norm_tmps()
            win = self.sb("win", [128, 8, 2336], BF16)
            wkrs = self.sb("wkrs", [128, 8, 32], BF16)
            wqn = self.sb("wqn", [128, 4, 512], BF16)
            wqr = self.sb("wqr", [128, 4, 256], BF16)
            wqs = self.sb("wqs", [128, 4, 256], BF16)
            wkn = self.sb("wkn", [128, 2, 512], BF16)
            wv = self.sb("wv", [128, 2, 512], BF16)
            gq = self.sb("gq", [128, 4])
            gkv = self.sb("gkv", [128, 2])
            W = self.ev_w_in[j]
            self.wload(win, W, 8, 0, 2336, 'win')
            self.wload(wkrs, W, 8, 768 + 16, 768 + 32, 'wkrs', 0)
            self.wload(wkrs, W, 8, 768, 768 + 16, 'wkrs', 16)
            UQ, UKV = self.ev_w_uq[j], self.ev_w_ukv[j]
            for h in range(8):
                self.wload(wqn, UQ, 4, h * 96, h * 96 + 64, 'wqn', h * 64)
                self.wload(wqr, UQ, 4, h * 96 + 64, h * 96 + 96, 'wqr', h * 32)
                self.wload(wqs, UQ, 4, h * 96 + 80, h * 96 + 96, 'wqs', h * 32)
                self.wload(wqs, UQ, 4, h * 96 + 64, h * 96 + 80, 'wqs', h * 32 + 16)
                self.wload(wkn, UKV, 2, h * 128, h * 128 + 64, 'wkn', h * 64)
                self.wload(wv, UKV, 2, h * 128 + 64, h * 128 + 128, 'wv', h * 64)
            em.dma('sp', gq[:], self.ev_q_norm[j], w=['gcols'])
            em.dma('sp', gkv[:], self.ev_kv_norm[j], w=['gcols'])
            hsrc = self.hsrc(l).rearrange("(k p) s -> p k s", p=128)
            hb = [self.sb("hb%d" % i, [128, 8, 512], BF16) for i in range(2)]
            cq = self.sb("cq", [128, 4, 512])
            ckv = self.sb("ckv", [128, 2, 512])
            cqn = self.sb("cqn", [128, 4, 512], BF16)
            ckvn = self.sb("ckvn", [128, 2, 512], BF16)
            t1 = self.sb("t1", [128, 512])
            t2 = self.sb("t2", [128, 512])
            ob = [self.sb("ob%d" % i, [128, 512], BF16) for i in range(4)]
            obn = [0]

            def evac_store(pi, rows, dst, eng=None):
                i = obn[0] % 4
                obn[0] += 1
                eng = eng or ('act' if i % 2 == 0 else 'dve')
                if eng == 'act':
                    em.op('act', lambda e: e.copy(out=ob[i][0:rows, :], in_=self.ps[pi][0:rows, :]), r=['ps%d' % pi], w=['ob%d' % i])
                else:
                    em.op('dve', lambda e: e.tensor_copy(out=ob[i][0:rows, :], in_=self.ps[pi][0:rows, :]), r=['ps%d' % pi], w=['ob%d' % i])
                em.dma('sp', dst, ob[i][0:rows, :], r=['ob%d' % i], w=['scr'])

            cs = [self.sb("cs%d" % i, [128, 512]) for i in range(2)]
            sn = [self.sb("sn%d" % i, [128, 512]) for i in range(2)]

            def rope_store(pa, pb, rows, g, dst):
                cg, sg_ = cs[g % 2], sn[g % 2]
                em.op('dve', lambda e: e.tensor_tensor(out=t1[0:rows, :], in0=self.ps[pa][0:rows, :], in1=cg[0:rows, :], op=ALU.mult), r=['ps%d' % pa, 'cs%d' % (g % 2)], w=['t1'])
                em.op('dve', lambda e: e.tensor_tensor(out=t2[0:rows, :], in0=self.ps[pb][0:rows, :], in1=sg_[0:rows, :], op=ALU.mult), r=['ps%d' % pb, 'sn%d' % (g % 2)], w=['t2'])
                i = obn[0] % 4
                obn[0] += 1
                em.op('pool', lambda e: e.tensor_tensor(out=ob[i][0:rows, :], in0=t1[0:rows, :], in1=t2[0:rows, :], op=ALU.add), r=['t1', 't2'], w=['ob%d' % i])
                em.dma('sp', dst, ob[i][0:rows, :], r=['ob%d' % i], w=['scr'])

            for g in range(self.NG):
                gs = slice(g * 512, (g + 1) * 512)
                H = hb[g % 2]
                ht = 'hb%d' % (g % 2)
                em.dma('pool', H[:], hsrc[:, :, gs], w=[ht])
                em.dma('sp', cs[g % 2][:], self.COS[:, gs], w=['cs%d' % (g % 2)])
                em.dma('sp', sn[g % 2][:], self.SIN[:, gs], w=['sn%d' % (g % 2)])
                for c in range(4):
                    pi = self.nps()
                    for k in range(8):
                        self.mm(pi, 128, win[:, k, c * 128:(c + 1) * 128], H[:, k, :], k == 0, k == 7, ['win', ht])
                    em.op('act', lambda e, c=c, pi=pi: e.copy(out=cq[:, c, :], in_=self.ps[pi][:]), r=['ps%d' % pi], w=['cq'])
                for c in range(2):
                    pi = self.nps()
                    for k in range(8):
                        self.mm(pi, 128, win[:, k, 512 + c * 128:512 + (c + 1) * 128], H[:, k, :], k == 0, k == 7, ['win', ht])
                    em.op('dve', lambda e, c=c, pi=pi: e.tensor_copy(out=ckv[:, c, :], in_=self.ps[pi][:]), r=['ps%d' % pi], w=['ckv'])
                self.rms_fm(cq, 4, 512, gq, cqn, 'cq')
                self.rms_fm(ckv, 2, 256, gkv, ckvn, 'ckv')
                pa, pb = self.nps(), self.nps()
                for k in range(8):
                    self.mm(pa, 32, win[:, k, 768:800], H[:, k, :], k == 0, k == 7, ['win', ht])
                for k in range(8):
                    self.mm(pb, 32, wkrs[:, k, :], H[:, k, :], k == 0, k == 7, ['wkrs', ht])
                rope_store(pa, pb, 32, g, self.KR[:, gs])
                for c in range(4):
                    for base, dst in ((800, self.QD), (1312, self.KD)):
                        pi = self.nps()
                        for k in range(8):
                            self.mm(pi, 128, win[:, k, base + c * 128:base + (c + 1) * 128], H[:, k, :], k == 0, k == 7, ['win', ht])
                        evac_store(pi, 128, dst[c * 128:(c + 1) * 128, gs])
                for tb in range(4):
                    pi = self.nps()
                    for k in range(8):
                        self.mm(pi, 128, H[:, k, tb * 128:(tb + 1) * 128], win[:, k, 1824:2336], k == 0, k == 7, ['win', ht])
                    evac_store(pi, 128, self.VD[g * 512 + tb * 128:g * 512 + (tb + 1) * 128, :])
                for rc in range(2):
                    pa, pb = self.nps(), self.nps()
                    for k in range(4):
                        self.mm(pa, 128, wqr[:, k, rc * 128:(rc + 1) * 128], cqn[:, k, :], k == 0, k == 3, ['wqr', 'cqn'])
                    for k in range(4):
                        self.mm(pb, 128, wqs[:, k, rc * 128:(rc + 1) * 128], cqn[:, k, :], k == 0, k == 3, ['wqs', 'cqn'])
                    rope_store(pa, pb, 128, g, self.QR[rc * 128:(rc + 1) * 128, gs])
                for c in range(4):
                    pi = self.nps()
                    for k in range(4):
                        self.mm(pi, 128, wqn[:, k, c * 128:(c + 1) * 128], cqn[:, k, :], k == 0, k == 3, ['wqn', 'cqn'])
                    evac_store(pi, 128, self.QN[c * 128:(c + 1) * 128, gs])
                    pi = self.nps()
                    for k in range(2):
                        self.mm(pi, 128, wkn[:, k, c * 128:(c + 1) * 128], ckvn[:, k, :], k == 0, k == 1, ['wkn', 'ckvn'])
                    evac_store(pi, 128, self.KN[c * 128:(c + 1) * 128, gs])
                for tb in range(4):
                    pi = self.nps()
                    for k in range(2):
                        self.mm(pi, 128, ckvn[:, k, tb * 128:(tb + 1) * 128], wv[:, k, :], k == 0, k == 1, ['wv', 'ckvn'])
                    evac_store(pi, 128, self.VA[g * 512 + tb * 128:g * 512 + (tb + 1) * 128, :])
        self.dbg_dump('QR', self.QR)
        self.dbg_dump('QN', self.QN)
        self.dbg_dump('KR', self.KR)
        self.dbg_dump('VA', self.VA)
        self.dbg_dump('QD', self.QD)

    def attn_begin(self, p, LA=2):
        p['q'] = []
        p['n'] = 4 * p['g'] + 4
        for kb in range(min(LA, p['n'])):
            p['q'].append(self.attn_qk(p, kb))

    def attn_qk(self, p, kb):
        pi = self.nps()
        while pi in self.resv:
            pi = self.nps()
        rows = p['rows']
        q0 = p['g'] * 512 if p.get('qoff') is None else p['qoff']
        self.mm(pi, 128, p['kt'][0:rows, kb * 128:(kb + 1) * 128], p['qt'][0:rows, q0:q0 + 512], True, True, p['tags'])
        return pi

    def attn_body(self, p, ptb, ptn, LA=2):
        em = self.em
        n = p['n']
        scale, bias_col = p['scale'], p['bias']
        for kb in range(n):
            pi = p['q'][kb]
            i = ptn[0] % len(ptb)
            ptn[0] += 1
            P = ptb[i]
            pt = 'pt%d' % i
            if bias_col is None:
                em.op('act', lambda e, pi=pi, P=P: e.activation(out=P[:], in_=self.ps[pi][:], func=ACT.Exp, scale=scale), r=['ps%d' % pi], w=[pt])
            else:
                em.op('act', lambda e, pi=pi, P=P: e.activation(out=P[:], in_=self.ps[pi][:], func=ACT.Exp, bias=bias_col, scale=scale), r=['ps%d' % pi, 'cb'], w=[pt])
            p['mult'](kb, P, pt)
            if kb + LA < n:
                p['q'].append(self.attn_qk(p, kb + LA))
            self.mm(p['pso'], 65, p['vt'][:, kb, :], P[:], kb == 0, kb == n - 1, [pt] + p['tags'])

    def attn_run(self, passes, ptb, ptn):
        if not passes:
            return
        self.attn_begin(passes[0])
        for i, p in enumerate(passes):
            if 'pre' in p:
                pass
            self.attn_body(p, ptb, ptn)
            if i + 1 < len(passes):
                if 'load' in passes[i + 1]:
                    passes[i + 1]['load']()
                self.attn_begin(passes[i + 1])
            p['finish']()

    def attn_norm(self, pso, osb, rc, tag):
        em = self.em
        em.op('act', lambda e: e.copy(out=osb[0:65, :], in_=self.ps[pso][0:65, :]), r=['ps%d' % pso], w=[tag])
        pi = self.nps()
        while pi in self.resv:
            pi = self.nps()
        self.mm(pi, 64, self.sel65[0:65, :], osb[0:65, :], True, True, ['sel', tag])
        em.op('dve', lambda e: e.reciprocal(out=rc[0:64, :], in_=self.ps[pi][0:64, :]), r=['ps%d' % pi], w=['rc'])
        em.op('dve', lambda e: e.tensor_tensor(out=osb[0:64, :], in0=osb[0:64, :], in1=rc[0:64, :], op=ALU.mult), r=['rc', tag], w=[tag])

    def attn_even(self, l):
        em, S, NB = self.em, self.S, self.NB
        j = l // 2
        lam_init = 0.8 - 0.6 * math.exp(-0.3 * l)
        with self.phase("attn_even"):
            ptb = [self.sb("pt%d" % i, [128, 512], BF16) for i in range(4)]
            ptn = [0]
            self.resv = {6, 7}
            osb = [self.sb("osb%d" % i, [128, 512]) for i in range(2)]
            rc = self.sb("rc", [128, 512])
            o16 = [self.sb("o16%d" % i, [64, 512], BF16) for i in range(2)]
            qt = [self.sb("qt%d" % i, [96, S], BF16) for i in range(2)]
            kt = [self.sb("kt%d" % i, [96, S], BF16) for i in range(2)]
            vt = [self.sb("vt%d" % i, [128, NB, 65], BF16) for i in range(2)]
            for i in range(2):
                em.op('pool', lambda e, i=i: e.memset(vt[i][:, :, 64:65], 1.0), w=['vt%d' % i])

            def mult_a(g):
                def f(kb, P, pt):
                    if kb >= 4 * g:
                        em.op('dve', lambda e: e.tensor_tensor(out=P[:], in0=P[:], in1=self.msk[:, kb - 4 * g, :], op=ALU.mult), r=[pt, 'msk'], w=[pt])
                return f
            passes = []
            n = 0
            for h in range(8):
                b = h % 2
                tg = ['qt%d' % b, 'kt%d' % b, 'vt%d' % b]

                def load(h=h, b=b, tg=tg):
                    em.dma('sp', qt[b][0:32, :], self.QR[h * 32:(h + 1) * 32, :], w=[tg[0]])
                    em.dma('sp', qt[b][32:96, :], self.QN[h * 64:(h + 1) * 64, :], w=[tg[0]])
                    em.dma('sp', kt[b][0:32, :], self.KR[:, :], w=[tg[1]])
                    em.dma('sp', kt[b][32:96, :], self.KN[h * 64:(h + 1) * 64, :], w=[tg[1]])
                    em.dma('sp', vt[b][:, :, 0:64], self.VA[:, h * 64:(h + 1) * 64].rearrange("(nb p) d -> p nb d", p=128), w=[tg[2]])
                for g in range(self.NG):
                    pso = 6 + n % 2
                    ob = n % 2
                    n += 1

                    def finish(h=h, g=g, pso=pso, ob=ob):
                        self.attn_norm(pso, osb[ob], rc, 'osb%d' % ob)
                        em.op('pool', lambda e: e.tensor_copy(out=o16[ob][:], in_=osb[ob][0:64, :]), r=['osb%d' % ob], w=['o16%d' % ob])
                        em.dma('sp', self.OT[h * 64:(h + 1) * 64, g * 512:(g + 1) * 512], o16[ob][:], r=['o16%d' % ob], w=['OT'])
                    p = dict(qt=qt[b], kt=kt[b], vt=vt[b], rows=96, g=g, scale=A_SCALE, bias=None, mult=mult_a(g), tags=tg, pso=pso, finish=finish)
                    if g == 0:
                        p['load'] = load
                    passes.append(p)
            passes[0]['load']()
            self.attn_run(passes, ptb, ptn)
        self.dbg_dump('OTA', self.OT)
        with self.phase("attn_even"):
            ptb = [self.sb("pt%d" % i, [128, 512], BF16) for i in range(4)]
            ptn = [0]
            self.resv = {4, 5, 6, 7}
            osb = [self.sb("osb%d" % i, [128, 512]) for i in range(2)]
            rc = self.sb("rc", [128, 512])
            od = self.sb("od", [64, 512])
            sq = self.sb("sq", [64, 512])
            o16 = [self.sb("o16%d" % i, [64, 512], BF16) for i in range(2)]
            qt = [self.sb("qt%d" % i, [32, S], BF16) for i in range(4)]
            kt = [self.sb("kt%d" % i, [32, S], BF16) for i in range(4)]
            vt = [self.sb("vt%d" % i, [128, NB, 65], BF16) for i in range(2)]
            eb = [self.sb("eb%d" % i, [128, 6, 512], BF16) for i in range(2)]
            for i in range(2):
                em.op('pool', lambda e, i=i: e.memset(vt[i][:, :, 64:65], 1.0), w=['vt%d' % i])
            lam4 = self.sb("lam4", [32, 4])
            lamp = self.sb("lamp", [32, 2])
            lamc = self.sb("lamc", [64, 4])
            sg = self.sb("sg", [64, 1])
            epsr = self.sb("epsr2", [64, 1])
            em.op('pool', lambda e: e.memset(epsr[:], RMS_EPS), w=['epsr2'])
            em.dma('sp', lam4[:], self.ev_lam[j], w=['lam4'])
            em.dma('sp', sg[:], self.ev_subln[j], w=['sg'])
            em.op('dve', lambda e: e.tensor_tensor(out=lamp[:, 0:1], in0=lam4[:, 0:1], in1=lam4[:, 1:2], op=ALU.mult), r=['lam4'], w=['lamp'])
            em.op('dve', lambda e: e.tensor_tensor(out=lamp[:, 1:2], in0=lam4[:, 2:3], in1=lam4[:, 3:4], op=ALU.mult), r=['lam4', 'lamp'], w=['lamp'])
            pi = 0
            self.mm(pi, 64, self.ones_f[0:32, 0:64], lamp[:, 0:2], True, True, ['ones', 'lamp'], cols=(0, 2))
            em.op('act', lambda e: e.activation(out=lamc[:, 0:2], in_=self.ps[pi][0:64, 0:2], func=ACT.Exp), r=['ps%d' % pi], w=['lamc'])
            em.op('dve', lambda e: e.tensor_tensor(out=lamc[:, 2:3], in0=lamc[:, 1:2], in1=lamc[:, 0:1], op=ALU.subtract), r=['lamc'], w=['lamc'])
            em.op('dve', lambda e: e.tensor_scalar(out=lamc[:, 3:4], in0=lamc[:, 2:3], scalar1=-lam_init, scalar2=None, op0=ALU.add), r=['lamc'], w=['lamc'])
            em.op('dve', lambda e: e.tensor_scalar(out=sg[:], in0=sg[:], scalar1=1.0 - lam_init, scalar2=None, op0=ALU.mult), r=['sg'], w=['sg'])

            def mult_b(g, E, et):
                def f(kb, P, pt):
                    if kb >= 4 * g - 2:
                        em.op('dve', lambda e: e.tensor_tensor(out=P[:], in0=P[:], in1=E[:, kb - 4 * g + 2, :], op=ALU.mult), r=[pt, et], w=[pt])
                return f
            passes = []
            n = 0
            for h in range(8):
                b = h % 2

                def load(h=h, b=b):
                    em.dma('sp', eb[b][:], self.EB[h].rearrange("o p q -> p o q"), w=['eb%d' % b])
                    em.dma('sp', vt[b][:, :, 0:64], self.VD[:, h * 64:(h + 1) * 64].rearrange("(nb p) d -> p nb d", p=128), w=['vt%d' % b])
                    for m in range(2):
                        i = b * 2 + m
                        r0 = (h * 2 + m) * 32
                        em.dma('sp', qt[i][:], self.QD[r0:r0 + 32, :], w=['qt%d' % i])
                        em.dma('sp', kt[i][:], self.KD[r0:r0 + 32, :], w=['kt%d' % i])
                for g in range(self.NG):
                    ob = n % 2
                    for m in range(2):
                        i = b * 2 + m
                        pso = 4 + m + 2 * (n % 2)
                        tg = ['qt%d' % i, 'kt%d' % i, 'vt%d' % b]

                        def finish(h=h, g=g, m=m, pso=pso, ob=ob):
                            self.attn_norm(pso, osb[m], rc, 'osb%d' % m)
                            if m == 0:
                                return
                            em.op('dve', lambda e: e.scalar_tensor_tensor(out=od[:], in0=osb[1][0:64, :], scalar=lamc[:, 3:4], in1=osb[0][0:64, :], op0=ALU.mult, op1=ALU.add), r=['osb0', 'osb1', 'lamc'], w=['od'])
                            em.op('act', lambda e: e.activation(out=sq[:], in_=od[:], func=ACT.Square), r=['od'], w=['sqd'])
                            pi = self.nps()
                            while pi in self.resv:
                                pi = self.nps()
                            self.mm(pi, 64, self.ones_f[0:64, 0:64], sq[:], True, True, ['ones', 'sqd'])
                            em.op('act', lambda e: e.activation(out=sq[:], in_=self.ps[pi][0:64, :], func=ACT.Sqrt, bias=epsr[:, 0:1], scale=1.0 / 64), r=['ps%d' % pi, 'epsr2'], w=['sqd'])
                            em.op('dve', lambda e: e.reciprocal(out=sq[:], in_=sq[:]), r=['sqd'], w=['sqd'])
                            em.op('dve', lambda e: e.scalar_tensor_tensor(out=o16[ob][:], in0=od[:], scalar=sg[:, 0:1], in1=sq[:], op0=ALU.mult, op1=ALU.mult), r=['od', 'sqd', 'sg'], w=['o16%d' % ob])
                            em.dma('sp', self.OT[512 + h * 64:512 + (h + 1) * 64, g * 512:(g + 1) * 512], o16[ob][:], r=['o16%d' % ob], w=['OT'])
                        p = dict(qt=qt[i], kt=kt[i], vt=vt[b], rows=32, g=g, scale=B_SCALE, bias=self.cb[:, h:h + 1], mult=mult_b(g, eb[b], 'eb%d' % b), tags=tg, pso=pso, finish=finish)
                        if g == 0 and m == 0:
                            p['load'] = load
                        passes.append(p)
                    n += 1
            passes[0]['load']()
            self.attn_run(passes, ptb, ptn)
        self.resv = set()
        self.dbg_dump('OT', self.OT)

    def outproj_ln1(self, l):
        em = self.em
        j = l // 2
        WO = self.ev_w_o[j] if l % 2 == 0 else self.od_w_o[j]
        with self.phase("outproj_ln1"):
            self.norm_tmps()
            wo = self.sb("wo", [128, 8, D], BF16)
            self.wload(wo, WO, 8, 0, D, 'wo')
            g1 = self.sb("g1", [128, 8])
            b1 = self.sb("b1", [128, 8])
            em.dma('sp', g1[:], self.ln["ln1_g"][l], w=['gcols'])
            em.dma('sp', b1[:], self.ln["ln1_b"][l], w=['gcols'])
            hsrc = self.hsrc(l).rearrange("(k p) s -> p k s", p=128)
            hdst = self.hT.rearrange("(k p) s -> p k s", p=128)
            otv = self.OT.rearrange("(k p) s -> p k s", p=128)
            ot = [self.sb("ot%d" % i, [128, 8, 512], BF16) for i in range(2)]
            hr = [self.sb("hr%d" % i, [128, 8, 512]) for i in range(2)]
            y = self.sb("y", [128, 8, 512])
            ho = [self.sb("ho%d" % i, [128, 8, 512]) for i in range(2)]
            for g in range(self.NG):
                gs = slice(g * 512, (g + 1) * 512)
                b = g % 2
                em.dma('sp', ot[b][:], otv[:, :, gs], w=['ot%d' % b])
                em.dma('sp', hr[b][:], hsrc[:, :, gs], w=['hr%d' % b])
                for n in range(8):
                    pi = self.nps()
                    for k in range(8):
                        self.mm(pi, 128, wo[:, k, n * 128:(n + 1) * 128], ot[b][:, k, :], k == 0, k == 7, ['wo', 'ot%d' % b])
                    em.op('dve', lambda e, n=n, pi=pi: e.scalar_tensor_tensor(out=y[:, n, :], in0=hr[b][:, n, :], scalar=DN_ALPHA, in1=self.ps[pi][:], op0=ALU.mult, op1=ALU.add), r=['ps%d' % pi, 'hr%d' % b], w=['y'])
                self.ln_fm(y, g1, b1, ho[b], 'y', 'ho%d' % b)
                em.dma('sp', hdst[:, :, gs], ho[b][:], r=['ho%d' % b], w=['hT'])
        self.dbg_dump('h1', self.hT)

    def odd_proj(self, l):
        em, S = self.em, self.S
        j = l // 2
        with self.phase("odd_proj"):
            win = self.sb("win", [128, 8, 2640], BF16)
            wsw = self.sb("wsw", [128, 8, 1088], BF16)
            W = self.od_w_in[j]
            self.wload(win, W, 8, 0, 2640, 'win')
            self.wload(wsw, W, 8, 1536, 2624, 'wsw')
            for hh in range(17):
                c0 = 1536 + hh * 64
                self.wload(wsw, W, 8, c0 + 16, c0 + 32, 'wsw', hh * 64)
                self.wload(wsw, W, 8, c0, c0 + 16, 'wsw', hh * 64 + 16)
            hsrc = self.hsrc(l).rearrange("(k p) s -> p k s", p=128)
            hb = [self.sb("hb%d" % i, [128, 8, 512], BF16) for i in range(2)]
            cs = [self.sb("cs%d" % i, [128, 512]) for i in range(2)]
            sn = [self.sb("sn%d" % i, [128, 512]) for i in range(2)]
            t1 = self.sb("t1", [128, 512])
            t2 = self.sb("t2", [128, 512])
            ob = [self.sb("ob%d" % i, [128, 512], BF16) for i in range(4)]
            wo_ = [self.sb("wio%d" % i, [128, 16]) for i in range(2)]
            obn = [0]

            def evac_store(pi, rows, dst, ncol=512):
                i = obn[0] % 4
                obn[0] += 1
                if i % 2 == 0:
                    em.op('act', lambda e: e.copy(out=ob[i][0:rows, 0:ncol], in_=self.ps[pi][0:rows, 0:ncol]), r=['ps%d' % pi], w=['ob%d' % i])
                else:
                    em.op('dve', lambda e: e.tensor_copy(out=ob[i][0:rows, 0:ncol], in_=self.ps[pi][0:rows, 0:ncol]), r=['ps%d' % pi], w=['ob%d' % i])
                em.dma('sp', dst, ob[i][0:rows, 0:ncol], r=['ob%d' % i], w=['scr'])

            def rope_store(pa, pb, rows, g, dst):
                cg, sg_ = cs[g % 2], sn[g % 2]
                em.op('dve', lambda e: e.tensor_tensor(out=t1[0:rows, :], in0=self.ps[pa][0:rows, :], in1=cg[0:rows, :], op=ALU.mult), r=['ps%d' % pa, 'cs%d' % (g % 2)], w=['t1'])
                em.op('dve', lambda e: e.tensor_tensor(out=t2[0:rows, :], in0=self.ps[pb][0:rows, :], in1=sg_[0:rows, :], op=ALU.mult), r=['ps%d' % pb, 'sn%d' % (g % 2)], w=['t2'])
                i = obn[0] % 4
                obn[0] += 1
                em.op('pool', lambda e: e.tensor_tensor(out=ob[i][0:rows, :], in0=t1[0:rows, :], in1=t2[0:rows, :], op=ALU.add), r=['t1', 't2'], w=['ob%d' % i])
                em.dma('sp', dst, ob[i][0:rows, :], r=['ob%d' % i], w=['scr'])

            nw = 0
            for g in range(self.NG):
                gs = slice(g * 512, (g + 1) * 512)
                H = hb[g % 2]
                ht = 'hb%d' % (g % 2)
                em.dma('pool', H[:], hsrc[:, :, gs], w=[ht])
                em.dma('sp', cs[g % 2][:], self.COSI[:, gs], w=['cs%d' % (g % 2)])
                em.dma('sp', sn[g % 2][:], self.SINI[:, gs], w=['sn%d' % (g % 2)])
                for c in range(8):
                    pi = self.nps()
                    for k in range(8):
                        self.mm(pi, 128, win[:, k, c * 128:(c + 1) * 128], H[:, k, :], k == 0, k == 7, ['win', ht])
                    evac_store(pi, 128, self.QC[c * 128:(c + 1) * 128, gs])
                for c in range(2):
                    pi = self.nps()
                    for k in range(8):
                        self.mm(pi, 128, win[:, k, 1024 + c * 128:1024 + (c + 1) * 128], H[:, k, :], k == 0, k == 7, ['win', ht])
                    evac_store(pi, 128, self.KC[c * 128:(c + 1) * 128, gs])
                for tb in range(4):
                    pi = self.nps()
                    for k in range(8):
                        self.mm(pi, 128, H[:, k, tb * 128:(tb + 1) * 128], win[:, k, 1280:1536], k == 0, k == 7, ['win', ht], cols=(0, 256))
                    r0 = g * 512 + tb * 128
                    evac_store(pi, 128, self.VC[r0:r0 + 128, :], ncol=256)
                    pi = self.nps()
                    for k in range(8):
                        self.mm(pi, 128, H[:, k, tb * 128:(tb + 1) * 128], win[:, k, 2624:2640], k == 0, k == 7, ['win', ht], cols=(0, 16))
                    wb = nw % 2
                    nw += 1
                    em.op('act', lambda e, pi=pi, wb=wb: e.activation(out=wo_[wb][:], in_=self.ps[pi][:, 0:16], func=ACT.Copy, scale=0.25 * IDX_SCALE), r=['ps%d' % pi], w=['wio%d' % wb])
                    em.dma('sp', self.WI[r0:r0 + 128, :], wo_[wb][:], r=['wio%d' % wb], w=['scr'])
                for c in range(8):
                    pa, pb = self.nps(), self.nps()
                    for k in range(8):
                        self.mm(pa, 128, win[:, k, 1536 + c * 128:1536 + (c + 1) * 128], H[:, k, :], k == 0, k == 7, ['win', ht])
                    for k in range(8):
                        self.mm(pb, 128, wsw[:, k, c * 128:(c + 1) * 128], H[:, k, :], k == 0, k == 7, ['wsw', ht])
                    rope_store(pa, pb, 128, g, self.QI[c * 128:(c + 1) * 128, gs])
                pa, pb = self.nps(), self.nps()
                for k in range(8):
                    self.mm(pa, 64, win[:, k, 2560:2624], H[:, k, :], k == 0, k == 7, ['win', ht])
                for k in range(8):
                    self.mm(pb, 64, wsw[:, k, 1024:1088], H[:, k, :], k == 0, k == 7, ['wsw', ht])
                rope_store(pa, pb, 64, g, self.KI[:, gs])
        self.dbg_dump('QC', self.QC)
        self.dbg_dump('QI', self.QI)
        self.dbg_dump('WI', self.WI)

    def dsa(self, l):
        em, S, NB = self.em, self.S, self.NB
        NIT = 26
        NSEL = float(min(256, S // 4))
        with self.phase("dsa"):
            ki = self.sb("ki", [64, S], BF16)
            kc = [self.sb("kc%d" % i, [64, S], BF16) for i in range(4)]
            vc = [self.sb("vc%d" % i, [128, NB, 65], BF16) for i in range(4)]
            em.dma('sp', ki[:], self.KI, w=['ki'])
            for i in range(4):
                em.dma('sp', kc[i][:], self.KC[i * 64:(i + 1) * 64, :], w=['kc%d' % i])
                em.op('pool', lambda e, i=i: e.memset(vc[i][:, :, 64:65], 1.0), w=['vc%d' % i])
                em.dma('sp', vc[i][:, :, 0:64], self.VC[:, i * 64:(i + 1) * 64].rearrange("(nb p) d -> p nb d", p=128), w=['vc%d' % i])
            qg = self.sb("qg", [64, 16, 512], BF16)
            wig = self.sb("wig", [128, 4, 16])
            acc = self.sb("acc", [128, S])
            mq = self.sb("mq", [128, S], BF16)
            maskT = self.sb("maskT", [128, NB, 512], BF16)
            rl = [self.sb("rl%d" % i, [128, 512]) for i in range(2)]
            ec = [self.sb("ec%d" % i, [128, 6, 512], BF16) for i in range(2)]
            ptb = [self.sb("pt%d" % i, [128, 512], BF16) for i in range(4)]
            ptn = [0]
            osb = [self.sb("osb%d" % i, [128, 512]) for i in range(2)]
            rc = self.sb("rc", [128, 512])
            o16 = [self.sb("o16%d" % i, [64, 512], BF16) for i in range(2)]
            bs = self.sb("bs", [128, 8])
            wt = self.sb("wt", [128, NIT + 1])
            wt2 = self.sb("wt2", [128, NIT + 1])
            pw2 = self.sb("pw2", [128, NIT + 1])
            cntT = self.sb("cntT", [128, NIT])
            for i in range(NIT + 1):
                em.op('pool', lambda e, i=i: e.memset(pw2[:, i:i + 1], 2.0 ** -(i + 1)), w=['pw2'])
            qiv = self.QI.rearrange("(h d) s -> d h s", d=64)
            qcv = self.QC.rearrange("(h d) s -> d h s", d=64)
            nrl = 0
            nec = 0
            no = 0
            for g in range(self.NG):
                gs = slice(g * 512, (g + 1) * 512)
                em.dma('sp', qg[:], qiv[:, :, gs], w=['qg'])
                em.dma('sp', wig[:], self.WI[g * 512:(g + 1) * 512, :].rearrange("(t p) h -> p t h", p=128), w=['wig'])
                em.op('pool', lambda e: e.memset(maskT[:], 0.0), w=['maskT'])
                for qt in range(4):
                    T = 4 * g + qt
                    Lk = 128 * (T + 1)
                    nck = (Lk + 511) // 512
                    for h in range(16):
                        for kcn in range(nck):
                            n = min(512, Lk - kcn * 512)
                            pi = self.nps()
                            self.mm(pi, 128, qg[:, h, qt * 128:(qt + 1) * 128], ki[:, kcn * 512:kcn * 512 + n], True, True, ['qg', 'ki'], cols=(0, n))
                            ri = nrl % 2
                            nrl += 1
                            em.op('act', lambda e, pi=pi, ri=ri, n=n: e.activation(out=rl[ri][:, 0:n], in_=self.ps[pi][:, 0:n], func=ACT.Relu), r=['ps%d' % pi], w=['rl%d' % ri])
                            ksl = slice(kcn * 512, kcn * 512 + n)
                            if h == 0:
                                em.op('dve', lambda e, ri=ri, n=n, ksl=ksl, qt=qt: e.tensor_scalar(out=acc[:, ksl], in0=rl[ri][:, 0:n], scalar1=wig[:, qt, 0:1], scalar2=None, op0=ALU.mult), r=['rl%d' % ri, 'wig'], w=['acc'])
                            else:
                                em.op('dve', lambda e, ri=ri, n=n, ksl=ksl, qt=qt, h=h: e.scalar_tensor_tensor(out=acc[:, ksl], in0=rl[ri][:, 0:n], scalar=wig[:, qt, h:h + 1], in1=acc[:, ksl], op0=ALU.mult, op1=ALU.add), r=['rl%d' % ri, 'wig', 'acc'], w=['acc'])
                    if T >= 2:
                        em.op('dve', lambda e, Lk=Lk: e.tensor_reduce(out=bs[:, 0:1], in_=acc[:, 0:Lk], axis=AX.X, op=ALU.max), r=['acc'], w=['bs'])
                        em.op('dve', lambda e, Lk=Lk: e.tensor_reduce(out=bs[:, 1:2], in_=acc[:, 0:Lk], axis=AX.X, op=ALU.min), r=['acc', 'bs'], w=['bs'])
                    em.op('dve', lambda e, Lk=Lk: e.memset(acc[0:64, Lk - 64:Lk], NEG), r=['acc'], w=['acc'])
                    if T >= 2:
                        em.op('dve', lambda e: e.tensor_tensor(out=bs[:, 2:3], in0=bs[:, 0:1], in1=bs[:, 1:2], op=ALU.subtract), r=['bs'], w=['bs'])
                        em.op('dve', lambda e: e.tensor_scalar(out=wt[:], in0=pw2[:], scalar1=bs[:, 2:3], scalar2=None, op0=ALU.mult), r=['bs', 'pw2'], w=['wt'])
                        em.op('dve', lambda e: e.tensor_scalar(out=wt2[:], in0=wt[:], scalar1=2.0, scalar2=None, op0=ALU.mult), r=['wt'], w=['wt2'])
                        em.op('dve', lambda e: e.tensor_tensor(out=bs[:, 4:5], in0=bs[:, 1:2], in1=wt[:, 0:1], op=ALU.add), r=['bs', 'wt'], w=['bs'])
                        em.op('dve', lambda e: e.memset(cntT[:], 0.0), w=['cntT'])
                        for i in range(NIT):
                            em.op('dve', lambda e, i=i, Lk=Lk: e.tensor_scalar(out=mq[:, 0:Lk], in0=acc[:, 0:Lk], scalar1=bs[:, 4:5], scalar2=0.0, op0=ALU.is_ge, op1=ALU.add, accum_out=cntT[:, i:i + 1]), r=['acc', 'bs', 'cntT'], w=['mq', 'cntT'])
                            em.op('dve', lambda e, i=i: e.scalar_tensor_tensor(out=bs[:, 5:6], in0=cntT[:, i:i + 1], scalar=NSEL, in1=wt2[:, i + 1:i + 2], op0=ALU.is_ge, op1=ALU.mult), r=['cntT', 'wt2', 'bs'], w=['bs'])
                            em.op('dve', lambda e, i=i: e.scalar_tensor_tensor(out=bs[:, 4:5], in0=bs[:, 4:5], scalar=wt[:, i + 1:i + 2], in1=bs[:, 5:6], op0=ALU.subtract, op1=ALU.add), r=['bs', 'wt'], w=['bs'])
                        em.op('dve', lambda e: e.tensor_tensor(out=bs[:, 3:4], in0=bs[:, 4:5], in1=wt[:, NIT:NIT + 1], op=ALU.subtract), r=['bs', 'wt'], w=['bs'])
                        em.op('dve', lambda e, Lk=Lk: e.tensor_scalar(out=mq[:, 0:Lk], in0=acc[:, 0:Lk], scalar1=bs[:, 3:4], scalar2=None, op0=ALU.is_ge), r=['acc', 'bs'], w=['mq'])
                    else:
                        em.op('dve', lambda e, Lk=Lk: e.tensor_scalar(out=mq[:, 0:Lk], in0=acc[:, 0:Lk], scalar1=-1.0e30, scalar2=None, op0=ALU.is_ge), r=['acc'], w=['mq'])
                    for kb0 in range(0, T + 1, 4):
                        nk = min(4, T + 1 - kb0)
                        pi = self.nps()
                        for q_ in range(nk):
                            kb = kb0 + q_
                            self.mm(pi, 128, mq[:, kb * 128:(kb + 1) * 128], self.ident_bf[:], True, True, ['mq'], cols=(q_ * 128, (q_ + 1) * 128), inc=(q_ == nk - 1))
                        src = fap(self.ps[pi][:, 0:nk * 128], [[128, nk], [1, 128]])
                        em.op('act', lambda e, src=src, kb0=kb0, nk=nk, qt=qt: e.copy(out=maskT[:, kb0:kb0 + nk, qt * 128:(qt + 1) * 128], in_=src), r=['ps%d' % pi], w=['maskT'])
                em.dma('sp', qg[:], qcv[:, :, gs], w=['qg'])
                self.resv = {6, 7}
                passes = []
                for h in range(16):
                    kv = h // 4
                    eb_ = nec % 2
                    nec += 1

                    def load(h=h, eb_=eb_):
                        em.dma('sp', ec[eb_][:], self.EB[8 + h].rearrange("o p q -> p o q"), w=['ec%d' % eb_])

                    def mult_c(kb, P, pt, g=g, eb_=eb_):
                        em.op('dve', lambda e: e.tensor_tensor(out=P[:], in0=P[:], in1=maskT[:, kb, :], op=ALU.mult), r=[pt, 'maskT'], w=[pt])
                        if kb >= 4 * g - 2:
                            em.op('dve', lambda e: e.tensor_tensor(out=P[:], in0=P[:], in1=ec[eb_][:, kb - 4 * g + 2, :], op=ALU.mult), r=[pt, 'ec%d' % eb_], w=[pt])
                    pso = 6 + no % 2
                    ob = no % 2
                    no += 1

                    def finish(h=h, pso=pso, ob=ob, gs=gs):
                        self.attn_norm(pso, osb[ob], rc, 'osb%d' % ob)
                        em.op('pool', lambda e: e.tensor_copy(out=o16[ob][:], in_=osb[ob][0:64, :]), r=['osb%d' % ob], w=['o16%d' % ob])
                        em.dma('sp', self.OT[h * 64:(h + 1) * 64, gs], o16[ob][:], r=['o16%d' % ob], w=['OT'])
                    passes.append(dict(qt=qg[:, h, :], kt=kc[kv], vt=vc[kv], rows=64, g=g, scale=C_SCALE, bias=self.cb[:, 8 + h:9 + h], mult=mult_c, tags=['qg', 'kc%d' % kv, 'vc%d' % kv], pso=pso, finish=finish, qoff=0, load=load))
                passes[0]['load']()
                self.attn_run(passes, ptb, ptn)
                self.resv = set()
        self.dbg_dump('OT', self.OT)

    def peer_scores(self, l):
        em, S = self.em, self.S
        with self.phase("peer_scores"):
            k1 = self.sb("k1", [128, 128])
            k2 = self.sb("k2", [128, 128])
            em.dma('sp', k1[:], self.peer_k1T[l], w=['k1'])
            em.dma('sp', k2[:], self.peer_k2T[l], w=['k2'])
            hsrc = self.hT.rearrange("(k p) s -> p k s", p=128)
            wqv = self.peer_w_q[l].rearrange("(k p) f -> p k f", p=128)
            xf = self.sb("xf", [128, 8, 512])
            wqc = [self.sb("wqc%d" % i, [128, 8, 128]) for i in range(2)]
            qT = [self.sb("qT%d" % i, [128, 512]) for i in range(2)]
            sall = [self.sb("sall%d" % i, [128, 16, 128]) for i in range(4)]
            m16 = self.sb("m16", [128, 16, 24])
            wkA = self.sb("wkA", [128, 16, 128])
            wkB = self.sb("wkB", [128, 16, 128])
            cand = self.sb("cand", [128, 8, 576])
            ckA = self.sb("ckA", [128, 8, 576])
            ckB = self.sb("ckB", [128, 8, 576])
            c24 = self.sb("c24", [128, 8, 24])
            ez = self.sb("ez", [128, 8, 16])
            st = self.sb("st", [128, 8, 6])
            eall = self.sb("eall", [128, 16, 128])
            e1o = [self.sb("e1o%d" % i, [128, 8, 128]) for i in range(2)]
            e2o = [self.sb("e2o%d" % i, [128, 8, 128]) for i in range(2)]
            tho = [self.sb("tho%d" % i, [128, 8]) for i in range(2)]
            nt = 0
            for g in range(self.NG):
                gs = slice(g * 512, (g + 1) * 512)
                em.dma('sp', xf[:], hsrc[:, :, gs], w=['xf'])
                for c in range(16):
                    b = c % 2
                    em.dma('sp', wqc[b][:], wqv[:, :, c * 128:(c + 1) * 128], w=['wqc%d' % b])
                    pi = self.nps()
                    for k in range(8):
                        self.mm(pi, 128, wqc[b][:, k, :], xf[:, k, :], k == 0, k == 7, ['wqc%d' % b, 'xf'])
                    em.op('act', lambda e, pi=pi, b=b: e.copy(out=qT[b][:], in_=self.ps[pi][:]), r=['ps%d' % pi], w=['qT%d' % b])
                    kk, kt = (k1, 'k1') if c % 2 == 0 else (k2, 'k2')
                    for tb in range(4):
                        pj = self.nps()
                        self.mm(pj, 128, qT[b][:, tb * 128:(tb + 1) * 128], kk[:], True, True, ['qT%d' % b, kt], cols=(0, 128))
                        em.op('dve' if tb % 2 else 'act', lambda e, pj=pj, tb=tb, c=c: (e.tensor_copy if tb % 2 else e.copy)(out=sall[tb][:, c, :], in_=self.ps[pj][:, 0:128]), r=['ps%d' % pj], w=['sall%d' % tb])
                for tb in range(4):
                    sa = sall[tb]
                    sat = 'sall%d' % tb
                    ob = nt % 2
                    nt += 1
                    for c in range(16):
                        em.op('dve', lambda e, c=c: e.max(out=m16[:, c, 0:8], in_=sa[:, c, :]), r=[sat], w=['m16_%d' % c])
                    for c in range(16):
                        em.op('dve', lambda e, c=c: e.match_replace(out=wkA[:, c, :], in_to_replace=m16[:, c, 0:8], in_values=sa[:, c, :], imm_value=NEG), r=[sat, 'm16_%d' % c], w=['wkA%d' % c])
                    for c in range(16):
                        em.op('dve', lambda e, c=c: e.max(out=m16[:, c, 8:16], in_=wkA[:, c, :]), r=['wkA%d' % c], w=['m16_%d' % c])
                    for c in range(16):
                        em.op('dve', lambda e, c=c: e.match_replace(out=wkB[:, c, :], in_to_replace=m16[:, c, 8:16], in_values=wkA[:, c, :], imm_value=NEG), r=['wkA%d' % c, 'm16_%d' % c], w=['wkB%d' % c])
                    for c in range(16):
                        em.op('dve', lambda e, c=c: e.max(out=m16[:, c, 16:24], in_=wkB[:, c, :]), r=['wkB%d' % c], w=['m16_%d' % c])
                    for h in range(8):
                        a_ap = fap(m16[:, 2 * h, :], [[1, 24], [0, 24]])
                        b_ap = fap(m16[:, 2 * h + 1, :], [[0, 24], [1, 24]])
                        o_ap = fap(cand[:, h, :], [[24, 24], [1, 24]])
                        em.op('pool', lambda e, a_ap=a_ap, b_ap=b_ap, o_ap=o_ap: e.tensor_tensor(out=o_ap, in0=a_ap, in1=b_ap, op=ALU.add), r=['m16_%d' % (2 * h), 'm16_%d' % (2 * h + 1)], w=['cand%d' % h])
                    for h in range(8):
                        em.op('dve', lambda e, h=h: e.max(out=c24[:, h, 0:8], in_=cand[:, h, :]), r=['cand%d' % h], w=['c24_%d' % h])
                    for h in range(8):
                        em.op('dve', lambda e, h=h: e.match_replace(out=ckA[:, h, :], in_to_replace=c24[:, h, 0:8], in_values=cand[:, h, :], imm_value=NEG), r=['cand%d' % h, 'c24_%d' % h], w=['ckA%d' % h])
                    for h in range(8):
                        em.op('dve', lambda e, h=h: e.max(out=c24[:, h, 8:16], in_=ckA[:, h, :]), r=['ckA%d' % h], w=['c24_%d' % h])
                    for h in range(8):
                        em.op('dve', lambda e, h=h: e.match_replace(out=ckB[:, h, :], in_to_replace=c24[:, h, 8:16], in_values=ckA[:, h, :], imm_value=NEG), r=['ckA%d' % h, 'c24_%d' % h], w=['ckB%d' % h])
                    for h in range(8):
                        em.op('dve', lambda e, h=h: e.max(out=c24[:, h, 16:24], in_=ckB[:, h, :]), r=['ckB%d' % h], w=['c24_%d' % h])
                    m16t = ['m16_%d' % c for c in range(16)]
                    c24t = ['c24_%d' % h for h in range(8)]
                    mx_ap = fap(m16[:, 0, 0:1], [[24, 16], [0, 128]])
                    em.op('dve', lambda e, mx_ap=mx_ap: e.tensor_tensor(out=eall[:], in0=sa[:], in1=mx_ap, op=ALU.subtract), r=[sat] + m16t, w=['eall'])
                    em.op('act', lambda e: e.activation(out=eall[:], in_=eall[:], func=ACT.Exp), r=['eall'], w=['eall'])
                    cm_ap = fap(c24[:, 0, 0:1], [[24, 8], [0, 16]])
                    em.op('dve', lambda e, cm_ap=cm_ap: e.tensor_tensor(out=ez[:], in0=c24[:, :, 0:16], in1=cm_ap, op=ALU.subtract), r=c24t, w=['ez'])
                    em.op('act', lambda e: e.activation(out=ez[:], in_=ez[:], func=ACT.Exp), r=['ez'], w=['ez'])
                    em.op('dve', lambda e: e.tensor_reduce(out=st[:, :, 0], in_=ez[:], axis=AX.X, op=ALU.add), r=['ez'], w=['st'])
                    em.op('dve', lambda e: e.reciprocal(out=st[:, :, 1], in_=st[:, :, 0]), r=['st'], w=['st'])
                    em.op('dve', lambda e: e.tensor_tensor(out=st[:, :, 2], in0=c24[:, :, 15], in1=c24[:, :, 16], op=ALU.add), r=c24t + ['st'], w=['st'])
                    em.op('dve', lambda e: e.scalar_tensor_tensor(out=st[:, :, 3], in0=st[:, :, 2], scalar=0.5, in1=c24[:, :, 0], op0=ALU.mult, op1=ALU.subtract), r=c24t + ['st'], w=['st'])
                    em.op('act', lambda e: e.activation(out=st[:, :, 4], in_=st[:, :, 3], func=ACT.Exp), r=['st'], w=['st'])
                    em.op('dve', lambda e, ob=ob: e.tensor_tensor(out=tho[ob][:], in0=st[:, :, 4], in1=st[:, :, 1], op=ALU.mult), r=['st'], w=['tho%d' % ob])
                    rz_ap = fap(st[:, 0, 1:2], [[6, 8], [0, 128]])
                    e1_ap = fap(eall[:, 0, :], [[256, 8], [1, 128]])
                    e2_ap = fap(eall[:, 1, :], [[256, 8], [1, 128]])
                    em.op('dve', lambda e, ob=ob, rz_ap=rz_ap, e1_ap=e1_ap: e.tensor_tensor(out=e1o[ob][:], in0=e1_ap, in1=rz_ap, op=ALU.mult), r=['eall', 'st'], w=['e1o%d' % ob])
                    em.op('pool', lambda e, ob=ob, e2_ap=e2_ap: e.tensor_copy(out=e2o[ob][:], in_=e2_ap), r=['eall'], w=['e2o%d' % ob])
                    r0 = g * 512 + tb * 128
                    em.dma('sp', self.E1[r0:r0 + 128, :], e1o[ob][:].rearrange("p h n -> p (h n)"), r=['e1o%d' % ob], w=['E1'])
                    em.dma('sp', self.E2[r0:r0 + 128, :], e2o[ob][:].rearrange("p h n -> p (h n)"), r=['e2o%d' % ob], w=['E2'])
                    em.dma('sp', self.TH[r0:r0 + 128, :], tho[ob][:], r=['tho%d' % ob], w=['TH'])
        self.dbg_dump('E1', self.E1)
        self.dbg_dump('TH', self.TH)

    def peer_main(self, l):
        em, S = self.em, self.S
        NEG_ = 32
        with self.phase("peer_main"):
            hsrc = self.hT.rearrange("(k p) s -> p k s", p=128)
            uTv = self.peer_uT[l].rearrange("(k p) e -> p k e", p=128)
            vv = self.peer_v[l].rearrange("(g c p) d -> g p c d", p=128, c=4)
            ftv = self.FT.rearrange("(k p) s -> p k s", p=128)
            xb = self.sb("xb", [128, 8, 512], BF16)
            acc = self.sb("acc", [128, 8, 512])
            u16 = [self.sb("u16%d" % i, [128, 8, 512], BF16) for i in range(2)]
            v16 = [self.sb("v16%d" % i, [128, 4, 1024], BF16) for i in range(2)]
            e1t = [self.sb("e1t%d" % i, [128, 8, 128]) for i in range(4)]
            e2t = [self.sb("e2t%d" % i, [128, 8, 128]) for i in range(4)]
            tht = [self.sb("tht%d" % i, [128, 8]) for i in range(4)]
            glT = self.sb("glT", [128, 4, 512])
            AT = self.sb("AT", [128, 4, 512], BF16)
            yb = [self.sb("yb%d" % i, [128, 512]) for i in range(6)]
            gb = [self.sb("gb%d" % i, [128, 512], BF16) for i in range(6)]
            ny = ng = 0
            for g in range(self.NG):
                gs = slice(g * 512, (g + 1) * 512)
                em.dma('pool', xb[:], hsrc[:, :, gs], w=['xb'])
                for tt in range(4):
                    r0 = g * 512 + tt * 128
                    em.dma('sp', e1t[tt][:].rearrange("p h n -> p (h n)"), self.E1[r0:r0 + 128, :], w=['e1t%d' % tt])
                    em.dma('sp', e2t[tt][:].rearrange("p h n -> p (h n)"), self.E2[r0:r0 + 128, :], w=['e2t%d' % tt])
                    em.dma('sp', tht[tt][:], self.TH[r0:r0 + 128, :], w=['tht%d' % tt])
                for eg in range(NEG_):
                    wb = eg % 2
                    ut, vt_ = 'u16%d' % wb, 'v16%d' % wb
                    em.dma('pool', u16[wb][:], uTv[:, :, eg * 512:(eg + 1) * 512], w=[ut])
                    em.dma('pool', v16[wb][:], vv[eg], w=[vt_])
                    for c in range(4):
                        for k in range(8):
                            self.mm(c, 128, u16[wb][:, k, c * 128:(c + 1) * 128], xb[:, k, :], k == 0, k == 7, [ut, 'xb'])
                        em.op('act', lambda e, c=c: e.activation(out=glT[:, c, :], in_=self.ps[c][:], func=ACT.Gelu_apprx_tanh), r=['ps%d' % c], w=['glT'])
                    for tt in range(4):
                        for h in range(8):
                            yi = ny % 6
                            ny += 1
                            gi = ng % 6
                            ng += 1
                            for i1 in range(2):
                                em.op('act', lambda e, i1=i1, yi=yi, tt=tt, h=h: e.activation(out=yb[yi][:, i1 * 128:(i1 + 1) * 128], in_=e2t[tt][:, h, :], func=ACT.Copy, scale=e1t[tt][:, h, eg * 4 + i1:eg * 4 + i1 + 1]), r=['e1t%d' % tt, 'e2t%d' % tt], w=['yb%d' % yi])
                            a_ap = fap(e1t[tt][:, h, eg * 4 + 2:eg * 4 + 4], [[1, 2], [0, 128]])
                            b_ap = fap(e2t[tt][:, h, :], [[0, 2], [1, 128]])
                            o_ap = fap(yb[yi][:, 256:512], [[128, 2], [1, 128]])
                            em.op('pool', lambda e, a_ap=a_ap, b_ap=b_ap, o_ap=o_ap: e.tensor_tensor(out=o_ap, in0=a_ap, in1=b_ap, op=ALU.mult), r=['e1t%d' % tt, 'e2t%d' % tt], w=['yb%d' % yi])
                            em.op('dve', lambda e, yi=yi, gi=gi, tt=tt, h=h: e.scalar_tensor_tensor(out=gb[gi][:], in0=yb[yi][:], scalar=tht[tt][:, h:h + 1], in1=yb[yi][:], op0=ALU.is_ge, op1=ALU.mult), r=['yb%d' % yi, 'tht%d' % tt], w=['gb%d' % gi])
                            for c in range(4):
                                self.mm(4 + c, 128, gb[gi][:, c * 128:(c + 1) * 128], self.ident_bf[:], h == 0, h == 7, ['gb%d' % gi], cols=(tt * 128, (tt + 1) * 128), inc=(c == 3))
                    for c in range(4):
                        em.op('dve', lambda e, c=c: e.tensor_tensor(out=AT[:, c, :], in0=glT[:, c, :], in1=self.ps[4 + c][:], op=ALU.mult), r=['glT', 'ps%d' % (4 + c)], w=['AT'])
                    for dch in range(8):
                        pi = dch % 4
                        for c in range(4):
                            self.mm(pi, 128, v16[wb][:, c, dch * 128:(dch + 1) * 128], AT[:, c, :], c == 0, c == 3, [vt_, 'AT'])
                        if eg == 0:
                            em.op('dve', lambda e, dch=dch, pi=pi: e.tensor_copy(out=acc[:, dch, :], in_=self.ps[pi][:]), r=['ps%d' % pi], w=['acc'])
                        else:
                            em.op('dve', lambda e, dch=dch, pi=pi: e.tensor_tensor(out=acc[:, dch, :], in0=acc[:, dch, :], in1=self.ps[pi][:], op=ALU.add), r=['ps%d' % pi, 'acc'], w=['acc'])
                em.dma('sp', ftv[:, :, gs], acc[:], r=['acc'], w=['FT'])
        self.dbg_dump('FT', self.FT)
        self.peer_post(l)

    def peer_post(self, l):
        em = self.em
        with self.phase("peer_post"):
            self.norm_tmps()
            wg = self.sb("wg", [128, 8, D], BF16)
            pw = self.sb("pw", [128, 2, D], BF16)
            self.wload(wg, self.ple_gate_w[l], 8, 0, D, 'wg')
            self.wload(pw, self.ple_w[l], 2, 0, D, 'pw')
            g2 = self.sb("g2", [128, 8])
            b2 = self.sb("b2", [128, 8])
            bg = self.sb("bg", [128, 8])
            em.dma('sp', g2[:], self.ln["ln2_g"][l], w=['gcols'])
            em.dma('sp', b2[:], self.ln["ln2_b"][l], w=['gcols'])
            em.dma('sp', bg[:], self.ln["ple_gate_b"][l], w=['gcols'])
            hv = self.hT.rearrange("(k p) s -> p k s", p=128)
            ftv = self.FT.rearrange("(k p) s -> p k s", p=128)
            pv = self.pT[l].rearrange("(k p) s -> p k s", p=128)
            hr = [self.sb("hr%d" % i, [128, 8, 512]) for i in range(1)]
            ft = [self.sb("ft%d" % i, [128, 8, 512]) for i in range(1)]
            pb = [self.sb("pb%d" % i, [128, 2, 512], BF16) for i in range(1)]
            y = self.sb("y", [128, 8, 512])
            h2 = self.sb("h2", [128, 8, 512])
            h2b = self.sb("h2b", [128, 8, 512], BF16)
            gt = self.sb("gt", [128, 512])
            ho = [self.sb("ho%d" % i, [128, 8, 512]) for i in range(1)]
            for g in range(self.NG):
                gs = slice(g * 512, (g + 1) * 512)
                b = 0
                em.dma('sp', hr[b][:], hv[:, :, gs], w=['hr%d' % b])
                em.dma('sp', ft[b][:], ftv[:, :, gs], w=['ft%d' % b])
                em.dma('pool', pb[b][:], pv[:, :, gs], w=['pb%d' % b])
                for n in range(8):
                    em.op('dve', lambda e, n=n: e.scalar_tensor_tensor(out=y[:, n, :], in0=hr[b][:, n, :], scalar=DN_ALPHA, in1=ft[b][:, n, :], op0=ALU.mult, op1=ALU.add), r=['hr%d' % b, 'ft%d' % b], w=['y'])
                self.ln_fm(y, g2, b2, h2, 'y', 'h2', outb=h2b)
                for n in range(8):
                    pi, pj = self.nps(), self.nps()
                    for k in range(8):
                        self.mm(pi, 128, wg[:, k, n * 128:(n + 1) * 128], h2b[:, k, :], k == 0, k == 7, ['wg', 'h2b'])
                    for k in range(2):
                        self.mm(pj, 128, pw[:, k, n * 128:(n + 1) * 128], pb[b][:, k, :], k == 0, k == 1, ['pw', 'pb%d' % b])
                    em.op('act', lambda e, n=n, pi=pi: e.activation(out=gt[:], in_=self.ps[pi][:], func=ACT.Sigmoid, bias=bg[:, n:n + 1], scale=1.0), r=['ps%d' % pi, 'gcols'], w=['gt'])
                    em.op('dve', lambda e, pj=pj: e.tensor_tensor(out=gt[:], in0=gt[:], in1=self.ps[pj][:], op=ALU.mult), r=['gt', 'ps%d' % pj], w=['gt'])
                    em.op('dve', lambda e, n=n: e.tensor_tensor(out=ho[b][:, n, :], in0=gt[:], in1=h2[:, n, :], op=ALU.add), r=['gt', 'h2'], w=['ho%d' % b])
                em.dma('sp', hv[:, :, gs], ho[b][:], r=['ho%d' % b], w=['hT%d' % g])
        self.dbg_dump('h2', self.hT)

    def finalize(self):
        em = self.em
        em.barrier()
        if not self.dbg:
            em.dma('sp', self.outT, self.hT)
        em.barrier()


def _cols(v, C):
    return np.ascontiguousarray(np.asarray(v, np.float32).reshape(C, 128).T)


def prep_inputs(inp, b, S, L):
    NE, NO = (L + 1) // 2, max(L // 2, 1)
    f = lambda a: np.ascontiguousarray(np.asarray(a, np.float32))
    m = {}
    m["xT"] = f(np.asarray(inp["x"])[b, :S].T)
    m["pT"] = f(np.transpose(np.asarray(inp["p"])[:L, b, :S], (0, 2, 1)))
    m["pos"] = np.ascontiguousarray(np.asarray(inp["positions"])[b, :S].reshape(1, S).astype(np.int32))
    m["rel_bias"] = f(inp["rel_bias"])
    m["ev_w_in"] = f(np.asarray(inp["ev_w_in"])[:NE])
    m["ev_w_uq"] = f(np.asarray(inp["ev_w_uq"])[:NE])
    m["ev_w_ukv"] = f(np.asarray(inp["ev_w_ukv"])[:NE])
    m["ev_q_norm"] = np.stack([_cols(np.asarray(inp["ev_q_norm"])[i], 4) for i in range(NE)])
    m["ev_kv_norm"] = np.stack([_cols(np.asarray(inp["ev_kv_norm"])[i], 2) for i in range(NE)])
    m["ev_lam"] = f(np.stack([np.asarray(inp[k])[:NE] for k in ("ev_lam_q1", "ev_lam_k1", "ev_lam_q2", "ev_lam_k2")], axis=-1))
    m["ev_subln"] = f(np.asarray(inp["ev_subln"])[:NE].reshape(NE, 64, 1))
    m["ev_w_o"] = f(np.asarray(inp["ev_w_o"])[:NE])
    m["od_w_in"] = f(np.asarray(inp["od_w_in"])[:NO])
    m["od_w_o"] = f(np.asarray(inp["od_w_o"])[:NO])
    for k in ("ln1_g", "ln1_b", "ln2_g", "ln2_b", "ple_gate_b"):
        m[k] = np.stack([_cols(np.asarray(inp[k])[i], 8) for i in range(L)])
    m["peer_w_q"] = f(np.asarray(inp["peer_w_q"])[:L])
    m["peer_k1T"] = f(np.transpose(np.asarray(inp["peer_k1"])[:L], (0, 2, 1)))
    m["peer_k2T"] = f(np.transpose(np.asarray(inp["peer_k2"])[:L], (0, 2, 1)))
    m["peer_uT"] = f(np.transpose(np.asarray(inp["peer_u"])[:L], (0, 2, 1)))
    m["peer_v"] = f(np.asarray(inp["peer_v"])[:L])
    m["ple_w"] = f(np.asarray(inp["ple_w"])[:L])
    m["ple_gate_w"] = f(np.asarray(inp["ple_gate_w"])[:L])
    return m


def kernel(**inputs):
    B, S, L = 8, 4096, 4
    prog = Prog(S, L)
    shared = None
    in_maps = []
    for b in range(B):
        m = prep_inputs(inputs, b, S, L)
        if shared is None:
            shared = m
        else:
            for k in m:
                if k not in ("xT", "pT", "pos"):
                    m[k] = shared[k]
        in_maps.append(m)
    res = run_bass_kernel_spmd(prog.nc, in_maps, core_ids=list(range(B)))
    out = np.stack([np.ascontiguousarray(res.results[b]["outT"].T) for b in range(B)])
    return out.astype(np.float32)
```

```python
import math
import numpy as np
import concourse.bass as bass
import concourse.mybir as mybir
from concourse.bass_utils import run_bass_kernel_spmd

F32 = mybir.dt.float32
BF16 = mybir.dt.bfloat16
I32 = mybir.dt.int32
ALU = mybir.AluOpType
ACT = mybir.ActivationFunctionType
AX = mybir.AxisListType

D = 1024
PLE = 256
LN_EPS = 1e-5
RMS_EPS = 1e-6
A_SCALE = 96 ** -0.5
B_SCALE = 32 ** -0.5
C_SCALE = 64 ** -0.5
IDX_SCALE = 64 ** -0.5
DEPTH_FULL = 4
DN_ALPHA = (2 * DEPTH_FULL) ** 0.25
NEG = -3.0e38
RREL = 1279
REL0 = 767


class Em:
    NDMA = 24

    def __init__(self, nc):
        self.nc = nc
        self.eng = {'pe': nc.tensor, 'act': nc.scalar, 'dve': nc.vector,
                    'pool': nc.gpsimd, 'sp': nc.sync}
        self.sems = {}
        self.cnt = {}
        self.seen = {e: {} for e in self.eng}
        self.lastw = {}
        self.readers = {}
        for e in ('pe', 'act', 'dve', 'pool'):
            self.sems[e] = nc.alloc_semaphore("s_" + e)
            self.cnt[e] = 0
        self.rings = {'sp': ['d%d' % i for i in range(self.NDMA)], 'pool': ['q%d' % i for i in range(8)]}
        self.rr = {'sp': 0, 'pool': 0}
        for ring in self.rings.values():
            for s in ring:
                self.sems[s] = nc.alloc_semaphore("s_" + s)
                self.cnt[s] = 0
        self.nins = 0

    def _wait(self, eng, deps):
        need = {}
        for (s, v) in deps:
            if s == 'pe' and eng == 'pe':
                continue
            if v > need.get(s, 0):
                need[s] = v
        for s, v in need.items():
            if self.seen[eng].get(s, 0) < v:
                self.eng[eng].wait_ge(self.sems[s], v)
                self.seen[eng][s] = v

    def _deps(self, r, w):
        deps = []
        for x in r:
            t = self.lastw.get(x)
            if t:
                deps.append(t)
        for x in w:
            t = self.lastw.get(x)
            if t:
                deps.append(t)
            rd = self.readers.get(x)
            if rd:
                deps.extend(rd.items())
        return deps

    def _update(self, tok, r, w):
        for x in w:
            self.lastw[x] = tok
            self.readers[x] = {}
        for x in r:
            d = self.readers.setdefault(x, {})
            if tok[1] > d.get(tok[0], 0):
                d[tok[0]] = tok[1]

    def op(self, eng, fn, r=(), w=(), inc=True):
        self._wait(eng, self._deps(r, w))
        ins = fn(self.eng[eng])
        self.nins += 1
        if inc:
            self.cnt[eng] += 1
            ins.then_inc(self.sems[eng], 1)
            tok = (eng, self.cnt[eng])
        else:
            tok = (eng, self.cnt[eng] + 1)
        self._update(tok, r, w)
        return ins

    def dma(self, q, out, in_, r=(), w=(), **kw):
        ring = self.rings[q]
        s = ring[self.rr[q]]
        self.rr[q] = (self.rr[q] + 1) % len(ring)
        deps = self._deps(r, w)
        if self.cnt[s] > 0:
            deps.append((s, self.cnt[s]))
        self._wait(q, deps)
        ins = self.eng[q].dma_start(out=out, in_=in_, **kw)
        ins.then_inc(self.sems[s], 16)
        self.nins += 1
        self.cnt[s] += 16
        tok = (s, self.cnt[s])
        self._update(tok, r, w)
        return ins

    def barrier(self):
        allv = [(s, v) for s, v in self.cnt.items() if v > 0]
        for e in ('pe', 'act', 'dve', 'pool', 'sp'):
            self._wait(e, allv)
        self.lastw.clear()
        self.readers.clear()


def fap(ap, dims):
    return bass.AP(ap.tensor, ap.offset, [list(ap.ap[0])] + [list(d) for d in dims])


class Prog:
    def __init__(self, S, L, dbg=None):
        self.S, self.L = S, L
        self.NB = S // 128
        self.NG = S // 512
        self.NE = (L + 1) // 2
        self.NO = L // 2
        self.dbg = dbg
        nc = self.nc = bass.Bass("TRN2", target_bir_lowering=False)
        self.em = Em(nc)
        self.ps = [nc.alloc_psum_tensor("ps%d" % i, [128, 512], F32) for i in range(8)]
        self.psn = 0
        self.uid = 0
        self.resv = set()
        self.io()
        self.consts()
        for l in range(L):
            if l % 2 == 0:
                self.even_proj(l)
                self.attn_even(l)
            else:
                self.odd_proj(l)
                self.dsa(l)
            self.outproj_ln1(l)
            self.peer_scores(l)
            self.peer_main(l)
        self.finalize()

    def din(self, name, shape, dt=F32):
        return self.nc.dram_tensor(name, list(shape), dt, kind="ExternalInput").ap()

    def dscr(self, name, shape, dt=F32):
        return self.nc.dram_tensor(name, list(shape), dt, kind="Internal").ap()

    def io(self):
        S, L, NE, NO = self.S, self.L, self.NE, max(self.NO, 1)
        self.xT = self.din("xT", [D, S])
        self.pT = self.din("pT", [L, PLE, S])
        self.pos = self.din("pos", [1, S], I32)
        self.rel_bias = self.din("rel_bias", [32, 24])
        self.ev_w_in = self.din("ev_w_in", [NE, D, 2336])
        self.ev_w_uq = self.din("ev_w_uq", [NE, 512, 768])
        self.ev_w_ukv = self.din("ev_w_ukv", [NE, 256, 1024])
        self.ev_q_norm = self.din("ev_q_norm", [NE, 128, 4])
        self.ev_kv_norm = self.din("ev_kv_norm", [NE, 128, 2])
        self.ev_lam = self.din("ev_lam", [NE, 32, 4])
        self.ev_subln = self.din("ev_subln", [NE, 64, 1])
        self.ev_w_o = self.din("ev_w_o", [NE, D, D])
        self.od_w_in = self.din("od_w_in", [NO, D, 2640])
        self.od_w_o = self.din("od_w_o", [NO, D, D])
        self.ln = {k: self.din(k, [L, 128, 8]) for k in ("ln1_g", "ln1_b", "ln2_g", "ln2_b", "ple_gate_b")}
        self.peer_w_q = self.din("peer_w_q", [L, D, 2048])
        self.peer_k1T = self.din("peer_k1T", [L, 128, 128])
        self.peer_k2T = self.din("peer_k2T", [L, 128, 128])
        self.peer_uT = self.din("peer_uT", [L, D, 16384])
        self.peer_v = self.din("peer_v", [L, 16384, D])
        self.ple_w = self.din("ple_w", [L, PLE, D])
        self.ple_gate_w = self.din("ple_gate_w", [L, D, D])
        self.outT = self.nc.dram_tensor("outT", [D, S], F32, kind="ExternalOutput").ap()
        self.hT = self.dscr("hT", [D, S])
        self.QR = self.dscr("QR", [256, S], BF16)
        self.QN = self.dscr("QN", [512, S], BF16)
        self.KR = self.dscr("KR", [32, S], BF16)
        self.KN = self.dscr("KN", [512, S], BF16)
        self.VA = self.dscr("VA", [S, 512], BF16)
        self.QD = self.dscr("QD", [512, S], BF16)
        self.KD = self.dscr("KD", [512, S], BF16)
        self.VD = self.dscr("VD", [S, 512], BF16)
        self.OT = self.dscr("OT", [D, S], BF16)
        self.EB = self.dscr("EB", [24, 6, 128, 512], BF16)
        self.fvec = self.dscr("fvec", [24, RREL])
        self.QC = self.dscr("QC", [1024, S], BF16)
        self.KC = self.dscr("KC", [256, S], BF16)
        self.VC = self.dscr("VC", [S, 256], BF16)
        self.QI = self.dscr("QI", [1024, S], BF16)
        self.KI = self.dscr("KI", [64, S], BF16)
        self.WI = self.dscr("WI", [S, 16])
        self.E1 = self.dscr("E1", [S, 1024])
        self.E2 = self.dscr("E2", [S, 1024])
        self.TH = self.dscr("TH", [S, 8])
        self.FT = self.dscr("FT", [D, S])
        self.COS = self.dscr("COS", [128, S])
        self.SIN = self.dscr("SIN", [128, S])
        self.COSI = self.dscr("COSI", [128, S])
        self.SINI = self.dscr("SINI", [128, S])
        if self.dbg:
            self.dbg_out = self.nc.dram_tensor("dbg", list(self.dbg[1]), self.dbg[2], kind="ExternalOutput").ap()

    def sb(self, name, shape, dt=F32):
        self.uid += 1
        return self.nc.alloc_sbuf_tensor("%s_%d" % (name, self.uid), list(shape), dt)

    def nps(self):
        i = self.psn
        self.psn = (i + 1) % 8
        return i

    def phase(self, name=None):
        prog = self
        prog.phn = getattr(prog, 'phn', 0) + 1
        name = "%s_%d" % (name or "ph", prog.phn)

        class _P:
            def __enter__(s2):
                prog.em.barrier()
                s2.ns = prog.nc.named_scope(name)
                s2.ns.__enter__()
                s2.snap = []
                s2.orig = prog.sb

                def sb(name, shape, dt=F32):
                    prog.uid += 1
                    cm = prog.nc.sbuf_tensor("%s_%d" % (name, prog.uid), list(shape), dt)
                    t = cm.__enter__()
                    s2.snap.append(cm)
                    return t
                prog.sb = sb
                return s2

            def __exit__(s2, *a):
                prog.em.barrier()
                for cm in reversed(s2.snap):
                    cm.__exit__(None, None, None)
                prog.sb = s2.orig
                s2.ns.__exit__(None, None, None)
                return False
        return _P()

    def mm(self, psi, rows, lhsT, rhs, start, stop, r, cols=None, inc=None):
        out = self.ps[psi][0:rows, :] if cols is None else self.ps[psi][0:rows, cols[0]:cols[1]]
        self.em.op('pe', lambda e: e.matmul(out, lhsT=lhsT, rhs=rhs, start=start, stop=stop),
                   r=r, w=['ps%d' % psi], inc=stop if inc is None else inc)

    def dbg_dump(self, key, ap_src_dram):
        if self.dbg and self.dbg[0] == key:
            self.em.barrier()
            self.em.dma('sp', self.dbg_out, ap_src_dram)
            self.em.barrier()

    def consts(self):
        em, nc, S = self.em, self.nc, self.S
        sb = self.sb
        self.ident_bf = sb("identb", [128, 128], BF16)
        self.ident_f = sb("identf", [128, 128])
        self.ones_f = sb("onesf", [128, 128])
        self.sel65 = sb("sel65", [128, 64])
        self.cb = sb("cb", [128, 24])
        self.ncb = sb("ncb", [128, 24])
        self.msk = sb("msk", [128, 4, 512], BF16)
        with self.phase("consts"):
            tmp = self.sb("ctmp", [128, 128])
            em.op('pool', lambda e: e.iota(tmp[:], [[1, 128]], base=0, channel_multiplier=-1,
                                           allow_small_or_imprecise_dtypes=True), w=['ctmp'])
            em.op('dve', lambda e: e.tensor_single_scalar(out=self.ident_bf[:], in_=tmp[:], scalar=0.0, op=ALU.is_equal), r=['ctmp'], w=['idb'])
            em.op('dve', lambda e: e.tensor_single_scalar(out=self.ident_f[:], in_=tmp[:], scalar=0.0, op=ALU.is_equal), r=['ctmp'], w=['idf'])
            em.op('pool', lambda e: e.memset(self.ones_f[:], 1.0), w=['ones'])
            em.op('pool', lambda e: e.memset(self.sel65[:], 0.0), w=['sel'])
            em.op('pool', lambda e: e.memset(self.sel65[64:65, :], 1.0), r=['sel'], w=['sel'])
            qrow = self.sb("qrow", [128, 512])
            em.op('pool', lambda e: e.iota(qrow[:], [[1, 512]], base=0, channel_multiplier=0,
                                           allow_small_or_imprecise_dtypes=True), w=['qrow'])
            for o in range(4):
                em.op('dve', lambda e, o=o: e.tensor_single_scalar(out=self.msk[0:64, o, :], in_=qrow[0:64, :], scalar=float(128 * o), op=ALU.is_ge), r=['qrow'], w=['msk'])
                em.op('dve', lambda e, o=o: e.tensor_single_scalar(out=self.msk[64:128, o, :], in_=qrow[64:128, :], scalar=float(128 * o + 64), op=ALU.is_ge), r=['qrow'], w=['msk'])
            em.dma('sp', self.cb[:], bass.AP(self.rel_bias.tensor, 15 * 24, [[0, 128], [1, 24]]), w=['cb'])
            em.op('dve', lambda e: e.tensor_scalar(out=self.ncb[:], in0=self.cb[:], scalar1=-1.0, scalar2=None, op0=ALU.mult), r=['cb'], w=['ncb'])
            self.rope_tables()
            self.bias_tiles()

    def rope_tables(self):
        em, S, NB = self.em, self.S, self.NB
        posi = self.sb("posi", [128, NB], I32)
        posf = self.sb("posf", [128, NB])
        frow = self.sb("frow", [128, 128])
        sgn = self.sb("sgn", [128, 128])
        ang = self.sb("ang", [128, 128])
        tr = self.sb("trg", [128, 128])
        kf = self.sb("kf", [128, 128])
        ki = self.sb("ki", [128, 128], I32)
        cosT = self.sb("cosT", [128, S])
        sinT = self.sb("sinT", [128, S])
        em.dma('sp', posi[:], bass.AP(self.pos.tensor, 0, [[1, 128], [128, NB]]), w=['posi'], allow_slow_non_contiguous=True)
        em.op('dve', lambda e: e.tensor_copy(out=posf[:], in_=posi[:]), r=['posi'], w=['posf'])
        for j in range(16):
            fr = float(np.float32(10000.0) ** np.float32(-j / 16.0))
            em.op('pool', lambda e, j=j, fr=fr: e.memset(fap(frow[:, j:j + 1], [[16, 8], [1, 1]]), fr), w=['frow'])
        em.op('pool', lambda e: e.memset(sgn[:], 1.0), w=['sgn'])
        em.op('pool', lambda e: e.memset(fap(sgn[:, 0:1], [[32, 4], [1, 16]]), -1.0), r=['sgn'], w=['sgn'])
        TWO_PI = 2.0 * math.pi
        for tb in range(NB):
            for which, tab in ((0, sinT), (1, cosT)):
                em.op('dve', lambda e, tb=tb: e.tensor_scalar(out=ang[:], in0=frow[:], scalar1=posf[:, tb:tb + 1], scalar2=None, op0=ALU.mult), r=['frow', 'posf'], w=['ang'])
                if which == 1:
                    em.op('dve', lambda e: e.tensor_scalar(out=ang[:], in0=ang[:], scalar1=math.pi / 2, scalar2=None, op0=ALU.add), r=['ang'], w=['ang'])
                em.op('dve', lambda e: e.tensor_scalar(out=kf[:], in0=ang[:], scalar1=1.0 / TWO_PI, scalar2=None, op0=ALU.mult), r=['ang'], w=['kf'])
                em.op('dve', lambda e: e.tensor_copy(out=ki[:], in_=kf[:]), r=['kf'], w=['ki'])
                em.op('dve', lambda e: e.tensor_copy(out=kf[:], in_=ki[:]), r=['ki'], w=['kf'])
                em.op('dve', lambda e: e.scalar_tensor_tensor(out=ang[:], in0=kf[:], scalar=-TWO_PI, in1=ang[:], op0=ALU.mult, op1=ALU.add), r=['kf', 'ang'], w=['ang'])
                em.op('dve', lambda e: e.tensor_scalar(out=kf[:], in0=ang[:], scalar1=math.pi, scalar2=-TWO_PI, op0=ALU.is_gt, op1=ALU.mult), r=['ang'], w=['kf'])
                em.op('dve', lambda e: e.tensor_tensor(out=ang[:], in0=ang[:], in1=kf[:], op=ALU.add), r=['kf', 'ang'], w=['ang'])
                em.op('dve', lambda e: e.tensor_scalar(out=kf[:], in0=ang[:], scalar1=-math.pi, scalar2=TWO_PI, op0=ALU.is_lt, op1=ALU.mult), r=['ang'], w=['kf'])
                em.op('dve', lambda e: e.tensor_tensor(out=ang[:], in0=ang[:], in1=kf[:], op=ALU.add), r=['kf', 'ang'], w=['ang'])
                em.op('dve', lambda e: e.tensor_scalar(out=ang[:], in0=ang[:], scalar1=-3.14159, scalar2=3.14159, op0=ALU.max, op1=ALU.min), r=['ang'], w=['ang'])
                em.op('act', lambda e: e.activation(out=tr[:], in_=ang[:], func=ACT.Sin), r=['ang'], w=['trg'])
                if which == 0:
                    em.op('dve', lambda e: e.tensor_tensor(out=tr[:], in0=tr[:], in1=sgn[:], op=ALU.mult), r=['trg', 'sgn'], w=['trg'])
                pi = self.nps()
                self.mm(pi, 128, tr[:], self.ident_f[:], True, True, ['trg', 'idf'], cols=(0, 128))
                em.op('act', lambda e, pi=pi, tab=tab, tb=tb: e.copy(out=tab[:, tb * 128:(tb + 1) * 128], in_=self.ps[pi][:, 0:128]), r=['ps%d' % pi], w=['ropetab'])
        em.dma('sp', self.COS, cosT[:], r=['ropetab'], w=['COS'])
        em.dma('sp', self.SIN, sinT[:], r=['ropetab'], w=['SIN'])
        for r0 in (32, 96):
            em.op('pool', lambda e, r0=r0: e.memset(cosT[r0:r0 + 32, :], 1.0), r=['ropetab', 'COS'], w=['ropetab'])
            em.op('pool', lambda e, r0=r0: e.memset(sinT[r0:r0 + 32, :], 0.0), r=['ropetab', 'SIN'], w=['ropetab'])
        em.dma('sp', self.COSI, cosT[:], r=['ropetab'], w=['COSI'])
        em.dma('sp', self.SINI, sinT[:], r=['ropetab'], w=['SINI'])

    def bias_tiles(self):
        em = self.em
        rel = self.sb("relrow", [32, RREL])
        bk = self.sb("bk", [32, RREL])
        oh = self.sb("oh", [32, RREL])
        pidx = self.sb("pidx", [32, 1])
        tab = self.sb("tab", [32, 24])
        fv = self.sb("fv", [24, RREL])
        em.dma('sp', tab[:], self.rel_bias, w=['tab'])
        em.op('pool', lambda e: e.iota(rel[:], [[1, RREL]], base=-REL0, channel_multiplier=0, allow_small_or_imprecise_dtypes=True), w=['rel'])
        em.op('pool', lambda e: e.iota(pidx[:], [[1, 1]], base=0, channel_multiplier=1, allow_small_or_imprecise_dtypes=True), w=['pidx'])
        em.op('dve', lambda e: e.memset(bk[:], 15.0), w=['bk'])
        steps = [(t, -1.0) for t in (-165, -107, -69, -45, -29, -19, -12, -7, -6, -5, -4, -3, -2, -1, 0)]
        steps += [(1, 17.0)] + [(t, 1.0) for t in (2, 3, 4, 5, 6, 7, 8, 13, 20, 30, 46, 70, 108, 166)]
        for th, dlt in steps:
            if dlt in (1.0, -1.0):
                op1 = ALU.add if dlt > 0 else ALU.subtract
            em.op('dve', lambda e, th=th, dlt=dlt: e.tensor_scalar(out=oh[:], in0=rel[:], scalar1=float(th), scalar2=float(dlt), op0=ALU.is_ge, op1=ALU.mult), r=['rel'], w=['oh'])
            em.op('dve', lambda e: e.tensor_tensor(out=bk[:], in0=bk[:], in1=oh[:], op=ALU.add), r=['oh', 'bk'], w=['bk'])
        em.op('dve', lambda e: e.tensor_scalar(out=oh[:], in0=bk[:], scalar1=pidx[:, 0:1], scalar2=None, op0=ALU.is_equal), r=['bk', 'pidx'], w=['oh'])
        for c0 in range(0, RREL, 512):
            n = min(512, RREL - c0)
            pi = self.nps()
            self.mm(pi, 24, tab[:], oh[:, c0:c0 + n], True, True, ['tab', 'oh'], cols=(0, n))
            em.op('act', lambda e, pi=pi, c0=c0, n=n: e.copy(out=fv[:, c0:c0 + n], in_=self.ps[pi][0:24, 0:n]), r=['ps%d' % pi], w=['fv'])
        em.dma('sp', self.fvec, fv[:], r=['fv'], w=['fvec'])
        antid = self.sb("antid", [128, 128])
        em.op('pool', lambda e: e.iota(antid[:], [[1, 128]], base=-127, channel_multiplier=1, allow_small_or_imprecise_dtypes=True), w=['antid'])
        em.op('dve', lambda e: e.tensor_single_scalar(out=antid[:], in_=antid[:], scalar=0.0, op=ALU.is_equal), r=['antid'], w=['antid'])
        tt = [self.sb("toep%d" % i, [128, 4, 128]) for i in range(2)]
        tx = [self.sb("toepx%d" % i, [128, 512]) for i in range(2)]
        te = [self.sb("toepb%d" % i, [128, 512], BF16) for i in range(2)]
        n = 0
        for h in range(24):
            for oi in range(6):
                o = oi - 2
                b = n % 2
                n += 1
                src = bass.AP(self.fvec.tensor, h * RREL + 128 * o + REL0 - 127, [[1, 128], [-128, 4], [1, 128]])
                em.dma('sp', tt[b][:], src, r=['fvec'], w=['toep%d' % b])
                pi = self.nps()
                for j in range(4):
                    self.mm(pi, 128, tt[b][:, j, :], antid[:], True, True, ['toep%d' % b, 'antid'], cols=(j * 128, (j + 1) * 128), inc=(j == 3))
                if h < 8 and o >= 0:
                    em.op('act', lambda e, b=b, h=h, pi=pi: e.activation(out=tx[b][:], in_=self.ps[pi][:], func=ACT.Exp, bias=self.ncb[:, h:h + 1], scale=1.0), r=['ps%d' % pi, 'ncb'], w=['toepx%d' % b])
                    em.op('dve', lambda e, b=b, o=o: e.tensor_tensor(out=te[b][:], in0=tx[b][:], in1=self.msk[:, o, :], op=ALU.mult), r=['toepx%d' % b, 'msk'], w=['toepb%d' % b])
                else:
                    em.op('act', lambda e, b=b, h=h, pi=pi: e.activation(out=te[b][:], in_=self.ps[pi][:], func=ACT.Exp, bias=self.ncb[:, h:h + 1], scale=1.0), r=['ps%d' % pi, 'ncb'], w=['toepb%d' % b])
                em.dma('sp', self.EB[h, oi], te[b][:], r=['toepb%d' % b], w=['EB'])
        self.dbg_dump('fvec', self.fvec)

    def rms_fm(self, src, C, F, gcol, dst, tag):
        em = self.em
        sq = self.t_sq
        pi = self.nps()
        for c in range(C):
            em.op('act', lambda e, c=c: e.activation(out=sq[:, c, :], in_=src[:, c, :], func=ACT.Square), r=[tag], w=['sq%d' % c])
            self.mm(pi, 128, self.ones_f[:], sq[:, c, :], c == 0, c == C - 1, ['ones', 'sq%d' % c])
        rs = self.t_rs
        em.op('act', lambda e: e.activation(out=rs[:], in_=self.ps[pi][:], func=ACT.Sqrt, bias=self.eps_rms[:, 0:1], scale=1.0 / F), r=['ps%d' % pi, 'eps'], w=['rs'])
        em.op('dve', lambda e: e.reciprocal(out=rs[:], in_=rs[:]), r=['rs'], w=['rs'])
        for c in range(C):
            em.op('dve', lambda e, c=c: e.scalar_tensor_tensor(out=dst[:, c, :], in0=src[:, c, :], scalar=gcol[:, c:c + 1], in1=rs[:], op0=ALU.mult, op1=ALU.mult), r=[tag, 'rs', 'gcols'], w=[tag + 'n'])

    def ln_fm(self, y, gcol, bcol, outf, tag, otag, outb=None):
        em = self.em
        sq = self.t_sq
        p1, p2 = self.nps(), self.nps()
        for c in range(8):
            self.mm(p1, 128, self.ones_f[:], y[:, c, :], c == 0, c == 7, [tag])
        for c in range(8):
            em.op('act', lambda e, c=c: e.activation(out=sq[:, c, :], in_=y[:, c, :], func=ACT.Square), r=[tag], w=['sq%d' % c])
            self.mm(p2, 128, self.ones_f[:], sq[:, c, :], c == 0, c == 7, ['ones', 'sq%d' % c])
        mean, rs, msq = self.t_mean, self.t_rs, self.t_msq
        em.op('act', lambda e: e.activation(out=mean[:], in_=self.ps[p1][:], func=ACT.Copy, scale=1.0 / D), r=['ps%d' % p1], w=['mean'])
        em.op('dve', lambda e: e.tensor_tensor(out=msq[:], in0=mean[:], in1=mean[:], op=ALU.mult), r=['mean'], w=['msq'])
        em.op('dve', lambda e: e.scalar_tensor_tensor(out=rs[:], in0=self.ps[p2][:], scalar=1.0 / D, in1=msq[:], op0=ALU.mult, op1=ALU.subtract), r=['ps%d' % p2, 'msq'], w=['rs'])
        em.op('act', lambda e: e.activation(out=rs[:], in_=rs[:], func=ACT.Sqrt, bias=self.eps_ln[:, 0:1], scale=1.0), r=['rs', 'eps'], w=['rs'])
        em.op('dve', lambda e: e.reciprocal(out=rs[:], in_=rs[:]), r=['rs'], w=['rs'])
        for c in range(8):
            em.op('dve', lambda e, c=c: e.tensor_tensor(out=y[:, c, :], in0=y[:, c, :], in1=mean[:], op=ALU.subtract), r=[tag, 'mean'], w=[tag])
            em.op('dve', lambda e, c=c: e.tensor_tensor(out=y[:, c, :], in0=y[:, c, :], in1=rs[:], op=ALU.mult), r=[tag, 'rs'], w=[tag])
            em.op('act', lambda e, c=c: e.activation(out=outf[:, c, :], in_=y[:, c, :], func=ACT.Identity, bias=bcol[:, c:c + 1], scale=gcol[:, c:c + 1]), r=[tag, 'gcols'], w=[otag])
            if outb is not None:
                em.op('pool', lambda e, c=c: e.tensor_copy(out=outb[:, c, :], in_=outf[:, c, :]), r=[otag], w=[otag + 'b'])

    def norm_tmps(self):
        self.t_sq = self.sb("sq", [128, 8, 512])
        self.t_rs = self.sb("rs", [128, 512])
        self.t_mean = self.sb("mean", [128, 512])
        self.t_msq = self.sb("msq", [128, 512])
        self.eps_rms = self.sb("epsr", [128, 1])
        self.eps_ln = self.sb("epsl", [128, 1])
        self.em.op('pool', lambda e: e.memset(self.eps_rms[:], RMS_EPS), w=['eps'])
        self.em.op('pool', lambda e: e.memset(self.eps_ln[:], LN_EPS), r=['eps'], w=['eps'])

    def wload(self, dst, src2d, K, c0, c1, tag, dcol=0):
        for k in range(K):
            self.em.dma('pool', dst[:, k, dcol:dcol + (c1 - c0)], src2d[k * 128:(k + 1) * 128, c0:c1], w=[tag])

    def hsrc(self, l):
        return self.xT if l == 0 else self.hT

    def even_proj(self, l):
        em, S = self.em, self.S
        j = l // 2
        with self.phase("even_proj"):
            self.norm_tmps()
            win = self.sb("win", [128, 8, 2336], BF16)
            wkrs = self.sb("wkrs", [128, 8, 32], BF16)
            wqn = self.sb("wqn", [128, 4, 512], BF16)
            wqr = self.sb("wqr", [128, 4, 256], BF16)
            wqs = self.sb("wqs", [128, 4, 256], BF16)
            wkn = self.sb("wkn", [128, 2, 512], BF16)
            wv = self.sb("wv", [128, 2, 512], BF16)
            gq = self.sb("gq", [128, 4])
            gkv = self.sb("gkv", [128, 2])
            W = self.ev_w_in[j]
            self.wload(win, W, 8, 0, 2336, 'win')
            self.wload(wkrs, W, 8, 768 + 16, 768 + 32, 'wkrs', 0)
            self.wload(wkrs, W, 8, 768, 768 + 16, 'wkrs', 16)
            UQ, UKV = self.ev_w_uq[j], self.ev_w_ukv[j]
            for h in range(8):
                self.wload(wqn, UQ, 4, h * 96, h * 96 + 64, 'wqn', h * 64)
                self.wload(wqr, UQ, 4, h * 96 + 64, h * 96 + 96, 'wqr', h * 32)
                self.wload(wqs, UQ, 4, h * 96 + 80, h * 96 + 96, 'wqs', h * 32)
                self.wload(wqs, UQ, 4, h * 96 + 64, h * 96 + 80, 'wqs', h * 32 + 16)
                self.wload(wkn, UKV, 2, h * 128, h * 128 + 64, 'wkn', h * 64)
                self.wload(wv, UKV, 2, h * 128 + 64, h * 128 + 128, 'wv', h * 64)
            em.dma('sp', gq[:], self.ev_q_norm[j], w=['gcols'])
            em.dma('sp', gkv[:], self.ev_kv_norm[j], w=['gcols'])
            hsrc = self.hsrc(l).rearrange("(k p) s -> p k s", p=128)
            hb = [self.sb("hb%d" % i, [128, 8, 512], BF16) for i in range(2)]
            cq = self.sb("cq", [128, 4, 512])
            ckv = self.sb("ckv", [128, 2, 512])
            cqn = self.sb("cqn", [128, 4, 512], BF16)
            ckvn = self.sb("ckvn", [128, 2, 512], BF16)
            t1 = self.sb("t1", [128, 512])
            t2 = self.sb("t2", [128, 512])
            ob = [self.sb("ob%d" % i, [128, 512], BF16) for i in range(4)]
            obn = [0]

            def evac_store(pi, rows, dst, eng=None):
                i = obn[0] % 4
                obn[0] += 1
                eng = eng or ('act' if i % 2 == 0 else 'dve')
                if eng == 'act':
                    em.op('act', lambda e: e.copy(out=ob[i][0:rows, :], in_=self.ps[pi][0:rows, :]), r=['ps%d' % pi], w=['ob%d' % i])
                else:
                    em.op('dve', lambda e: e.tensor_copy(out=ob[i][0:rows, :], in_=self.ps[pi][0:rows, :]), r=['ps%d' % pi], w=['ob%d' % i])
                em.dma('sp', dst, ob[i][0:rows, :], r=['ob%d' % i], w=['scr'])

            cs = [self.sb("cs%d" % i, [128, 512]) for i in range(2)]
            sn = [self.sb("sn%d" % i, [128, 512]) for i in range(2)]

            def rope_store(pa, pb, rows, g, dst):
                cg, sg_ = cs[g % 2], sn[g % 2]
                em.op('dve', lambda e: e.tensor_tensor(out=t1[0:rows, :], in0=self.ps[pa][0:rows, :], in1=cg[0:rows, :], op=ALU.mult), r=['ps%d' % pa, 'cs%d' % (g % 2)], w=['t1'])
                em.op('dve', lambda e: e.tensor_tensor(out=t2[0:rows, :], in0=self.ps[pb][0:rows, :], in1=sg_[0:rows, :], op=ALU.mult), r=['ps%d' % pb, 'sn%d' % (g % 2)], w=['t2'])
                i = obn[0] % 4
                obn[0] += 1
                em.op('pool', lambda e: e.tensor_tensor(out=ob[i][0:rows, :], in0=t1[0:rows, :], in1=t2[0:rows, :], op=ALU.add), r=['t1', 't2'], w=['ob%d' % i])
                em.dma('sp', dst, ob[i][0:rows, :], r=['ob%d' % i], w=['scr'])

            for g in range(self.NG):
                gs = slice(g * 512, (g + 1) * 512)
                H = hb[g % 2]
                ht = 'hb%d' % (g % 2)
                em.dma('pool', H[:], hsrc[:, :, gs], w=[ht])
                em.dma('sp', cs[g % 2][:], self.COS[:, gs], w=['cs%d' % (g % 2)])
                em.dma('sp', sn[g % 2][:], self.SIN[:, gs], w=['sn%d' % (g % 2)])
                for c in range(4):
                    pi = self.nps()
                    for k in range(8):
                        self.mm(pi, 128, win[:, k, c * 128:(c + 1) * 128], H[:, k, :], k == 0, k == 7, ['win', ht])
                    em.op('act', lambda e, c=c, pi=pi: e.copy(out=cq[:, c, :], in_=self.ps[pi][:]), r=['ps%d' % pi], w=['cq'])
                for c in range(2):
                    pi = self.nps()
                    for k in range(8):
                        self.mm(pi, 128, win[:, k, 512 + c * 128:512 + (c + 1) * 128], H[:, k, :], k == 0, k == 7, ['win', ht])
                    em.op('dve', lambda e, c=c, pi=pi: e.tensor_copy(out=ckv[:, c, :], in_=self.ps[pi][:]), r=['ps%d' % pi], w=['ckv'])
                self.rms_fm(cq, 4, 512, gq, cqn, 'cq')
                self.rms_fm(ckv, 2, 256, gkv, ckvn, 'ckv')
                pa, pb = self.nps(), self.nps()
                for k in range(8):
                    self.mm(pa, 32, win[:, k, 768:800], H[:, k, :], k == 0, k == 7, ['win', ht])
                for k in range(8):
                    self.mm(pb, 32, wkrs[:, k, :], H[:, k, :], k == 0, k == 7, ['wkrs', ht])
                rope_store(pa, pb, 32, g, self.KR[:, gs])
                for c in range(4):
                    for base, dst in ((800, self.QD), (1312, self.KD)):
                        pi = self.nps()
                        for k in range(8):
                            self.mm(pi, 128, win[:, k, base + c * 128:base + (c + 1) * 128], H[:, k, :], k == 0, k == 7, ['win', ht])
                        evac_store(pi, 128, dst[c * 128:(c + 1) * 128, gs])
                for tb in range(4):
                    pi = self.nps()
                    for k in range(8):
                        self.mm(pi, 128, H[:, k, tb * 128:(tb + 1) * 128], win[:, k, 1824:2336], k == 0, k == 7, ['win', ht])
                    evac_store(pi, 128, self.VD[g * 512 + tb * 128:g * 512 + (tb + 1) * 128, :])
                for rc in range(2):
                    pa, pb = self.nps(), self.nps()
                    for k in range(4):
                        self.mm(pa, 128, wqr[:, k, rc * 128:(rc + 1) * 128], cqn[:, k, :], k == 0, k == 3, ['wqr', 'cqn'])
                    for k in range(4):
                        self.mm(pb, 128, wqs[:, k, rc * 128:(rc + 1) * 128], cqn[:, k, :], k == 0, k == 3, ['wqs', 'cqn'])
                    rope_store(pa, pb, 128, g, self.QR[rc * 128:(rc + 1) * 128, gs])
                for c in range(4):
                    pi = self.nps()
                    for k in range(4):
                        self.mm(pi, 128, wqn[:, k, c * 128:(c + 1) * 128], cqn[:, k, :], k == 0, k == 3, ['wqn', 'cqn'])
                    evac_store(pi, 128, self.QN[c * 128:(c + 1) * 128, gs])
                    pi = self.nps()
                    for k in range(2):
                        self.mm(pi, 128, wkn[:, k, c * 128:(c + 1) * 128], ckvn[:, k, :], k == 0, k == 1, ['wkn', 'ckvn'])
                    evac_store(pi, 128, self.KN[c * 128:(c + 1) * 128, gs])
                for tb in range(4):
                    pi = self.nps()
                    for k in range(2):
                        self.mm(pi, 128, ckvn[:, k, tb * 128:(tb + 1) * 128], wv[:, k, :], k == 0, k == 1, ['wv', 'ckvn'])
                    evac_store(pi, 128, self.VA[g * 512 + tb * 128:g * 512 + (tb + 1) * 128, :])
        self.dbg_dump('QR', self.QR)
        self.dbg_dump('QN', self.QN)
        self.dbg_dump('KR', self.KR)
        self.dbg_dump('VA', self.VA)
        self.dbg_dump('QD', self.QD)

    def attn_begin(self, p, LA=2):
        p['q'] = []
        p['n'] = 4 * p['g'] + 4
        for kb in range(min(LA, p['n'])):
            p['q'].append(self.attn_qk(p, kb))

    def attn_qk(self, p, kb):
        pi = self.nps()
        while pi in self.resv:
            pi = self.nps()
        rows = p['rows']
        q0 = p['g'] * 512 if p.get('qoff') is None else p['qoff']
        self.mm(pi, 128, p['kt'][0:rows, kb * 128:(kb + 1) * 128], p['qt'][0:rows, q0:q0 + 512], True, True, p['tags'])
        return pi

    def attn_body(self, p, ptb, ptn, LA=2):
        em = self.em
        n = p['n']
        scale, bias_col = p['scale'], p['bias']
        for kb in range(n):
            pi = p['q'][kb]
            i = ptn[0] % len(ptb)
            ptn[0] += 1
            P = ptb[i]
            pt = 'pt%d' % i
            if bias_col is None:
                em.op('act', lambda e, pi=pi, P=P: e.activation(out=P[:], in_=self.ps[pi][:], func=ACT.Exp, scale=scale), r=['ps%d' % pi], w=[pt])
            else:
                em.op('act', lambda e, pi=pi, P=P: e.activation(out=P[:], in_=self.ps[pi][:], func=ACT.Exp, bias=bias_col, scale=scale), r=['ps%d' % pi, 'cb'], w=[pt])
            p['mult'](kb, P, pt)
            if kb + LA < n:
                p['q'].append(self.attn_qk(p, kb + LA))
            self.mm(p['pso'], 65, p['vt'][:, kb, :], P[:], kb == 0, kb == n - 1, [pt] + p['tags'])

    def attn_run(self, passes, ptb, ptn):
        if not passes:
            return
        self.attn_begin(passes[0])
        for i, p in enumerate(passes):
            if 'pre' in p:
                pass
            self.attn_body(p, ptb, ptn)
            if i + 1 < len(passes):
                if 'load' in passes[i + 1]:
                    passes[i + 1]['load']()
                self.attn_begin(passes[i + 1])
            p['finish']()

    def attn_norm(self, pso, osb, rc, tag):
        em = self.em
        em.op('act', lambda e: e.copy(out=osb[0:65, :], in_=self.ps[pso][0:65, :]), r=['ps%d' % pso], w=[tag])
        pi = self.nps()
        while pi in self.resv:
            pi = self.nps()
        self.mm(pi, 64, self.sel65[0:65, :], osb[0:65, :], True, True, ['sel', tag])
        em.op('dve', lambda e: e.reciprocal(out=rc[0:64, :], in_=self.ps[pi][0:64, :]), r=['ps%d' % pi], w=['rc'])
        em.op('dve', lambda e: e.tensor_tensor(out=osb[0:64, :], in0=osb[0:64, :], in1=rc[0:64, :], op=ALU.mult), r=['rc', tag], w=[tag])

    def attn_even(self, l):
        em, S, NB = self.em, self.S, self.NB
        j = l // 2
        lam_init = 0.8 - 0.6 * math.exp(-0.3 * l)
        with self.phase("attn_even"):
            ptb = [self.sb("pt%d" % i, [128, 512], BF16) for i in range(4)]
            ptn = [0]
            self.resv = {6, 7}
            osb = [self.sb("osb%d" % i, [128, 512]) for i in range(2)]
            rc = self.sb("rc", [128, 512])
            o16 = [self.sb("o16%d" % i, [64, 512], BF16) for i in range(2)]
            qt = [self.sb("qt%d" % i, [96, S], BF16) for i in range(2)]
            kt = [self.sb("kt%d" % i, [96, S], BF16) for i in range(2)]
            vt = [self.sb("vt%d" % i, [128, NB, 65], BF16) for i in range(2)]
            for i in range(2):
                em.op('pool', lambda e, i=i: e.memset(vt[i][:, :, 64:65], 1.0), w=['vt%d' % i])

            def mult_a(g):
                def f(kb, P, pt):
                    if kb >= 4 * g:
                        em.op('dve', lambda e: e.tensor_tensor(out=P[:], in0=P[:], in1=self.msk[:, kb - 4 * g, :], op=ALU.mult), r=[pt, 'msk'], w=[pt])
                return f
            passes = []
            n = 0
            for h in range(8):
                b = h % 2
                tg = ['qt%d' % b, 'kt%d' % b, 'vt%d' % b]

                def load(h=h, b=b, tg=tg):
                    em.dma('sp', qt[b][0:32, :], self.QR[h * 32:(h + 1) * 32, :], w=[tg[0]])
                    em.dma('sp', qt[b][32:96, :], self.QN[h * 64:(h + 1) * 64, :], w=[tg[0]])
                    em.dma('sp', kt[b][0:32, :], self.KR[:, :], w=[tg[1]])
                    em.dma('sp', kt[b][32:96, :], self.KN[h * 64:(h + 1) * 64, :], w=[tg[1]])
                    em.dma('sp', vt[b][:, :, 0:64], self.VA[:, h * 64:(h + 1) * 64].rearrange("(nb p) d -> p nb d", p=128), w=[tg[2]])
                for g in range(self.NG):
                    pso = 6 + n % 2
                    ob = n % 2
                    n += 1

                    def finish(h=h, g=g, pso=pso, ob=ob):
                        self.attn_norm(pso, osb[ob], rc, 'osb%d' % ob)
                        em.op('pool', lambda e: e.tensor_copy(out=o16[ob][:], in_=osb[ob][0:64, :]), r=['osb%d' % ob], w=['o16%d' % ob])
                        em.dma('sp', self.OT[h * 64:(h + 1) * 64, g * 512:(g + 1) * 512], o16[ob][:], r=['o16%d' % ob], w=['OT'])
                    p = dict(qt=qt[b], kt=kt[b], vt=vt[b], rows=96, g=g, scale=A_SCALE, bias=None, mult=mult_a(g), tags=tg, pso=pso, finish=finish)
                    if g == 0:
                        p['load'] = load
                    passes.append(p)
            passes[0]['load']()
            self.attn_run(passes, ptb, ptn)
        self.dbg_dump('OTA', self.OT)
        with self.phase("attn_even"):
            ptb = [self.sb("pt%d" % i, [128, 512], BF16) for i in range(4)]
            ptn = [0]
            self.resv = {4, 5, 6, 7}
            osb = [self.sb("osb%d" % i, [128, 512]) for i in range(2)]
            rc = self.sb("rc", [128, 512])
            od = self.sb("od", [64, 512])
            sq = self.sb("sq", [64, 512])
            o16 = [self.sb("o16%d" % i, [64, 512], BF16) for i in range(2)]
            qt = [self.sb("qt%d" % i, [32, S], BF16) for i in range(4)]
            kt = [self.sb("kt%d" % i, [32, S], BF16) for i in range(4)]
            vt = [self.sb("vt%d" % i, [128, NB, 65], BF16) for i in range(2)]
            eb = [self.sb("eb%d" % i, [128, 6, 512], BF16) for i in range(2)]
            for i in range(2):
                em.op('pool', lambda e, i=i: e.memset(vt[i][:, :, 64:65], 1.0), w=['vt%d' % i])
            lam4 = self.sb("lam4", [32, 4])
            lamp = self.sb("lamp", [32, 2])
            lamc = self.sb("lamc", [64, 4])
            sg = self.sb("sg", [64, 1])
            epsr = self.sb("epsr2", [64, 1])
            em.op('pool', lambda e: e.memset(epsr[:], RMS_EPS), w=['epsr2'])
            em.dma('sp', lam4[:], self.ev_lam[j], w=['lam4'])
            em.dma('sp', sg[:], self.ev_subln[j], w=['sg'])
            em.op('dve', lambda e: e.tensor_tensor(out=lamp[:, 0:1], in0=lam4[:, 0:1], in1=lam4[:, 1:2], op=ALU.mult), r=['lam4'], w=['lamp'])
            em.op('dve', lambda e: e.tensor_tensor(out=lamp[:, 1:2], in0=lam4[:, 2:3], in1=lam4[:, 3:4], op=ALU.mult), r=['lam4', 'lamp'], w=['lamp'])
            pi = 0
            self.mm(pi, 64, self.ones_f[0:32, 0:64], lamp[:, 0:2], True, True, ['ones', 'lamp'], cols=(0, 2))
            em.op('act', lambda e: e.activation(out=lamc[:, 0:2], in_=self.ps[pi][0:64, 0:2], func=ACT.Exp), r=['ps%d' % pi], w=['lamc'])
            em.op('dve', lambda e: e.tensor_tensor(out=lamc[:, 2:3], in0=lamc[:, 1:2], in1=lamc[:, 0:1], op=ALU.subtract), r=['lamc'], w=['lamc'])
            em.op('dve', lambda e: e.tensor_scalar(out=lamc[:, 3:4], in0=lamc[:, 2:3], scalar1=-lam_init, scalar2=None, op0=ALU.add), r=['lamc'], w=['lamc'])
            em.op('dve', lambda e: e.tensor_scalar(out=sg[:], in0=sg[:], scalar1=1.0 - lam_init, scalar2=None, op0=ALU.mult), r=['sg'], w=['sg'])

            def mult_b(g, E, et):
                def f(kb, P, pt):
                    if kb >= 4 * g - 2:
                        em.op('dve', lambda e: e.tensor_tensor(out=P[:], in0=P[:], in1=E[:, kb - 4 * g + 2, :], op=ALU.mult), r=[pt, et], w=[pt])
                return f
            passes = []
            n = 0
            for h in range(8):
                b = h % 2

                def load(h=h, b=b):
                    em.dma('sp', eb[b][:], self.EB[h].rearrange("o p q -> p o q"), w=['eb%d' % b])
                    em.dma('sp', vt[b][:, :, 0:64], self.VD[:, h * 64:(h + 1) * 64].rearrange("(nb p) d -> p nb d", p=128), w=['vt%d' % b])
                    for m in range(2):
                        i = b * 2 + m
                        r0 = (h * 2 + m) * 32
                        em.dma('sp', qt[i][:], self.QD[r0:r0 + 32, :], w=['qt%d' % i])
                        em.dma('sp', kt[i][:], self.KD[r0:r0 + 32, :], w=['kt%d' % i])
                for g in range(self.NG):
                    ob = n % 2
                    for m in range(2):
                        i = b * 2 + m
                        pso = 4 + m + 2 * (n % 2)
                        tg = ['qt%d' % i, 'kt%d' % i, 'vt%d' % b]

                        def finish(h=h, g=g, m=m, pso=pso, ob=ob):
                            self.attn_norm(pso, osb[m], rc, 'osb%d' % m)
                            if m == 0:
                                return
                            em.op('dve', lambda e: e.scalar_tensor_tensor(out=od[:], in0=osb[1][0:64, :], scalar=lamc[:, 3:4], in1=osb[0][0:64, :], op0=ALU.mult, op1=ALU.add), r=['osb0', 'osb1', 'lamc'], w=['od'])
                            em.op('act', lambda e: e.activation(out=sq[:], in_=od[:], func=ACT.Square), r=['od'], w=['sqd'])
                            pi = self.nps()
                            while pi in self.resv:
                                pi = self.nps()
                            self.mm(pi, 64, self.ones_f[0:64, 0:64], sq[:], True, True, ['ones', 'sqd'])
                            em.op('act', lambda e: e.activation(out=sq[:], in_=self.ps[pi][0:64, :], func=ACT.Sqrt, bias=epsr[:, 0:1], scale=1.0 / 64), r=['ps%d' % pi, 'epsr2'], w=['sqd'])
                            em.op('dve', lambda e: e.reciprocal(out=sq[:], in_=sq[:]), r=['sqd'], w=['sqd'])
                            em.op('dve', lambda e: e.scalar_tensor_tensor(out=o16[ob][:], in0=od[:], scalar=sg[:, 0:1], in1=sq[:], op0=ALU.mult, op1=ALU.mult), r=['od', 'sqd', 'sg'], w=['o16%d' % ob])
                            em.dma('sp', self.OT[512 + h * 64:512 + (h + 1) * 64, g * 512:(g + 1) * 512], o16[ob][:], r=['o16%d' % ob], w=['OT'])
                        p = dict(qt=qt[i], kt=kt[i], vt=vt[b], rows=32, g=g, scale=B_SCALE, bias=self.cb[:, h:h + 1], mult=mult_b(g, eb[b], 'eb%d' % b), tags=tg, pso=pso, finish=finish)
                        if g == 0 and m == 0:
                            p['load'] = load
                        passes.append(p)
                    n += 1
            passes[0]['load']()
            self.attn_run(passes, ptb, ptn)
        self.resv = set()
        self.dbg_dump('OT', self.OT)

    def outproj_ln1(self, l):
        em = self.em
        j = l // 2
        WO = self.ev_w_o[j] if l % 2 == 0 else self.od_w_o[j]
        with self.phase("outproj_ln1"):
            self.norm_tmps()
            wo = self.sb("wo", [128, 8, D], BF16)
            self.wload(wo, WO, 8, 0, D, 'wo')
            g1 = self.sb("g1", [128, 8])
            b1 = self.sb("b1", [128, 8])
            em.dma('sp', g1[:], self.ln["ln1_g"][l], w=['gcols'])
            em.dma('sp', b1[:], self.ln["ln1_b"][l], w=['gcols'])
            hsrc = self.hsrc(l).rearrange("(k p) s -> p k s", p=128)
            hdst = self.hT.rearrange("(k p) s -> p k s", p=128)
            otv = self.OT.rearrange("(k p) s -> p k s", p=128)
            ot = [self.sb("ot%d" % i, [128, 8, 512], BF16) for i in range(2)]
            hr = [self.sb("hr%d" % i, [128, 8, 512]) for i in range(2)]
            y = self.sb("y", [128, 8, 512])
            ho = [self.sb("ho%d" % i, [128, 8, 512]) for i in range(2)]
            for g in range(self.NG):
                gs = slice(g * 512, (g + 1) * 512)
                b = g % 2
                em.dma('sp', ot[b][:], otv[:, :, gs], w=['ot%d' % b])
                em.dma('sp', hr[b][:], hsrc[:, :, gs], w=['hr%d' % b])
                for n in range(8):
                    pi = self.nps()
                    for k in range(8):
                        self.mm(pi, 128, wo[:, k, n * 128:(n + 1) * 128], ot[b][:, k, :], k == 0, k == 7, ['wo', 'ot%d' % b])
                    em.op('dve', lambda e, n=n, pi=pi: e.scalar_tensor_tensor(out=y[:, n, :], in0=hr[b][:, n, :], scalar=DN_ALPHA, in1=self.ps[pi][:], op0=ALU.mult, op1=ALU.add), r=['ps%d' % pi, 'hr%d' % b], w=['y'])
                self.ln_fm(y, g1, b1, ho[b], 'y', 'ho%d' % b)
                em.dma('sp', hdst[:, :, gs], ho[b][:], r=['ho%d' % b], w=['hT'])
        self.dbg_dump('h1', self.hT)

    def odd_proj(self, l):
        em, S = self.em, self.S
        j = l // 2
        with self.phase("odd_proj"):
            win = self.sb("win", [128, 8, 2640], BF16)
            wsw = self.sb("wsw", [128, 8, 1088], BF16)
            W = self.od_w_in[j]
            self.wload(win, W, 8, 0, 2640, 'win')
            self.wload(wsw, W, 8, 1536, 2624, 'wsw')
            for hh in range(17):
                c0 = 1536 + hh * 64
                self.wload(wsw, W, 8, c0 + 16, c0 + 32, 'wsw', hh * 64)
                self.wload(wsw, W, 8, c0, c0 + 16, 'wsw', hh * 64 + 16)
            hsrc = self.hsrc(l).rearrange("(k p) s -> p k s", p=128)
            hb = [self.sb("hb%d" % i, [128, 8, 512], BF16) for i in range(2)]
            cs = [self.sb("cs%d" % i, [128, 512]) for i in range(2)]
            sn = [self.sb("sn%d" % i, [128, 512]) for i in range(2)]
            t1 = self.sb("t1", [128, 512])
            t2 = self.sb("t2", [128, 512])
            ob = [self.sb("ob%d" % i, [128, 512], BF16) for i in range(4)]
            wo_ = [self.sb("wio%d" % i, [128, 16]) for i in range(2)]
            obn = [0]

            def evac_store(pi, rows, dst, ncol=512):
                i = obn[0] % 4
                obn[0] += 1
                if i % 2 == 0:
                    em.op('act', lambda e: e.copy(out=ob[i][0:rows, 0:ncol], in_=self.ps[pi][0:rows, 0:ncol]), r=['ps%d' % pi], w=['ob%d' % i])
                else:
                    em.op('dve', lambda e: e.tensor_copy(out=ob[i][0:rows, 0:ncol], in_=self.ps[pi][0:rows, 0:ncol]), r=['ps%d' % pi], w=['ob%d' % i])
                em.dma('sp', dst, ob[i][0:rows, 0:ncol], r=['ob%d' % i], w=['scr'])

            def rope_store(pa, pb, rows, g, dst):
                cg, sg_ = cs[g % 2], sn[g % 2]
                em.op('dve', lambda e: e.tensor_tensor(out=t1[0:rows, :], in0=self.ps[pa][0:rows, :], in1=cg[0:rows, :], op=ALU.mult), r=['ps%d' % pa, 'cs%d' % (g % 2)], w=['t1'])
                em.op('dve', lambda e: e.tensor_tensor(out=t2[0:rows, :], in0=self.ps[pb][0:rows, :], in1=sg_[0:rows, :], op=ALU.mult), r=['ps%d' % pb, 'sn%d' % (g % 2)], w=['t2'])
                i = obn[0] % 4
                obn[0] += 1
                em.op('pool', lambda e: e.tensor_tensor(out=ob[i][0:rows, :], in0=t1[0:rows, :], in1=t2[0:rows, :], op=ALU.add), r=['t1', 't2'], w=['ob%d' % i])
                em.dma('sp', dst, ob[i][0:rows, :], r=['ob%d' % i], w=['scr'])

            nw = 0
            for g in range(self.NG):
                gs = slice(g * 512, (g + 1) * 512)
                H = hb[g % 2]
                ht = 'hb%d' % (g % 2)
                em.dma('pool', H[:], hsrc[:, :, gs], w=[ht])
                em.dma('sp', cs[g % 2][:], self.COSI[:, gs], w=['cs%d' % (g % 2)])
                em.dma('sp', sn[g % 2][:], self.SINI[:, gs], w=['sn%d' % (g % 2)])
                for c in range(8):
                    pi = self.nps()
                    for k in range(8):
                        self.mm(pi, 128, win[:, k, c * 128:(c + 1) * 128], H[:, k, :], k == 0, k == 7, ['win', ht])
                    evac_store(pi, 128, self.QC[c * 128:(c + 1) * 128, gs])
                for c in range(2):
                    pi = self.nps()
                    for k in range(8):
                        self.mm(pi, 128, win[:, k, 1024 + c * 128:1024 + (c + 1) * 128], H[:, k, :], k == 0, k == 7, ['win', ht])
                    evac_store(pi, 128, self.KC[c * 128:(c + 1) * 128, gs])
                for tb in range(4):
                    pi = self.nps()
                    for k in range(8):
                        self.mm(pi, 128, H[:, k, tb * 128:(tb + 1) * 128], win[:, k, 1280:1536], k == 0, k == 7, ['win', ht], cols=(0, 256))
                    r0 = g * 512 + tb * 128
                    evac_store(pi, 128, self.VC[r0:r0 + 128, :], ncol=256)
                    pi = self.nps()
                    for k in range(8):
                        self.mm(pi, 128, H[:, k, tb * 128:(tb + 1) * 128], win[:, k, 2624:2640], k == 0, k == 7, ['win', ht], cols=(0, 16))
                    wb = nw % 2
                    nw += 1
                    em.op('act', lambda e, pi=pi, wb=wb: e.activation(out=wo_[wb][:], in_=self.ps[pi][:, 0:16], func=ACT.Copy, scale=0.25 * IDX_SCALE), r=['ps%d' % pi], w=['wio%d' % wb])
                    em.dma('sp', self.WI[r0:r0 + 128, :], wo_[wb][:], r=['wio%d' % wb], w=['scr'])
                for c in range(8):
                    pa, pb = self.nps(), self.nps()
                    for k in range(8):
                        self.mm(pa, 128, win[:, k, 1536 + c * 128:1536 + (c + 1) * 128], H[:, k, :], k == 0, k == 7, ['win', ht])
                    for k in range(8):
                        self.mm(pb, 128, wsw[:, k, c * 128:(c + 1) * 128], H[:, k, :], k == 0, k == 7, ['wsw', ht])
                    rope_store(pa, pb, 128, g, self.QI[c * 128:(c + 1) * 128, gs])
                pa, pb = self.nps(), self.nps()
                for k in range(8):
                    self.mm(pa, 64, win[:, k, 2560:2624], H[:, k, :], k == 0, k == 7, ['win', ht])
                for k in range(8):
                    self.mm(pb, 64, wsw[:, k, 1024:1088], H[:, k, :], k == 0, k == 7, ['wsw', ht])
                rope_store(pa, pb, 64, g, self.KI[:, gs])
        self.dbg_dump('QC', self.QC)
        self.dbg_dump('QI', self.QI)
        self.dbg_dump('WI', self.WI)

    def dsa(self, l):
        em, S, NB = self.em, self.S, self.NB
        NIT = 26
        NSEL = float(min(256, S // 4))
        with self.phase("dsa"):
            ki = self.sb("ki", [64, S], BF16)
            kc = [self.sb("kc%d" % i, [64, S], BF16) for i in range(4)]
            vc = [self.sb("vc%d" % i, [128, NB, 65], BF16) for i in range(4)]
            em.dma('sp', ki[:], self.KI, w=['ki'])
            for i in range(4):
                em.dma('sp', kc[i][:], self.KC[i * 64:(i + 1) * 64, :], w=['kc%d' % i])
                em.op('pool', lambda e, i=i: e.memset(vc[i][:, :, 64:65], 1.0), w=['vc%d' % i])
                em.dma('sp', vc[i][:, :, 0:64], self.VC[:, i * 64:(i + 1) * 64].rearrange("(nb p) d -> p nb d", p=128), w=['vc%d' % i])
            qg = self.sb("qg", [64, 16, 512], BF16)
            wig = self.sb("wig", [128, 4, 16])
            acc = self.sb("acc", [128, S])
            mq = self.sb("mq", [128, S], BF16)
            maskT = self.sb("maskT", [128, NB, 512], BF16)
            rl = [self.sb("rl%d" % i, [128, 512]) for i in range(2)]
            ec = [self.sb("ec%d" % i, [128, 6, 512], BF16) for i in range(2)]
            ptb = [self.sb("pt%d" % i, [128, 512], BF16) for i in range(4)]
            ptn = [0]
            osb = [self.sb("osb%d" % i, [128, 512]) for i in range(2)]
            rc = self.sb("rc", [128, 512])
            o16 = [self.sb("o16%d" % i, [64, 512], BF16) for i in range(2)]
            bs = self.sb("bs", [128, 8])
            wt = self.sb("wt", [128, NIT + 1])
            wt2 = self.sb("wt2", [128, NIT + 1])
            pw2 = self.sb("pw2", [128, NIT + 1])
            cntT = self.sb("cntT", [128, NIT])
            for i in range(NIT + 1):
                em.op('pool', lambda e, i=i: e.memset(pw2[:, i:i + 1], 2.0 ** -(i + 1)), w=['pw2'])
            qiv = self.QI.rearrange("(h d) s -> d h s", d=64)
            qcv = self.QC.rearrange("(h d) s -> d h s", d=64)
            nrl = 0
            nec = 0
            no = 0
            for g in range(self.NG):
                gs = slice(g * 512, (g + 1) * 512)
                em.dma('sp', qg[:], qiv[:, :, gs], w=['qg'])
                em.dma('sp', wig[:], self.WI[g * 512:(g + 1) * 512, :].rearrange("(t p) h -> p t h", p=128), w=['wig'])
                em.op('pool', lambda e: e.memset(maskT[:], 0.0), w=['maskT'])
                for qt in range(4):
                    T = 4 * g + qt
                    Lk = 128 * (T + 1)
                    nck = (Lk + 511) // 512
                    for h in range(16):
                        for kcn in range(nck):
                            n = min(512, Lk - kcn * 512)
                            pi = self.nps()
                            self.mm(pi, 128, qg[:, h, qt * 128:(qt + 1) * 128], ki[:, kcn * 512:kcn * 512 + n], True, True, ['qg', 'ki'], cols=(0, n))
                            ri = nrl % 2
                            nrl += 1
                            em.op('act', lambda e, pi=pi, ri=ri, n=n: e.activation(out=rl[ri][:, 0:n], in_=self.ps[pi][:, 0:n], func=ACT.Relu), r=['ps%d' % pi], w=['rl%d' % ri])
                            ksl = slice(kcn * 512, kcn * 512 + n)
                            if h == 0:
                                em.op('dve', lambda e, ri=ri, n=n, ksl=ksl, qt=qt: e.tensor_scalar(out=acc[:, ksl], in0=rl[ri][:, 0:n], scalar1=wig[:, qt, 0:1], scalar2=None, op0=ALU.mult), r=['rl%d' % ri, 'wig'], w=['acc'])
                            else:
                                em.op('dve', lambda e, ri=ri, n=n, ksl=ksl, qt=qt, h=h: e.scalar_tensor_tensor(out=acc[:, ksl], in0=rl[ri][:, 0:n], scalar=wig[:, qt, h:h + 1], in1=acc[:, ksl], op0=ALU.mult, op1=ALU.add), r=['rl%d' % ri, 'wig', 'acc'], w=['acc'])
                    if T >= 2:
                        em.op('dve', lambda e, Lk=Lk: e.tensor_reduce(out=bs[:, 0:1], in_=acc[:, 0:Lk], axis=AX.X, op=ALU.max), r=['acc'], w=['bs'])
                        em.op('dve', lambda e, Lk=Lk: e.tensor_reduce(out=bs[:, 1:2], in_=acc[:, 0:Lk], axis=AX.X, op=ALU.min), r=['acc', 'bs'], w=['bs'])
                    em.op('dve', lambda e, Lk=Lk: e.memset(acc[0:64, Lk - 64:Lk], NEG), r=['acc'], w=['acc'])
                    if T >= 2:
                        em.op('dve', lambda e: e.tensor_tensor(out=bs[:, 2:3], in0=bs[:, 0:1], in1=bs[:, 1:2], op=ALU.subtract), r=['bs'], w=['bs'])
                        em.op('dve', lambda e: e.tensor_scalar(out=wt[:], in0=pw2[:], scalar1=bs[:, 2:3], scalar2=None, op0=ALU.mult), r=['bs', 'pw2'], w=['wt'])
                        em.op('dve', lambda e: e.tensor_scalar(out=wt2[:], in0=wt[:], scalar1=2.0, scalar2=None, op0=ALU.mult), r=['wt'], w=['wt2'])
                        em.op('dve', lambda e: e.tensor_tensor(out=bs[:, 4:5], in0=bs[:, 1:2], in1=wt[:, 0:1], op=ALU.add), r=['bs', 'wt'], w=['bs'])
                        em.op('dve', lambda e: e.memset(cntT[:], 0.0), w=['cntT'])
                        for i in range(NIT):
                            em.op('dve', lambda e, i=i, Lk=Lk: e.tensor_scalar(out=mq[:, 0:Lk], in0=acc[:, 0:Lk], scalar1=bs[:, 4:5], scalar2=0.0, op0=ALU.is_ge, op1=ALU.add, accum_out=cntT[:, i:i + 1]), r=['acc', 'bs', 'cntT'], w=['mq', 'cntT'])
                            em.op('dve', lambda e, i=i: e.scalar_tensor_tensor(out=bs[:, 5:6], in0=cntT[:, i:i + 1], scalar=NSEL, in1=wt2[:, i + 1:i + 2], op0=ALU.is_ge, op1=ALU.mult), r=['cntT', 'wt2', 'bs'], w=['bs'])
                            em.op('dve', lambda e, i=i: e.scalar_tensor_tensor(out=bs[:, 4:5], in0=bs[:, 4:5], scalar=wt[:, i + 1:i + 2], in1=bs[:, 5:6], op0=ALU.subtract, op1=ALU.add), r=['bs', 'wt'], w=['bs'])
                        em.op('dve', lambda e: e.tensor_tensor(out=bs[:, 3:4], in0=bs[:, 4:5], in1=wt[:, NIT:NIT + 1], op=ALU.subtract), r=['bs', 'wt'], w=['bs'])
                        em.op('dve', lambda e, Lk=Lk: e.tensor_scalar(out=mq[:, 0:Lk], in0=acc[:, 0:Lk], scalar1=bs[:, 3:4], scalar2=None, op0=ALU.is_ge), r=['acc', 'bs'], w=['mq'])
                    else:
                        em.op('dve', lambda e, Lk=Lk: e.tensor_scalar(out=mq[:, 0:Lk], in0=acc[:, 0:Lk], scalar1=-1.0e30, scalar2=None, op0=ALU.is_ge), r=['acc'], w=['mq'])
                    for kb0 in range(0, T + 1, 4):
                        nk = min(4, T + 1 - kb0)
                        pi = self.nps()
                        for q_ in range(nk):
                            kb = kb0 + q_
                            self.mm(pi, 128, mq[:, kb * 128:(kb + 1) * 128], self.ident_bf[:], True, True, ['mq'], cols=(q_ * 128, (q_ + 1) * 128), inc=(q_ == nk - 1))
                        src = fap(self.ps[pi][:, 0:nk * 128], [[128, nk], [1, 128]])
                        em.op('act', lambda e, src=src, kb0=kb0, nk=nk, qt=qt: e.copy(out=maskT[:, kb0:kb0 + nk, qt * 128:(qt + 1) * 128], in_=src), r=['ps%d' % pi], w=['maskT'])
                em.dma('sp', qg[:], qcv[:, :, gs], w=['qg'])
                self.resv = {6, 7}
                passes = []
                for h in range(16):
                    kv = h // 4
                    eb_ = nec % 2
                    nec += 1

                    def load(h=h, eb_=eb_):
                        em.dma('sp', ec[eb_][:], self.EB[8 + h].rearrange("o p q -> p o q"), w=['ec%d' % eb_])

                    def mult_c(kb, P, pt, g=g, eb_=eb_):
                        em.op('dve', lambda e: e.tensor_tensor(out=P[:], in0=P[:], in1=maskT[:, kb, :], op=ALU.mult), r=[pt, 'maskT'], w=[pt])
                        if kb >= 4 * g - 2:
                            em.op('dve', lambda e: e.tensor_tensor(out=P[:], in0=P[:], in1=ec[eb_][:, kb - 4 * g + 2, :], op=ALU.mult), r=[pt, 'ec%d' % eb_], w=[pt])
                    pso = 6 + no % 2
                    ob = no % 2
                    no += 1

                    def finish(h=h, pso=pso, ob=ob, gs=gs):
                        self.attn_norm(pso, osb[ob], rc, 'osb%d' % ob)
                        em.op('pool', lambda e: e.tensor_copy(out=o16[ob][:], in_=osb[ob][0:64, :]), r=['osb%d' % ob], w=['o16%d' % ob])
                        em.dma('sp', self.OT[h * 64:(h + 1) * 64, gs], o16[ob][:], r=['o16%d' % ob], w=['OT'])
                    passes.append(dict(qt=qg[:, h, :], kt=kc[kv], vt=vc[kv], rows=64, g=g, scale=C_SCALE, bias=self.cb[:, 8 + h:9 + h], mult=mult_c, tags=['qg', 'kc%d' % kv, 'vc%d' % kv], pso=pso, finish=finish, qoff=0, load=load))
                passes[0]['load']()
                self.attn_run(passes, ptb, ptn)
                self.resv = set()
        self.dbg_dump('OT', self.OT)

    def peer_scores(self, l):
        em, S = self.em, self.S
        with self.phase("peer_scores"):
            k1 = self.sb("k1", [128, 128])
            k2 = self.sb("k2", [128, 128])
            em.dma('sp', k1[:], self.peer_k1T[l], w=['k1'])
            em.dma('sp', k2[:], self.peer_k2T[l], w=['k2'])
            hsrc = self.hT.rearrange("(k p) s -> p k s", p=128)
            wqv = self.peer_w_q[l].rearrange("(k p) f -> p k f", p=128)
            xf = self.sb("xf", [128, 8, 512])
            wqc = [self.sb("wqc%d" % i, [128, 8, 128]) for i in range(2)]
            qT = [self.sb("qT%d" % i, [128, 512]) for i in range(2)]
            sall = [self.sb("sall%d" % i, [128, 16, 128]) for i in range(4)]
            m16 = self.sb("m16", [128, 16, 24])
            wkA = self.sb("wkA", [128, 16, 128])
            wkB = self.sb("wkB", [128, 16, 128])
            cand = self.sb("cand", [128, 8, 576])
            ckA = self.sb("ckA", [128, 8, 576])
            ckB = self.sb("ckB", [128, 8, 576])
            c24 = self.sb("c24", [128, 8, 24])
            ez = self.sb("ez", [128, 8, 16])
            st = self.sb("st", [128, 8, 6])
            eall = self.sb("eall", [128, 16, 128])
            e1o = [self.sb("e1o%d" % i, [128, 8, 128]) for i in range(2)]
            e2o = [self.sb("e2o%d" % i, [128, 8, 128]) for i in range(2)]
            tho = [self.sb("tho%d" % i, [128, 8]) for i in range(2)]
            nt = 0
            for g in range(self.NG):
                gs = slice(g * 512, (g + 1) * 512)
                em.dma('sp', xf[:], hsrc[:, :, gs], w=['xf'])
                for c in range(16):
                    b = c % 2
                    em.dma('sp', wqc[b][:], wqv[:, :, c * 128:(c + 1) * 128], w=['wqc%d' % b])
                    pi = self.nps()
                    for k in range(8):
                        self.mm(pi, 128, wqc[b][:, k, :], xf[:, k, :], k == 0, k == 7, ['wqc%d' % b, 'xf'])
                    em.op('act', lambda e, pi=pi, b=b: e.copy(out=qT[b][:], in_=self.ps[pi][:]), r=['ps%d' % pi], w=['qT%d' % b])
                    kk, kt = (k1, 'k1') if c % 2 == 0 else (k2, 'k2')
                    for tb in range(4):
                        pj = self.nps()
                        self.mm(pj, 128, qT[b][:, tb * 128:(tb + 1) * 128], kk[:], True, True, ['qT%d' % b, kt], cols=(0, 128))
                        em.op('dve' if tb % 2 else 'act', lambda e, pj=pj, tb=tb, c=c: (e.tensor_copy if tb % 2 else e.copy)(out=sall[tb][:, c, :], in_=self.ps[pj][:, 0:128]), r=['ps%d' % pj], w=['sall%d' % tb])
                for tb in range(4):
                    sa = sall[tb]
                    sat = 'sall%d' % tb
                    ob = nt % 2
                    nt += 1
                    for c in range(16):
                        em.op('dve', lambda e, c=c: e.max(out=m16[:, c, 0:8], in_=sa[:, c, :]), r=[sat], w=['m16_%d' % c])
                    for c in range(16):
                        em.op('dve', lambda e, c=c: e.match_replace(out=wkA[:, c, :], in_to_replace=m16[:, c, 0:8], in_values=sa[:, c, :], imm_value=NEG), r=[sat, 'm16_%d' % c], w=['wkA%d' % c])
                    for c in range(16):
                        em.op('dve', lambda e, c=c: e.max(out=m16[:, c, 8:16], in_=wkA[:, c, :]), r=['wkA%d' % c], w=['m16_%d' % c])
                    for c in range(16):
                        em.op('dve', lambda e, c=c: e.match_replace(out=wkB[:, c, :], in_to_replace=m16[:, c, 8:16], in_values=wkA[:, c, :], imm_value=NEG), r=['wkA%d' % c, 'm16_%d' % c], w=['wkB%d' % c])
                    for c in range(16):
                        em.op('dve', lambda e, c=c: e.max(out=m16[:, c, 16:24], in_=wkB[:, c, :]), r=['wkB%d' % c], w=['m16_%d' % c])
                    for h in range(8):
                        a_ap = fap(m16[:, 2 * h, :], [[1, 24], [0, 24]])
                        b_ap = fap(m16[:, 2 * h + 1, :], [[0, 24], [1, 24]])
                        o_ap = fap(cand[:, h, :], [[24, 24], [1, 24]])
                        em.op('pool', lambda e, a_ap=a_ap, b_ap=b_ap, o_ap=o_ap: e.tensor_tensor(out=o_ap, in0=a_ap, in1=b_ap, op=ALU.add), r=['m16_%d' % (2 * h), 'm16_%d' % (2 * h + 1)], w=['cand%d' % h])
                    for h in range(8):
                        em.op('dve', lambda e, h=h: e.max(out=c24[:, h, 0:8], in_=cand[:, h, :]), r=['cand%d' % h], w=['c24_%d' % h])
                    for h in range(8):
                        em.op('dve', lambda e, h=h: e.match_replace(out=ckA[:, h, :], in_to_replace=c24[:, h, 0:8], in_values=cand[:, h, :], imm_value=NEG), r=['cand%d' % h, 'c24_%d' % h], w=['ckA%d' % h])
                    for h in range(8):
                        em.op('dve', lambda e, h=h: e.max(out=c24[:, h, 8:16], in_=ckA[:, h, :]), r=['ckA%d' % h], w=['c24_%d' % h])
                    for h in range(8):
                        em.op('dve', lambda e, h=h: e.match_replace(out=ckB[:, h, :], in_to_replace=c24[:, h, 8:16], in_values=ckA[:, h, :], imm_value=NEG), r=['ckA%d' % h, 'c24_%d' % h], w=['ckB%d' % h])
                    for h in range(8):
                        em.op('dve', lambda e, h=h: e.max(out=c24[:, h, 16:24], in_=ckB[:, h, :]), r=['ckB%d' % h], w=['c24_%d' % h])
                    m16t = ['m16_%d' % c for c in range(16)]
                    c24t = ['c24_%d' % h for h in range(8)]
                    mx_ap = fap(m16[:, 0, 0:1], [[24, 16], [0, 128]])
                    em.op('dve', lambda e, mx_ap=mx_ap: e.tensor_tensor(out=eall[:], in0=sa[:], in1=mx_ap, op=ALU.subtract), r=[sat] + m16t, w=['eall'])
                    em.op('act', lambda e: e.activation(out=eall[:], in_=eall[:], func=ACT.Exp), r=['eall'], w=['eall'])
                    cm_ap = fap(c24[:, 0, 0:1], [[24, 8], [0, 16]])
                    em.op('dve', lambda e, cm_ap=cm_ap: e.tensor_tensor(out=ez[:], in0=c24[:, :, 0:16], in1=cm_ap, op=ALU.subtract), r=c24t, w=['ez'])
                    em.op('act', lambda e: e.activation(out=ez[:], in_=ez[:], func=ACT.Exp), r=['ez'], w=['ez'])
                    em.op('dve', lambda e: e.tensor_reduce(out=st[:, :, 0], in_=ez[:], axis=AX.X, op=ALU.add), r=['ez'], w=['st'])
                    em.op('dve', lambda e: e.reciprocal(out=st[:, :, 1], in_=st[:, :, 0]), r=['st'], w=['st'])
                    em.op('dve', lambda e: e.tensor_tensor(out=st[:, :, 2], in0=c24[:, :, 15], in1=c24[:, :, 16], op=ALU.add), r=c24t + ['st'], w=['st'])
                    em.op('dve', lambda e: e.scalar_tensor_tensor(out=st[:, :, 3], in0=st[:, :, 2], scalar=0.5, in1=c24[:, :, 0], op0=ALU.mult, op1=ALU.subtract), r=c24t + ['st'], w=['st'])
                    em.op('act', lambda e: e.activation(out=st[:, :, 4], in_=st[:, :, 3], func=ACT.Exp), r=['st'], w=['st'])
                    em.op('dve', lambda e, ob=ob: e.tensor_tensor(out=tho[ob][:], in0=st[:, :, 4], in1=st[:, :, 1], op=ALU.mult), r=['st'], w=['tho%d' % ob])
                    rz_ap = fap(st[:, 0, 1:2], [[6, 8], [0, 128]])
                    e1_ap = fap(eall[:, 0, :], [[256, 8], [1, 128]])
                    e2_ap = fap(eall[:, 1, :], [[256, 8], [1, 128]])
                    em.op('dve', lambda e, ob=ob, rz_ap=rz_ap, e1_ap=e1_ap: e.tensor_tensor(out=e1o[ob][:], in0=e1_ap, in1=rz_ap, op=ALU.mult), r=['eall', 'st'], w=['e1o%d' % ob])
                    em.op('pool', lambda e, ob=ob, e2_ap=e2_ap: e.tensor_copy(out=e2o[ob][:], in_=e2_ap), r=['eall'], w=['e2o%d' % ob])
                    r0 = g * 512 + tb * 128
                    em.dma('sp', self.E1[r0:r0 + 128, :], e1o[ob][:].rearrange("p h n -> p (h n)"), r=['e1o%d' % ob], w=['E1'])
                    em.dma('sp', self.E2[r0:r0 + 128, :], e2o[ob][:].rearrange("p h n -> p (h n)"), r=['e2o%d' % ob], w=['E2'])
                    em.dma('sp', self.TH[r0:r0 + 128, :], tho[ob][:], r=['tho%d' % ob], w=['TH'])
        self.dbg_dump('E1', self.E1)
        self.dbg_dump('TH', self.TH)

    def peer_main(self, l):
        em, S = self.em, self.S
        NEG_ = 32
        with self.phase("peer_main"):
            hsrc = self.hT.rearrange("(k p) s -> p k s", p=128)
            uTv = self.peer_uT[l].rearrange("(k p) e -> p k e", p=128)
            vv = self.peer_v[l].rearrange("(g c p) d -> g p c d", p=128, c=4)
            ftv = self.FT.rearrange("(k p) s -> p k s", p=128)
            xb = self.sb("xb", [128, 8, 512], BF16)
            acc = self.sb("acc", [128, 8, 512])
            u16 = [self.sb("u16%d" % i, [128, 8, 512], BF16) for i in range(2)]
            v16 = [self.sb("v16%d" % i, [128, 4, 1024], BF16) for i in range(2)]
            e1t = [self.sb("e1t%d" % i, [128, 8, 128]) for i in range(4)]
            e2t = [self.sb("e2t%d" % i, [128, 8, 128]) for i in range(4)]
            tht = [self.sb("tht%d" % i, [128, 8]) for i in range(4)]
            glT = self.sb("glT", [128, 4, 512])
            AT = self.sb("AT", [128, 4, 512], BF16)
            yb = [self.sb("yb%d" % i, [128, 512]) for i in range(6)]
            gb = [self.sb("gb%d" % i, [128, 512], BF16) for i in range(6)]
            ny = ng = 0
            for g in range(self.NG):
                gs = slice(g * 512, (g + 1) * 512)
                em.dma('pool', xb[:], hsrc[:, :, gs], w=['xb'])
                for tt in range(4):
                    r0 = g * 512 + tt * 128
                    em.dma('sp', e1t[tt][:].rearrange("p h n -> p (h n)"), self.E1[r0:r0 + 128, :], w=['e1t%d' % tt])
                    em.dma('sp', e2t[tt][:].rearrange("p h n -> p (h n)"), self.E2[r0:r0 + 128, :], w=['e2t%d' % tt])
                    em.dma('sp', tht[tt][:], self.TH[r0:r0 + 128, :], w=['tht%d' % tt])
                for eg in range(NEG_):
                    wb = eg % 2
                    ut, vt_ = 'u16%d' % wb, 'v16%d' % wb
                    em.dma('pool', u16[wb][:], uTv[:, :, eg * 512:(eg + 1) * 512], w=[ut])
                    em.dma('pool', v16[wb][:], vv[eg], w=[vt_])
                    for c in range(4):
                        for k in range(8):
                            self.mm(c, 128, u16[wb][:, k, c * 128:(c + 1) * 128], xb[:, k, :], k == 0, k == 7, [ut, 'xb'])
                        em.op('act', lambda e, c=c: e.activation(out=glT[:, c, :], in_=self.ps[c][:], func=ACT.Gelu_apprx_tanh), r=['ps%d' % c], w=['glT'])
                    for tt in range(4):
                        for h in range(8):
                            yi = ny % 6
                            ny += 1
                            gi = ng % 6
                            ng += 1
                            for i1 in range(2):
                                em.op('act', lambda e, i1=i1, yi=yi, tt=tt, h=h: e.activation(out=yb[yi][:, i1 * 128:(i1 + 1) * 128], in_=e2t[tt][:, h, :], func=ACT.Copy, scale=e1t[tt][:, h, eg * 4 + i1:eg * 4 + i1 + 1]), r=['e1t%d' % tt, 'e2t%d' % tt], w=['yb%d' % yi])
                            a_ap = fap(e1t[tt][:, h, eg * 4 + 2:eg * 4 + 4], [[1, 2], [0, 128]])
                            b_ap = fap(e2t[tt][:, h, :], [[0, 2], [1, 128]])
                            o_ap = fap(yb[yi][:, 256:512], [[128, 2], [1, 128]])
                            em.op('pool', lambda e, a_ap=a_ap, b_ap=b_ap, o_ap=o_ap: e.tensor_tensor(out=o_ap, in0=a_ap, in1=b_ap, op=ALU.mult), r=['e1t%d' % tt, 'e2t%d' % tt], w=['yb%d' % yi])
                            em.op('dve', lambda e, yi=yi, gi=gi, tt=tt, h=h: e.scalar_tensor_tensor(out=gb[gi][:], in0=yb[yi][:], scalar=tht[tt][:, h:h + 1], in1=yb[yi][:], op0=ALU.is_ge, op1=ALU.mult), r=['yb%d' % yi, 'tht%d' % tt], w=['gb%d' % gi])
                            for c in range(4):
                                self.mm(4 + c, 128, gb[gi][:, c * 128:(c + 1) * 128], self.ident_bf[:], h == 0, h == 7, ['gb%d' % gi], cols=(tt * 128, (tt + 1) * 128), inc=(c == 3))
                    for c in range(4):
                        em.op('dve', lambda e, c=c: e.tensor_tensor(out=AT[:, c, :], in0=glT[:, c, :], in1=self.ps[4 + c][:], op=ALU.mult), r=['glT', 'ps%d' % (4 + c)], w=['AT'])
                    for dch in range(8):
                        pi = dch % 4
                        for c in range(4):
                            self.mm(pi, 128, v16[wb][:, c, dch * 128:(dch + 1) * 128], AT[:, c, :], c == 0, c == 3, [vt_, 'AT'])
                        if eg == 0:
                            em.op('dve', lambda e, dch=dch, pi=pi: e.tensor_copy(out=acc[:, dch, :], in_=self.ps[pi][:]), r=['ps%d' % pi], w=['acc'])
                        else:
                            em.op('dve', lambda e, dch=dch, pi=pi: e.tensor_tensor(out=acc[:, dch, :], in0=acc[:, dch, :], in1=self.ps[pi][:], op=ALU.add), r=['ps%d' % pi, 'acc'], w=['acc'])
                em.dma('sp', ftv[:, :, gs], acc[:], r=['acc'], w=['FT'])
        self.dbg_dump('FT', self.FT)
        self.peer_post(l)

    def peer_post(self, l):
        em = self.em
        with self.phase("peer_post"):
            self.norm_tmps()
            wg = self.sb("wg", [128, 8, D], BF16)
            pw = self.sb("pw", [128, 2, D], BF16)
            self.wload(wg, self.ple_gate_w[l], 8, 0, D, 'wg')
            self.wload(pw, self.ple_w[l], 2, 0, D, 'pw')
            g2 = self.sb("g2", [128, 8])
            b2 = self.sb("b2", [128, 8])
            bg = self.sb("bg", [128, 8])
            em.dma('sp', g2[:], self.ln["ln2_g"][l], w=['gcols'])
            em.dma('sp', b2[:], self.ln["ln2_b"][l], w=['gcols'])
            em.dma('sp', bg[:], self.ln["ple_gate_b"][l], w=['gcols'])
            hv = self.hT.rearrange("(k p) s -> p k s", p=128)
            ftv = self.FT.rearrange("(k p) s -> p k s", p=128)
            pv = self.pT[l].rearrange("(k p) s -> p k s", p=128)
            hr = [self.sb("hr%d" % i, [128, 8, 512]) for i in range(1)]
            ft = [self.sb("ft%d" % i, [128, 8, 512]) for i in range(1)]
            pb = [self.sb("pb%d" % i, [128, 2, 512], BF16) for i in range(1)]
            y = self.sb("y", [128, 8, 512])
            h2 = self.sb("h2", [128, 8, 512])
            h2b = self.sb("h2b", [128, 8, 512], BF16)
            gt = self.sb("gt", [128, 512])
            ho = [self.sb("ho%d" % i, [128, 8, 512]) for i in range(1)]
            for g in range(self.NG):
                gs = slice(g * 512, (g + 1) * 512)
                b = 0
                em.dma('sp', hr[b][:], hv[:, :, gs], w=['hr%d' % b])
                em.dma('sp', ft[b][:], ftv[:, :, gs], w=['ft%d' % b])
                em.dma('pool', pb[b][:], pv[:, :, gs], w=['pb%d' % b])
                for n in range(8):
                    em.op('dve', lambda e, n=n: e.scalar_tensor_tensor(out=y[:, n, :], in0=hr[b][:, n, :], scalar=DN_ALPHA, in1=ft[b][:, n, :], op0=ALU.mult, op1=ALU.add), r=['hr%d' % b, 'ft%d' % b], w=['y'])
                self.ln_fm(y, g2, b2, h2, 'y', 'h2', outb=h2b)
                for n in range(8):
                    pi, pj = self.nps(), self.nps()
                    for k in range(8):
                        self.mm(pi, 128, wg[:, k, n * 128:(n + 1) * 128], h2b[:, k, :], k == 0, k == 7, ['wg', 'h2b'])
                    for k in range(2):
                        self.mm(pj, 128, pw[:, k, n * 128:(n + 1) * 128], pb[b][:, k, :], k == 0, k == 1, ['pw', 'pb%d' % b])
                    em.op('act', lambda e, n=n, pi=pi: e.activation(out=gt[:], in_=self.ps[pi][:], func=ACT.Sigmoid, bias=bg[:, n:n + 1], scale=1.0), r=['ps%d' % pi, 'gcols'], w=['gt'])
                    em.op('dve', lambda e, pj=pj: e.tensor_tensor(out=gt[:], in0=gt[:], in1=self.ps[pj][:], op=ALU.mult), r=['gt', 'ps%d' % pj], w=['gt'])
                    em.op('dve', lambda e, n=n: e.tensor_tensor(out=ho[b][:, n, :], in0=gt[:], in1=h2[:, n, :], op=ALU.add), r=['gt', 'h2'], w=['ho%d' % b])
                em.dma('sp', hv[:, :, gs], ho[b][:], r=['ho%d' % b], w=['hT%d' % g])
        self.dbg_dump('h2', self.hT)

    def finalize(self):
        em = self.em
        em.barrier()
        if not self.dbg:
            em.dma('sp', self.outT, self.hT)
        em.barrier()


def _cols(v, C):
    return np.ascontiguousarray(np.asarray(v, np.float32).reshape(C, 128).T)


def prep_inputs(inp, b, S, L):
    NE, NO = (L + 1) // 2, max(L // 2, 1)
    f = lambda a: np.ascontiguousarray(np.asarray(a, np.float32))
    m = {}
    m["xT"] = f(np.asarray(inp["x"])[b, :S].T)
    m["pT"] = f(np.transpose(np.asarray(inp["p"])[:L, b, :S], (0, 2, 1)))
    m["pos"] = np.ascontiguousarray(np.asarray(inp["positions"])[b, :S].reshape(1, S).astype(np.int32))
    m["rel_bias"] = f(inp["rel_bias"])
    m["ev_w_in"] = f(np.asarray(inp["ev_w_in"])[:NE])
    m["ev_w_uq"] = f(np.asarray(inp["ev_w_uq"])[:NE])
    m["ev_w_ukv"] = f(np.asarray(inp["ev_w_ukv"])[:NE])
    m["ev_q_norm"] = np.stack([_cols(np.asarray(inp["ev_q_norm"])[i], 4) for i in range(NE)])
    m["ev_kv_norm"] = np.stack([_cols(np.asarray(inp["ev_kv_norm"])[i], 2) for i in range(NE)])
    m["ev_lam"] = f(np.stack([np.asarray(inp[k])[:NE] for k in ("ev_lam_q1", "ev_lam_k1", "ev_lam_q2", "ev_lam_k2")], axis=-1))
    m["ev_subln"] = f(np.asarray(inp["ev_subln"])[:NE].reshape(NE, 64, 1))
    m["ev_w_o"] = f(np.asarray(inp["ev_w_o"])[:NE])
    m["od_w_in"] = f(np.asarray(inp["od_w_in"])[:NO])
    m["od_w_o"] = f(np.asarray(inp["od_w_o"])[:NO])
    for k in ("ln1_g", "ln1_b", "ln2_g", "ln2_b", "ple_gate_b"):
        m[k] = np.stack([_cols(np.asarray(inp[k])[i], 8) for i in range(L)])
    m["peer_w_q"] = f(np.asarray(inp["peer_w_q"])[:L])
    m["peer_k1T"] = f(np.transpose(np.asarray(inp["peer_k1"])[:L], (0, 2, 1)))
    m["peer_k2T"] = f(np.transpose(np.asarray(inp["peer_k2"])[:L], (0, 2, 1)))
    m["peer_uT"] = f(np.transpose(np.asarray(inp["peer_u"])[:L], (0, 2, 1)))
    m["peer_v"] = f(np.asarray(inp["peer_v"])[:L])
    m["ple_w"] = f(np.asarray(inp["ple_w"])[:L])
    m["ple_gate_w"] = f(np.asarray(inp["ple_gate_w"])[:L])
    return m


def kernel(**inputs):
    B, S, L = 8, 4096, 4
    prog = Prog(S, L)
    shared = None
    in_maps = []
    for b in range(B):
        m = prep_inputs(inputs, b, S, L)
        if shared is None:
            shared = m
        else:
            for k in m:
                if k not in ("xT", "pT", "pos"):
                    m[k] = shared[k]
        in_maps.append(m)
    res = run_bass_kernel_spmd(prog.nc, in_maps, core_ids=list(range(B)))
    out = np.stack([np.ascontiguousarray(res.results[b]["outT"].T) for b in range(B)])
    return out.astype(np.float32)
```
